# Optimizing a Trainium2 kernel written in Bass

```python
import jax, jax.numpy as jnp
from jax import lax
import numpy as np

D_MODEL = 1024
BATCH = 8
SEQ = 2048
DEPTH = 2

HEAD_DIM = 64
N_HEADS = D_MODEL // HEAD_DIM
N_A_LAYERS = max(1, DEPTH // 2)
N_B_LAYERS = DEPTH - N_A_LAYERS
DECAY_LORA = 64
ICLR_LORA = 64
N_SHIFT_MIX = 6
DIL_GROUPS = ((128, 1), (512, 4), (2048, 16))
N_GROUPS = len(DIL_GROUPS)
BAND_BLOCK = 128
ROPE_THETA = 10000.0
NORM_EPS = 1e-6
GN_EPS = 64e-5
NEG_INF = -1e30

kernel_name = "yoco_rwkv7_dilated_hybrid"


def _rms(x, g):
    xf = x.astype(jnp.float32)
    return xf * lax.rsqrt(jnp.mean(xf * xf, axis=-1, keepdims=True) + NORM_EPS) * g.astype(jnp.float32)


def _adaln(c, w, b):
    mod = jax.nn.silu(c.astype(jnp.float32)) @ w + b
    shift, scale, gate = jnp.split(mod, 3, axis=-1)
    return shift[:, None, :], scale[:, None, :], gate[:, None, :]


def _rope_tables(seq):
    pos = jnp.arange(seq, dtype=jnp.float32)
    inv = ROPE_THETA ** (-jnp.arange(0, HEAD_DIM, 2, dtype=jnp.float32) / HEAD_DIM)
    ang = pos[:, None] * inv[None, :]
    return jnp.cos(ang), jnp.sin(ang)


def _rope(x, cos, sin):
    c, s = cos[None, :, None, :], sin[None, :, None, :]
    x1, x2 = x[..., : HEAD_DIM // 2], x[..., HEAD_DIM // 2 :]
    return jnp.concatenate([x1 * c - x2 * s, x2 * c + x1 * s], axis=-1)


def _wkv7_scan(r, w, k, v, a, b):
    B, S, H, N = r.shape
    seq_major = lambda t: jnp.moveaxis(t, 1, 0)

    def step(state, inp):
        r_t, w_t, k_t, v_t, a_t, b_t = inp
        sa = jnp.einsum('bhvk,bhk->bhv', state, a_t)
        state = (state * w_t[:, :, None, :]
                 + sa[..., None] * b_t[:, :, None, :]
                 + v_t[..., None] * k_t[:, :, None, :])
        return state, jnp.einsum('bhvk,bhk->bhv', state, r_t)

    init = jnp.zeros((B, H, N, N), jnp.float32)
    _, ys = lax.scan(step, init, tuple(seq_major(t) for t in (r, w, k, v, a, b)))
    return jnp.moveaxis(ys, 0, 1)


def _rwkv7_time_mix(h, mix_mu, w_in, w0, w1, w2, a0, a1, a2, k_k, k_a, r_k, ln_g, ln_b, w_out):
    B, S, D = h.shape
    hf = h.astype(jnp.float32)
    xx = jnp.pad(hf, ((0, 0), (1, 0), (0, 0)))[:, :-1] - hf
    xs = hf[None] + xx[None] * mix_mu.astype(jnp.float32)[:, None, None, :]
    proj = jnp.einsum('pbsd,dpe->pbse', xs[:4], w_in.reshape(D, 4, D))
    r, k, v, g = proj[0], proj[1], proj[2], proj[3]
    w_log = -jax.nn.softplus(-(w0 + jnp.tanh(xs[4] @ w1) @ w2)) - 0.5
    decay = jnp.exp(-jnp.exp(w_log))
    a = jax.nn.sigmoid(a0 + (xs[5] @ a1) @ a2)
    heads = lambda t: t.reshape(B, S, N_HEADS, HEAD_DIM)
    kk = heads(k * k_k)
    kk = kk / jnp.maximum(jnp.sqrt(jnp.sum(kk * kk, axis=-1, keepdims=True)), 1e-12)
    k = k * (1.0 + (a - 1.0) * k_a)
    r, k, v, decay, a = heads(r), heads(k), heads(v), heads(decay), heads(a)
    y = _wkv7_scan(r, decay, k, v, -kk, kk * a)
    mu = jnp.mean(y, axis=-1, keepdims=True)
    var = jnp.mean(jnp.square(y - mu), axis=-1, keepdims=True)
    y = ((y - mu) * lax.rsqrt(var + GN_EPS)).reshape(B, S, D) * ln_g + ln_b
    bonus = jnp.sum(r * k * r_k, axis=-1, keepdims=True) * v
    y = (y + bonus.reshape(B, S, D)) * jax.nn.silu(g)
    return y @ w_out


def _dilated_band_attention(q, k, v, dil, win_sub):
    B, S, H, Dh = q.shape
    L = S // dil
    nb = -(-L // BAND_BLOCK)
    Lp = nb * BAND_BLOCK

    def by_residue(t):
        return t.reshape(B, L, dil, H, Dh).transpose(0, 2, 3, 1, 4)

    qb = jnp.pad(by_residue(q), ((0, 0), (0, 0), (0, 0), (0, Lp - L), (0, 0)))
    qb = qb.reshape(B, dil, H, nb, BAND_BLOCK, Dh)

    def band(t):
        tp = jnp.pad(by_residue(t), ((0, 0), (0, 0), (0, 0), (BAND_BLOCK, Lp - L), (0, 0)))
        tp = tp.reshape(B, dil, H, nb + 1, BAND_BLOCK, Dh)
        return jnp.concatenate([tp[:, :, :, :-1], tp[:, :, :, 1:]], axis=-2)

    kb, vb = band(k), band(v)
    s = jnp.einsum('bdhnqe,bdhnke->bdhnqk', qb, kb)
    qi = jnp.arange(BAND_BLOCK)[:, None]
    kj = jnp.arange(2 * BAND_BLOCK)[None, :]
    diff = BAND_BLOCK + qi - kj
    key_pos = (jnp.arange(nb)[:, None, None] - 1) * BAND_BLOCK + kj[None]
    valid = (diff >= 0) & (diff <= win_sub) & (key_pos >= 0)
    s = jnp.where(valid, s, NEG_INF)
    m = jnp.max(s, axis=-1)
    p = jnp.exp(s - m[..., None])
    l = jnp.sum(p, axis=-1)
    o = jnp.einsum('bdhnqk,bdhnke->bdhnqe', p, vb) / l[..., None]
    o = o.reshape(B, dil, H, Lp, Dh)[:, :, :, :L].transpose(0, 3, 1, 2, 4).reshape(B, S, H, Dh)
    back = lambda t: t.reshape(B, dil, H, Lp)[..., :L].transpose(0, 3, 1, 2).reshape(B, S, H)
    return o, back(m), back(l)


def _dilated_mixer(h, k_sh, v_sh, w_in, q_norm_g, w_out, cos, sin):
    B, S, D = h.shape
    proj = h.astype(jnp.float32) @ w_in
    q = proj[..., : N_GROUPS * D].reshape(B, S, N_GROUPS * N_HEADS, HEAD_DIM)
    gate = proj[..., N_GROUPS * D :]
    q = _rope(_rms(q, q_norm_g), cos, sin) * (HEAD_DIM ** -0.5)
    q = q.reshape(B, S, N_GROUPS, N_HEADS, HEAD_DIM)
    outs, maxes, denoms = [], [], []
    for gi, (win, dil) in enumerate(DIL_GROUPS):
        o, m, l = _dilated_band_attention(q[:, :, gi], k_sh, v_sh, dil, win // dil)
        outs.append(o)
        maxes.append(m)
        denoms.append(l)
    m_all = jnp.stack(maxes)
    wgt = jnp.exp(m_all - jnp.max(m_all, axis=0, keepdims=True)) * jnp.stack(denoms)
    out = jnp.einsum('gbsh,gbshe->bshe', wgt, jnp.stack(outs)) / jnp.sum(wgt, axis=0)[..., None]
    y = out.reshape(B, S, D) * jax.nn.silu(gate)
    return y @ w_out


def setup_inputs(seed: int = 0) -> dict:
    key = jax.random.key(seed)
    ks = jax.random.split(key, 32)
    D, nA, nB = D_MODEL, N_A_LAYERS, N_B_LAYERS
    f32 = jnp.float32
    nrm = lambda k, shape, s: jax.random.normal(k, shape, f32) * s
    return {
        "x": nrm(ks[0], (BATCH, SEQ, D), 1.0),
        "c": nrm(ks[1], (BATCH, D), 1.0),
        "a_ada_w": nrm(ks[2], (nA, D, 3 * D), 0.5 * D ** -0.5),
        "a_ada_b": nrm(ks[3], (nA, 3 * D), 0.02),
        "a_norm_g": 1.0 + nrm(ks[4], (nA, D), 0.02),
        "a_mix_mu": jax.random.uniform(ks[5], (nA, N_SHIFT_MIX, D), f32),
        "a_w_in": nrm(ks[6], (nA, D, 4 * D), D ** -0.5),
        "a_w0": -6.5 + 5.0 * jax.random.uniform(ks[7], (nA, D), f32),
        "a_w1": nrm(ks[8], (nA, D, DECAY_LORA), D ** -0.5),
        "a_w2": nrm(ks[9], (nA, DECAY_LORA, D), 0.5 * DECAY_LORA ** -0.5),
        "a_a0": nrm(ks[10], (nA, D), 0.1),
        "a_a1": nrm(ks[11], (nA, D, ICLR_LORA), D ** -0.5),
        "a_a2": nrm(ks[12], (nA, ICLR_LORA, D), 0.5 * ICLR_LORA ** -0.5),
        "a_k_k": 0.85 + nrm(ks[13], (nA, D), 0.02),
        "a_k_a": 1.0 + nrm(ks[14], (nA, D), 0.02),
        "a_r_k": nrm(ks[15], (nA, N_HEADS, HEAD_DIM), 0.1),
        "a_ln_g": 1.0 + nrm(ks[16], (nA, D), 0.02),
        "a_ln_b": nrm(ks[17], (nA, D), 0.02),
        "a_w_out": nrm(ks[18], (nA, D, D), D ** -0.5),
        "kv_norm_g": 1.0 + nrm(ks[19], (D,), 0.02),
        "w_kv": nrm(ks[20], (D, 2 * D), D ** -0.5),
        "k_norm_g": 1.0 + nrm(ks[21], (HEAD_DIM,), 0.02),
        "b_ada_w": nrm(ks[22], (nB, D, 3 * D), 0.5 * D ** -0.5),
        "b_ada_b": nrm(ks[23], (nB, 3 * D), 0.02),
        "b_norm_g": 1.0 + nrm(ks[24], (nB, D), 0.02),
        "b_w_in": nrm(ks[25], (nB, D, (N_GROUPS + 1) * D), D ** -0.5),
        "b_q_norm_g": 1.0 + nrm(ks[26], (nB, HEAD_DIM), 0.02),
        "b_w_out": nrm(ks[27], (nB, D, D), D ** -0.5),
    }


def reference(x, c, a_ada_w, a_ada_b, a_norm_g, a_mix_mu, a_w_in, a_w0, a_w1, a_w2, a_a0, a_a1, a_a2,
              a_k_k, a_k_a, a_r_k, a_ln_g, a_ln_b, a_w_out, kv_norm_g, w_kv, k_norm_g,
              b_ada_w, b_ada_b, b_norm_g, b_w_in, b_q_norm_g, b_w_out):
    B, S, D = x.shape
    cos, sin = _rope_tables(S)
    xr = x.astype(jnp.float32)
    k_sh = None
    v_sh = None
    for layer in range(DEPTH):
        if layer < N_A_LAYERS:
            i = layer
            shift, scale, gate = _adaln(c, a_ada_w[i], a_ada_b[i])
            h = _rms(xr, a_norm_g[i]) * (1.0 + scale) + shift
            xr = xr + gate * _rwkv7_time_mix(
                h, a_mix_mu[i], a_w_in[i], a_w0[i], a_w1[i], a_w2[i], a_a0[i], a_a1[i], a_a2[i],
                a_k_k[i], a_k_a[i], a_r_k[i], a_ln_g[i], a_ln_b[i], a_w_out[i])
            if layer == N_A_LAYERS - 1:
                kv = _rms(xr, kv_norm_g) @ w_kv
                k_sh = kv[..., :D].reshape(B, S, N_HEADS, HEAD_DIM)
                v_sh = kv[..., D:].reshape(B, S, N_HEADS, HEAD_DIM)
                k_sh = _rope(_rms(k_sh, k_norm_g), cos, sin)
        else:
            j = layer - N_A_LAYERS
            shift, scale, gate = _adaln(c, b_ada_w[j], b_ada_b[j])
            h = _rms(xr, b_norm_g[j]) * (1.0 + scale) + shift
            xr = xr + gate * _dilated_mixer(h, k_sh, v_sh, b_w_in[j], b_q_norm_g[j], b_w_out[j], cos, sin)
    return xr.astype(x.dtype)
```

```python
import math
from contextlib import ExitStack
import numpy as np
import concourse.bass as bass
import concourse.mybir as mybir
from concourse.bass_utils import run_bass_kernel_spmd

F32, BF16 = mybir.dt.float32, mybir.dt.bfloat16
ALU = mybir.AluOpType
AF = mybir.ActivationFunctionType
D, S, H, N = 1024, 2048, 16, 64
NJ = 8
C = 128
NCH = S // C
ENG = ('pe', 'act', 'dve', 'pool', 'sp')
NDS = 24
C0 = math.exp(-0.5)
DEBUG = False
LAST_PHASE = 9
SAME_ENGINE_SYNC = True
P2STOP = 99
SKIP_PHASES = ()
EMBED_WAIT = True
NSTG = 6
TMP_AFTER = 1
TRANSITIVE = True
PE_EMBED = True
P2_ACOPY_ACT = 2
STQ = 'act'
P2_POST_POOL = 0
P1_ORDER = (0, 3, 1, 2)
P3_ORDER = None
P4_LOOK = 2
P4_NPM = 4
P2_MERGE_NA = 1
P2_XADD_PE = 0

PFM = ['mu0', 'mu1', 'mu2', 'mu3', 'mu4', 'mu5', 'a_norm_g', 'w0', 'a0', 'k_k', 'k_a', 'r_k',
       'kv_norm_g', 'b_norm_g', 'a_ada_b_shift', 'a_ada_b_scale', 'b_ada_b_shift', 'b_ada_b_scale', 'c']
PTM = ['ln_g', 'ln_b', 'a_gate_b', 'b_gate_b']
CW = 3456


class Buf:
    __slots__ = ('ap', 'w', 'r', 'name', 'excl')

    def __init__(self, ap, name='', excl=False):
        self.ap, self.w, self.r, self.name, self.excl = ap, None, {}, name, excl

    def __getitem__(self, idx):
        return self.ap[idx]


def split(buf, n):
    return [Buf(buf.ap[:, i], '%s[%d]' % (buf.name, i)) for i in range(n)]


class KB:
    def __init__(self, nc, stack):
        self.nc = nc
        self.q = {e: [] for e in ENG}
        self.sem = {e: stack.enter_context(nc.semaphore('s_' + e)) for e in ENG}
        self.cnt = {e: 0 for e in ENG}
        self.seen = {e: {} for e in ENG}
        self.dsems = [stack.enter_context(nc.semaphore('d%d' % i)) for i in range(NDS)]
        self.dval = [0] * NDS
        self.dnext = 0
        self.mute = False
        self.simq = {e: [] for e in ENG}
        self.know = {}

    def _semof(self, key):
        return self.sem[key] if isinstance(key, str) else self.dsems[key[1]]

    def _deps(self, eng, reads, writes, extra=()):
        waits = {}

        def need(key, val):
            if key == eng and (eng == 'pe' or not SAME_ENGINE_SYNC):
                return
            if self.seen[eng].get(key, 0) >= val:
                return
            if waits.get(key, 0) < val:
                waits[key] = val
        for b in reads:
            if b.w:
                need(*b.w)
        self.read_keys = set(waits)
        for b in writes:
            if b.w:
                need(*b.w)
            for k, v in b.r.items():
                need(k, v)
        for k, v in extra:
            need(k, v)
        if TRANSITIVE and waits:
            waits = {k: v for k, v in waits.items() if not any(
                k != k3 and self.know.get((k3, v3), {}).get(k, 0) >= v for k3, v3 in waits.items())}
            sn = self.seen[eng]
            for k, v in waits.items():
                for k2, v2 in self.know.get((k, v), {}).items():
                    if sn.get(k2, 0) < v2:
                        sn[k2] = v2
        for k, v in waits.items():
            self.seen[eng][k] = v
        self.last_wk = list(waits.items())
        return [(self._semof(k), v) for k, v in waits.items()]

    def op(self, eng, name, reads, writes, sig=True, **kw):
        if self.mute:
            return
        writes = list(writes) + [b for b in reads if b.excl]
        reads = [b for b in reads if not b.excl]
        wl = self._deps(eng, reads, writes)
        if sig:
            self.cnt[eng] += 1
            seq = self.cnt[eng]
        else:
            seq = self.cnt[eng] + 1
        if eng == 'pe' and PE_EMBED and name in ('matmul', 'transpose'):
            wo = [i_ for i_, (k_, v_) in enumerate(self.last_wk) if k_ not in self.read_keys]
            if wo:
                i_ = wo[0]
                wl = wl[:i_] + wl[i_ + 1:] + [wl[i_]]
                kw = dict(kw, _embed_last=True)
        self.q[eng].append((wl, name, kw, self.sem[eng] if sig else None, 1))
        if sig and TRANSITIVE:
            self.know[(eng, seq)] = dict(self.seen[eng])
        self.simq[eng].append((self.last_wk, name, kw, (eng if sig else None), eng))
        for b in reads:
            b.r[eng] = max(b.r.get(eng, 0), seq)
        for b in writes:
            b.w = (eng, seq)
            b.r = {}

    def dma(self, eng, out, in_, reads=(), writes=(), **kw):
        if self.mute:
            return
        i = self.dnext
        self.dnext = (i + 1) % NDS
        prev = self.dval[i]
        key = ('d', i)
        wl = self._deps(eng, reads, writes, extra=((key, prev),) if prev else ())
        val = prev + 16
        self.dval[i] = val
        kw = dict(kw, out=out, in_=in_)
        self.q[eng].append((wl, 'dma_start', kw, self.dsems[i], 16))
        if TRANSITIVE:
            self.know[(key, val)] = dict(self.seen[eng])
        self.simq[eng].append((self.last_wk, 'dma_start', kw, key, eng))
        for b in reads:
            b.r[key] = val
        for b in writes:
            b.w = (key, val)
            b.r = {}

    def barrier(self):
        targets = [(e, self.cnt[e]) for e in ENG if self.cnt[e]] + \
                  [(('d', i), self.dval[i]) for i in range(NDS) if self.dval[i]]
        for eng in ENG:
            wl = self._deps(eng, (), (), extra=targets)
            self.q[eng].append((wl, None, None, None, 0))
            self.simq[eng].append((self.last_wk, None, None, None, eng))

    def emit(self):
        nc = self.nc

        def play(e, lst, ename=''):
            for (wl, name, kw, sem, inc) in lst:
                pe_embed = False
                if name is not None and kw is not None and kw.get('_embed_last'):
                    kw = {k_: v_ for k_, v_ in kw.items() if k_ != '_embed_last'}
                    pe_embed = True
                embed = (EMBED_WAIT and len(wl) > 0 and name in ('activation', 'tensor_tensor', 'tensor_scalar', 'tensor_copy', 'scalar_tensor_tensor', 'tensor_reduce', 'tensor_tensor_scan', 'memset')
                         and kw.get('accum_out') is None and ename != 'pe')
                for s, v in (wl[1:] if embed else (wl[:-1] if pe_embed else wl)):
                    e.wait_ge(s, v)
                if name is None:
                    continue
                ins = getattr(e, name)(**kw)
                if embed:
                    ins._wait_ge(wl[0][0], wl[0][1])
                elif pe_embed:
                    ins._wait_ge(wl[-1][0], wl[-1][1])
                if sem is not None:
                    ins.then_inc(sem, inc)
        with nc.Block() as block:
            @block.tensor
            def _(e):
                play(e, self.q['pe'], 'pe')

            @block.scalar
            def _(e):
                play(e, self.q['act'])

            @block.vector
            def _(e):
                play(e, self.q['dve'])

            @block.gpsimd
            def _(e):
                play(e, self.q['pool'])

            @block.sync
            def _(e):
                play(e, self.q['sp'])


class Arena:
    def __init__(self, ap_bf16, nbytes):
        self.base, self.cap, self.off, self.top = ap_bf16, nbytes, 0, nbytes

    def reset(self, keep_top=False):
        self.off = 0
        if not keep_top:
            self.top = self.cap

    def alloc_top(self, shape, dt, name=''):
        n = int(np.prod(shape))
        nb = (n * (4 if dt == F32 else 2) + 63) // 64 * 64
        self.top -= nb
        assert self.off <= self.top, ('arena overflow (top)', name, self.off, self.top)
        v = self.base[:, self.top // 2:(self.top + nb) // 2]
        if dt == F32:
            v = v.bitcast(F32)
        v = v[:, 0:n]
        if len(shape) == 2:
            v = v.rearrange('p (a b) -> p a b', a=shape[0])
        return Buf(v, name)

    def alloc(self, shape, dt, name=''):
        n = int(np.prod(shape))
        nb = n * (4 if dt == F32 else 2)
        nb = (nb + 63) // 64 * 64
        assert self.off + nb <= self.top, ('arena overflow', name, self.off, nb, self.top)
        v = self.base[:, self.off // 2:(self.off + nb) // 2]
        self.off += nb
        if dt == F32:
            v = v.bitcast(F32)
        v = v[:, 0:n]
        if len(shape) == 2:
            v = v.rearrange('p (a b) -> p a b', a=shape[0])
        elif len(shape) == 3:
            v = v.rearrange('p (a b c) -> p a b c', a=shape[0], b=shape[1])
        return Buf(v, name)


class Ring:
    def __init__(self, bufs):
        self.bufs, self.i = bufs, 0

    def next(self):
        b = self.bufs[self.i]
        self.i = (self.i + 1) % len(self.bufs)
        return b


def wavefront(n, stages, order=None):
    ns = len(stages)
    order = list(reversed(range(ns))) if order is None else order
    for w in range(n + ns - 1):
        for s_ in order:
            i = w - s_
            if 0 <= i < n:
                stages[s_](i)


def build_nc():
    nc = bass.Bass("TRN2", target_bir_lowering=False)
    dram_in = lambda name, shape: nc.dram_tensor(name, list(shape), F32, kind="ExternalInput").ap()
    skind = "ExternalOutput" if DEBUG else "Internal"
    scr = lambda name, shape, dt: nc.dram_tensor(name, list(shape), dt, kind=skind).ap()

    x_d = dram_in('x', [S, D])
    pfm_d = dram_in('pfm', [128, len(PFM), NJ])
    ptm_d = dram_in('ptm', [128, len(PTM), D])
    cst_d = dram_in('cst', [128, CW])
    sel_d = dram_in('sel', [128, NJ, H])
    rope_d = dram_in('rope', [128, 2, S])
    qkg_d = dram_in('qkg', [128, 4])
    a_ada_w = dram_in('a_ada_w', [D, 3 * D]); b_ada_w = dram_in('b_ada_w', [D, 3 * D])
    a_w_in = dram_in('a_w_in', [D, 4 * D]); b_w_in = dram_in('b_w_in', [D, 4 * D])
    a_w1 = dram_in('a_w1', [D, 64]); a_w2 = dram_in('a_w2', [64, D])
    a_a1 = dram_in('a_a1', [D, 64]); a_a2 = dram_in('a_a2', [64, D])
    a_w_out = dram_in('a_w_out', [D, D]); b_w_out = dram_in('b_w_out', [D, D])
    w_kv = dram_in('w_kv', [D, 2 * D])
    out_d = nc.dram_tensor('out', [S, D], F32, kind="ExternalOutput").ap()

    s_ar = scr('s_ar', [NJ, 128, NCH, 2, C], BF16)
    s_kt = scr('s_kt', [NJ, 128, S], BF16)
    s_bt = scr('s_bt', [NJ, 128, S], BF16)
    s_tm = {n: scr('s_' + n, [S, D], BF16) for n in ('At', 'Bbt', 'Kbt', 'Vt', 'SGt')}
    s_bonus = scr('s_bonus', [S, H], F32)
    s_xr1 = scr('s_xr1', [S, D], F32)

    with ExitStack() as st:
        sb = lambda name, shape, dt: st.enter_context(nc.sbuf_tensor('sb_' + name, list(shape), dt))
        kb = KB(nc, st)
        op, dma = kb.op, kb.dma
        pfm = Buf(sb('pfm', [128, len(PFM), NJ], F32)[:], 'pfm')
        cst = Buf(sb('cst', [128, 384], F32)[:], 'cst')
        cbf = Buf(sb('cbf', [128, CW], BF16)[:], 'cbf')
        sel = Buf(sb('sel', [128, NJ, H], F32)[:], 'sel')
        selb = Buf(sb('selb', [128, NJ, H], BF16)[:], 'selb')
        modA = Buf(sb('modA', [128, 16], F32)[:], 'modA')
        modB = Buf(sb('modB', [128, 16], F32)[:], 'modB')
        gsh = Buf(sb('gsh', [128, 4, NJ], F32)[:], 'gsh')
        gateA = Buf(sb('gateA', [128, D], F32)[:], 'gateA')
        gateB = Buf(sb('gateB', [128, D], F32)[:], 'gateB')
        glast = Buf(sb('glast', [128, NJ, NCH], F32)[:], 'glast')
        ST_all = Buf(sb('ST', [128, NJ, N], F32)[:], 'ST')
        STb_all = Buf(sb('STb', [128, NJ, N], BF16)[:], 'STb')
        ST, STb = split(ST_all, NJ), split(STb_all, NJ)
        GT = [Buf(sb('GT%d' % j, [128, 128], F32)[:], 'GT%d' % j) for j in range(NJ)]
        zeros = Buf(sb('zeros', [128, 512], F32)[:], 'zeros')
        epsb = Buf(sb('epsb', [128, 8], F32)[:], 'epsb')
        npar = Buf(sb('npar', [128, 2, NJ], F32)[:], 'npar')
        ARENA_BYTES = 180 * 1024
        arena = Arena(sb('arena', [128, ARENA_BYTES // 2], BF16)[:], ARENA_BYTES)
        banks = [Buf(st.enter_context(nc.psum_tensor('bank%d' % i, [128, 512], F32))[:], 'bank%d' % i, excl=True)
                 for i in range(8)]
        psum = Ring(banks)

        pidx = {n: i for i, n in enumerate(PFM)}
        P = lambda name, j: pfm[:, pidx[name], j:j + 1]
        ident = cst[:, 0:128]
        identb = cbf[:, 0:128]
        m_su_ui = cbf[:, 128:640].rearrange('p (h c) -> p h c', h=2)
        m_sbd_ui = cbf[:, 1920:2432].rearrange('p (h c) -> p h c', h=2)
        m_slbd2 = cbf[:, 2432:2688].rearrange('p (h c) -> p h c', h=2)
        m_off2 = cbf[:, 2688:2944].rearrange('p (h c) -> p h c', h=2)
        ident2b = cbf[:, 896:1152].rearrange('p (h c) -> p h c', h=2)
        blockones = cbf[:, 1152:1280]
        rotp = cbf[:, 1280:1408]

        dma('sp', pfm[:], pfm_d, writes=[pfm])
        dma('sp', sel[:], sel_d, writes=[sel])
        op('dve', 'tensor_copy', [sel], [selb], out=selb[:], in_=sel[:])
        op('dve', 'tensor_scalar', [pfm], [npar], out=npar[:, 0, :], in0=pfm[:, pidx['w0'], :], scalar1=-1.0, scalar2=None, op0=ALU.mult)
        op('dve', 'tensor_scalar', [pfm, npar], [npar], out=npar[:, 1, :], in0=pfm[:, pidx['a0'], :], scalar1=-1.0, scalar2=None, op0=ALU.mult)
        op('pool', 'memset', [], [zeros], ap=zeros[:], constant=0.0)
        for i_, v_ in enumerate((1e-6, 1e-24, 64e-5, 64e-6, 1.0)):
            op('pool', 'memset', [epsb], [epsb], ap=epsb[:, i_:i_ + 1], constant=v_)
        op('pool', 'memset', [], ST, ap=ST_all[:], constant=0.0)
        op('pool', 'memset', [], STb, ap=STb_all[:], constant=0.0)
        for j in range(NJ):
            op('pool', 'memset', [], [GT[j]], ap=GT[j][:], constant=0.0)

        LAYERS = {'a': (a_ada_w, modA, gateA, 'a_ada_b_shift', 'a_ada_b_scale', 0, 0, 'a_norm_g'),
                  'b': (b_ada_w, modB, gateB, 'b_ada_b_shift', 'b_ada_b_scale', 1, 2, 'b_norm_g')}

        def ada_setup(nbuf=2):
            silc = arena.alloc([NJ], F32, 'silc')
            ptmg = arena.alloc([2, D], F32, 'ptmg')
            dma('sp', ptmg[:], ptm_d[:, 2:4, :], writes=[ptmg])
            silrep = arena.alloc([NJ, 128], F32, 'silrep')
            adaw_p = Ring([arena.alloc([NJ, 512], F32, 'adaw%d' % i) for i in range(nbuf)])
            op('act', 'activation', [pfm], [silc], out=silc[:], in_=pfm[:, pidx['c'], :], func=AF.Silu)
            for kc in range(NJ):
                op('dve', 'tensor_scalar', [silc, zeros], [silrep], out=silrep[:, kc, :], in0=zeros[:, 0:128],
                   scalar1=silc[:, kc:kc + 1], scalar2=None, op0=ALU.add)
            return dict(silc=silc, ptmg=ptmg, silrep=silrep, adaw_p=adaw_p)

        def ada_load(ctx, layer, ct):
            ada_w = LAYERS[layer][0]
            wv = ada_w.rearrange('(kc p) n -> p kc n', p=128)
            wt = ctx['adaw_p'].next()
            dma('sp', wt[:], wv[:, :, ct * 512:(ct + 1) * 512], writes=[wt])
            ctx['wt'] = wt

        def ada_ct(ctx, layer, ct, load=True):
            (ada_w, mod, gate, bshift, bscale, gi_, li, ng) = LAYERS[layer]
            silc, ptmg, silrep = ctx['silc'], ctx['ptmg'], ctx['silrep']
            if load:
                ada_load(ctx, layer, ct)
            wt = ctx['wt']
            bk = psum.next()
            if ct < 4:
                for fc in range(4):
                    col = ct * 4 + fc
                    for kc in range(NJ):
                        op('pe', 'matmul', [wt, silc], [bk], sig=(kc == NJ - 1), out=bk[:, col:col + 1],
                           lhsT=wt[:, kc, fc * 128:(fc + 1) * 128], rhs=silc[:, kc:kc + 1],
                           start=(kc == 0), stop=(kc == NJ - 1))
                op('dve', 'tensor_copy', [bk], [mod], out=mod[:, ct * 4:ct * 4 + 4], in_=bk[:, ct * 4:ct * 4 + 4])
            else:
                for kc in range(NJ):
                    op('pe', 'matmul', [wt, silrep], [bk], sig=(kc == NJ - 1), out=bk[:, :],
                       lhsT=silrep[:, kc, :], rhs=wt[:, kc, :], start=(kc == 0), stop=(kc == NJ - 1))
                c0_ = (ct - 4) * 512
                op('dve', 'tensor_tensor', [bk, ptmg], [gate], out=gate[:, c0_:c0_ + 512], in0=bk[:, :],
                   in1=ptmg[:, gi_, c0_:c0_ + 512], op=ALU.add)
            if ct == 5:
                op('dve', 'tensor_tensor', [mod, pfm, gsh], [gsh], out=gsh[:, li + 1, :], in0=mod[:, 0:8],
                   in1=pfm[:, pidx[bshift], :], op=ALU.add)
                op('dve', 'tensor_tensor', [mod, pfm, gsh], [gsh], out=gsh[:, li, :], in0=mod[:, 8:16],
                   in1=pfm[:, pidx[bscale], :], op=ALU.add)
                op('dve', 'scalar_tensor_tensor', [gsh, pfm], [gsh], out=gsh[:, li, :], in0=gsh[:, li, :], scalar=1.0,
                   in1=pfm[:, pidx[ng], :], op0=ALU.add, op1=ALU.mult)

        def load_w_bf16(dst, src_view, ncols, cw=512):
            for c0_ in range(0, ncols, cw):
                dma('pool', dst[:, :, c0_:c0_ + cw], src_view[:, :, c0_:c0_ + cw], writes=[dst])

        arena.reset()
        Win = arena.alloc_top([NJ, 4 * D], BF16, 'Win')
        load_w_bf16(Win, a_w_in.rearrange('(kc p) n -> p kc n', p=128), 4 * D)
        cstage = arena.alloc([CW], F32, 'cstage')
        dma('sp', cstage[:], cst_d, writes=[cstage])
        op('dve', 'tensor_copy', [cstage], [cbf], out=cbf[:], in_=cstage[:])
        op('act', 'activation', [cstage], [cst], out=cst[:, 0:128], in_=cstage[:, 0:128], func=AF.Copy)
        op('act', 'activation', [cstage, cst], [cst], out=cst[:, 128:384], in_=cstage[:, 3200:3456], func=AF.Copy)
        actx = ada_setup()
        for ct in range(6):
            ada_ct(actx, 'a', ct)
        kb.barrier()

        def load_rows(src_dram, t0, TB, xt):
            dma('sp', xt[:, 0:TB, :], src_dram.rearrange('(b p) d -> p b d', p=128)[:, t0 // 128:t0 // 128 + TB, :],
                writes=[xt])

        def norm_transpose(TB, xt, sq, ss, xn, hT, hTj, gs_ap, sh_ap, col0):
            for b in range(TB):
                op('act', 'activation', [xt], [sq, ss], out=sq[:], in_=xt[:, b, :], func=AF.Square,
                   accum_out=ss[:, b:b + 1])
            op('act', 'activation', [ss, epsb], [ss], out=ss[:, 0:TB], in_=ss[:, 0:TB], func=AF.Ln, scale=1.0 / D, bias=epsb[:, 0:1])
            op('act', 'activation', [ss], [ss], out=ss[:, 0:TB], in_=ss[:, 0:TB], func=AF.Exp, scale=-0.5)
            for b in range(TB):
                op('act', 'activation', [xt, ss], [xn], out=xn[:, b, :], in_=xt[:, b, :], func=AF.Copy,
                   scale=ss[:, b:b + 1])
            for j in range(NJ):
                bk = psum.next()
                for b in range(TB):
                    op('pe', 'transpose', [xn, cst], [bk], sig=(b == TB - 1), out=bk[:, b * 128:(b + 1) * 128],
                       in_=xn[:, b, j * 128:(j + 1) * 128], identity=ident)
                if j % 2 == 0:
                    op('dve', 'tensor_scalar', [bk, gsh], [hTj[j]], out=hT[:, j, col0:col0 + TB * 128],
                       in0=bk[:, 0:TB * 128], scalar1=gs_ap(j), scalar2=sh_ap(j), op0=ALU.mult, op1=ALU.add)
                else:
                    op('act', 'activation', [bk, gsh], [hTj[j]], out=hT[:, j, col0:col0 + TB * 128],
                       in_=bk[:, 0:TB * 128], func=AF.Identity, scale=gs_ap(j), bias=sh_ap(j))

        if LAST_PHASE >= 1:
            kb.mute = 1 in SKIP_PHASES
            TB = 2
            T = TB * 128
            arena.reset(keep_top=True)
            W1 = arena.alloc([NJ, 64], BF16, 'W1'); A1 = arena.alloc([NJ, 64], BF16, 'A1')
            W2 = arena.alloc([D], BF16, 'W2'); A2 = arena.alloc([D], BF16, 'A2')
            xt_r = Ring([arena.alloc([TB, D], F32, 'xt%d' % i) for i in range(1)])
            ss = arena.alloc([4], F32, 'ss')
            hT = arena.alloc([NJ, T + 1], F32, 'hT'); hTj = split(hT, NJ)
            xx = arena.alloc([NJ, T], F32, 'xx'); xxj = split(xx, NJ)
            _xsb = [arena.alloc([NJ, T], BF16, 'xs%d' % i) for i in range(2)]
            xs_r_ = Ring([(b_, split(b_, NJ)) for b_ in _xsb])
            lt = Ring([arena.alloc([T], BF16, 'lt%d' % i) for i in range(2)])
            _tsz = {'rk': (3, [2, T]), 'sg': (2, [2, T]), 'cs': (2, [T]), 'tqa': (2, [T]), 'tqb': (1, [T]), 'kkr': (2, [T]), 'k2': (3, [T]),
                    'gam': (2, [T]), 'ginv': (2, [T]), 'gprev': (2, [T]), 'ginvl': (2, [T]), 'rn': (1, [T]), 'kkn': (1, [T]), 'b_': (1, [T])}
            tmp = {n: Ring([arena.alloc(sh_, F32, n + str(i)) for i in range(k_)]) for n, (k_, sh_) in _tsz.items()}
            sqb = Ring([arena.alloc([T], BF16, 'sqb%d' % i) for i in range(2)])
            o_ar = arena.alloc([NJ, TB, 2 * C], BF16, 'o_ar'); o_arj = split(o_ar, NJ)
            o_k = arena.alloc([NJ, T], BF16, 'o_k'); o_kj = split(o_k, NJ)
            o_b = arena.alloc([NJ, T], BF16, 'o_b'); o_bj = split(o_b, NJ)
            o_kb = arena.alloc([NJ, T], BF16, 'o_kb'); o_kbj = split(o_kb, NJ)
            o_bb = arena.alloc([NJ, T], BF16, 'o_bb'); o_bbj = split(o_bb, NJ)
            o_rk = arena.alloc([NJ, T], BF16, 'o_rk'); o_rkj = split(o_rk, NJ)
            stg = Ring([arena.alloc([D], BF16, 'stg%d' % i) for i in range(NSTG)])
            sq = stg.bufs[2]
            bon = arena.alloc([TB, H], F32, 'bon')
            nbias = Ring([arena.alloc([TB], F32, 'nbias%d' % i) for i in range(2)])

            dma('pool', W1[:], a_w1.rearrange('(kc p) n -> p kc n', p=128), writes=[W1])
            dma('pool', A1[:], a_a1.rearrange('(kc p) n -> p kc n', p=128), writes=[A1])
            dma('pool', W2[0:64, :], a_w2, writes=[W2])
            dma('pool', A2[0:64, :], a_a2, writes=[A2])
            op('dve', 'memset', [], hTj, ap=hT[:, :, 0:1], constant=0.0)
            nT = S // T
            xt = xt_r.next()
            load_rows(x_d, 0, TB, xt)
            for tt in range(nT):
                t0 = tt * T
                norm_transpose(TB, xt, sq, ss, xt, hT, hTj, lambda j: gsh[:, 0, j:j + 1], lambda j: gsh[:, 1, j:j + 1], 1)
                if tt + 1 < nT:
                    load_rows(x_d, t0 + T, TB, xt)
                op('dve', 'tensor_tensor', hTj, xxj, out=xx[:], in0=hT[:, :, 0:T], in1=hT[:, :, 1:T + 1], op=ALU.subtract)

                def make_xs(p):
                    xs, xsj = xs_r_.next()
                    for j in range(NJ):
                        op('dve', 'scalar_tensor_tensor', [xxj[j], hTj[j], pfm], [xsj[j]], out=xs[:, j, :], in0=xx[:, j, :],
                           scalar=P('mu%d' % p, j), in1=hT[:, j, 1:T + 1], op0=ALU.mult, op1=ALU.add)
                    return xs, xsj

                def lora_mid(xs, xsj, Wl, func):
                    l_ = lt.next()
                    bk = psum.next()
                    for kc in range(NJ):
                        op('pe', 'matmul', [Wl, xsj[kc]], [bk], sig=(kc == NJ - 1), out=bk[0:64, 0:T], lhsT=Wl[:, kc, :],
                           rhs=xs[:, kc, :], start=(kc == 0), stop=(kc == NJ - 1))
                    op('act', 'activation', [bk], [l_], out=l_[0:64, :], in_=bk[0:64, 0:T], func=func)
                    return l_
                xs_w, xs_wj = make_xs(4)
                ltw = lora_mid(xs_w, xs_wj, W1, AF.Tanh)
                xs_a, xs_aj = make_xs(5)
                lta = lora_mid(xs_a, xs_aj, A1, AF.Copy)
                xs_r, xs_rj = make_xs(0)
                xs_k, xs_kj = make_xs(1)

                jx = {}
                v3 = lambda ap: ap.rearrange('p (b t) -> p b t', b=TB)

                def sP(j):
                    fs = slice(j * 128, (j + 1) * 128)
                    b_rk, b_z = psum.next(), psum.next()
                    for (c0b, xs_, xsj_, cb) in ((0, xs_r, xs_rj, 0), (T, xs_k, xs_kj, D)):
                        for kc in range(NJ):
                            op('pe', 'matmul', [Win, xsj_[kc]], [b_rk], sig=(kc == NJ - 1), out=b_rk[:, c0b:c0b + T],
                               lhsT=Win[:, kc, cb + j * 128:cb + (j + 1) * 128], rhs=xs_[:, kc, :],
                               start=(kc == 0), stop=(kc == NJ - 1))
                    op('pe', 'matmul', [W2, ltw], [b_z], out=b_z[:, 0:T], lhsT=W2[0:64, fs], rhs=ltw[0:64, :], start=True, stop=True)
                    op('pe', 'matmul', [A2, lta], [b_z], out=b_z[:, T:2 * T], lhsT=A2[0:64, fs], rhs=lta[0:64, :], start=True, stop=True)
                    jx[j] = dict(b_rk=b_rk, b_z=b_z)

                def sA(j):
                    b_rk, b_z = jx[j]['b_rk'], jx[j]['b_z']
                    t_ = {n: tmp[n].next() for n in ('rk', 'sg', 'cs', 'tqa', 'tqb', 'kkr', 'k2')}
                    rk_, sg_, cs, tqa, tqb, kkr, k2 = (t_[n] for n in ('rk', 'sg', 'cs', 'tqa', 'tqb', 'kkr', 'k2'))
                    r_, k_, s1, ic = rk_[:, 0, :], rk_[:, 1, :], sg_[:, 0, :], sg_[:, 1, :]
                    nb_ = nbias.next()
                    op('act', 'activation', [b_rk], [rk_], out=rk_[:], in_=b_rk[:, 0:2 * T].rearrange('p (a t) -> p a t', a=2), func=AF.Copy)
                    for pi_ in range(2):
                        op('act', 'activation', [b_z, npar], [sg_], out=sg_[:, pi_, :], in_=b_z[:, pi_ * T:(pi_ + 1) * T], func=AF.Exp, scale=-1.0,
                           bias=npar[:, pi_, j:j + 1])
                    op('act', 'activation', [sg_, epsb], [sg_], out=sg_[:], in_=sg_[:], func=AF.Ln, bias=epsb[:, 4:5])
                    op('act', 'activation', [sg_], [sg_], out=sg_[:], in_=sg_[:], func=AF.Exp, scale=-1.0)
                    for b in range(TB):
                        bs = slice(b * C, (b + 1) * C)
                        op('dve', 'tensor_tensor_scan', [sg_, zeros], [cs], out=cs[:, bs], data0=s1[:, bs], data1=zeros[:, 0:C],
                           initial=0.0, op0=ALU.add, op1=ALU.add)
                    op('dve', 'tensor_scalar', [cs], [nb_], out=nb_[:, 0:TB], in0=cs[:, C - 1::C], scalar1=-C0, scalar2=None, op0=ALU.mult)
                    op('dve', 'tensor_tensor', [cs, sg_], [tqa], out=tqa[:], in0=cs[:], in1=s1, op=ALU.subtract)
                    op('dve', 'tensor_scalar', [rk_, pfm], [kkr], out=kkr[:], in0=k_, scalar1=P('k_k', j), scalar2=None, op0=ALU.mult)
                    op('dve', 'tensor_scalar', [sg_, pfm], [tqb], out=tqb[:], in0=ic, scalar1=1.0, scalar2=P('k_a', j),
                       op0=ALU.subtract, op1=ALU.mult)
                    op('dve', 'scalar_tensor_tensor', [tqb, rk_], [k2], out=k2[:], in0=tqb[:], scalar=1.0, in1=k_, op0=ALU.add, op1=ALU.mult)
                    jx[j] = dict(rk=rk_, sg=sg_, cs=cs, tqa=tqa, kkr=kkr, k2=k2, nb=nb_)

                def sB(j):
                    c_ = jx[j]
                    cs, nb_, kkr = c_['cs'], c_['nb'], c_['kkr']
                    t_ = {n: tmp[n].next() for n in ('gam', 'ginv', 'gprev', 'ginvl')}
                    gam, ginv, gprev, ginvl = (t_[n] for n in ('gam', 'ginv', 'gprev', 'ginvl'))
                    op('act', 'activation', [cs], [gam], out=gam[:], in_=cs[:], func=AF.Exp, scale=-C0)
                    op('act', 'activation', [cs], [ginv], out=ginv[:], in_=cs[:], func=AF.Exp, scale=C0)
                    op('act', 'activation', [c_['tqa']], [gprev], out=gprev[:], in_=c_['tqa'][:], func=AF.Exp, scale=-C0)
                    for b in range(TB):
                        bs = slice(b * C, (b + 1) * C)
                        op('act', 'activation', [cs, nb_], [ginvl], out=ginvl[:, bs], in_=cs[:, bs], func=AF.Exp, scale=C0, bias=nb_[:, b:b + 1])
                    op('act', 'activation', [gam], [glast], out=glast[:, j, tt * TB:(tt + 1) * TB], in_=gam[:, C - 1::C], func=AF.Copy)
                    sq_ = sqb.next()
                    op('act', 'activation', [kkr], [sq_], out=sq_[:], in_=kkr[:], func=AF.Square)
                    b_ss = psum.next()
                    op('pe', 'matmul', [cbf, sq_], [b_ss], out=b_ss[:, 0:T], lhsT=blockones, rhs=sq_[:], start=True, stop=True)
                    c_.update(gam=gam, ginv=ginv, gprev=gprev, ginvl=ginvl, b_ss=b_ss)

                def sC(j):
                    c_ = jx.pop(j)
                    rk_, sg_, kkr, k2 = c_['rk'], c_['sg'], c_['kkr'], c_['k2']
                    gam, ginv, gprev, ginvl, b_ss = c_['gam'], c_['ginv'], c_['gprev'], c_['ginvl'], c_['b_ss']
                    r_, ic = rk_[:, 0, :], sg_[:, 1, :]
                    rn, kkn, b__ = tmp['rn'].next(), tmp['kkn'].next(), tmp['b_'].next()
                    op('act', 'activation', [b_ss, epsb], [rn], out=rn[:], in_=b_ss[:, 0:T], func=AF.Ln, bias=epsb[:, 1:2])
                    op('act', 'activation', [rn], [rn], out=rn[:], in_=rn[:], func=AF.Exp, scale=-0.5)
                    op('dve', 'tensor_tensor', [kkr, rn], [kkn], out=kkn[:], in0=kkr[:], in1=rn[:], op=ALU.mult)
                    op('dve', 'tensor_tensor', [kkn, sg_], [b__], out=b__[:], in0=kkn[:], in1=ic, op=ALU.mult)
                    op('pool', 'tensor_tensor', [rk_, gam], [o_arj[j]], out=o_ar[:, j, :, C:2 * C], in0=v3(r_), in1=v3(gam[:]), op=ALU.mult)
                    for b in range(TB):
                        bs = slice(b * C, (b + 1) * C)
                        op('dve', 'scalar_tensor_tensor', [kkn, gprev, o_arj[j]], [o_arj[j]], out=o_ar[:, j, b, 0:C], in0=kkn[:, bs],
                           scalar=-1.0, in1=gprev[:, bs], op0=ALU.mult, op1=ALU.mult)
                    op('dve', 'tensor_tensor', [k2, ginv], [o_kj[j]], out=o_k[:, j, :], in0=k2[:], in1=ginv[:], op=ALU.mult)
                    op('pool', 'tensor_tensor', [b__, ginv], [o_bj[j]], out=o_b[:, j, :], in0=b__[:], in1=ginv[:], op=ALU.mult)
                    op('dve', 'tensor_tensor', [k2, ginvl], [o_kbj[j]], out=o_kb[:, j, :], in0=k2[:], in1=ginvl[:], op=ALU.mult)
                    op('pool', 'tensor_tensor', [b__, ginvl], [o_bbj[j]], out=o_bb[:, j, :], in0=b__[:], in1=ginvl[:], op=ALU.mult)
                    op('dve', 'scalar_tensor_tensor', [rk_, k2, pfm], [o_rkj[j]], out=o_rk[:, j, :], in0=r_, scalar=P('r_k', j), in1=k2[:],
                       op0=ALU.mult, op1=ALU.mult)
                wavefront(NJ, [sP, sA, sB, sC], order=list(P1_ORDER))
                ch0 = t0 // C
                dma('sp', s_ar.rearrange('j p c a t -> p j c (a t)')[:, :, ch0:ch0 + TB, :], o_ar[:], reads=o_arj)
                dma('sp', s_kt.rearrange('j p t -> p j t')[:, :, t0:t0 + T], o_k[:], reads=o_kj)
                dma('sp', s_bt.rearrange('j p t -> p j t')[:, :, t0:t0 + T], o_b[:], reads=o_bj)
                for (srcf, srcj, nm) in ((lambda j, b: o_ar[:, j, b, 0:C], o_arj, 'At'),
                                         (lambda j, b: o_kb[:, j, b * C:(b + 1) * C], o_kbj, 'Kbt'),
                                         (lambda j, b: o_bb[:, j, b * C:(b + 1) * C], o_bbj, 'Bbt')):
                    for b in range(TB):
                        bk = psum.next()
                        bkb = bk[:].bitcast(BF16)
                        for j in range(NJ):
                            op('pe', 'transpose', [srcj[j], cbf], [bk], sig=(j == NJ - 1), out=bkb[:, j * 128:(j + 1) * 128],
                               in_=srcf(j, b), identity=identb)
                        sg_ = stg.next()
                        if b % 2 == 0:
                            op('act', 'activation', [bk], [sg_], out=sg_[:], in_=bkb, func=AF.Copy)
                            dma(STQ, s_tm[nm][t0 + b * C:t0 + (b + 1) * C, :], sg_[:], reads=[sg_])
                        else:
                            op('dve', 'tensor_copy', [bk], [sg_], out=sg_[:], in_=bkb)
                            dma('sp', s_tm[nm][t0 + b * C:t0 + (b + 1) * C, :], sg_[:], reads=[sg_])
                for (pidx_, cb, nm) in ((2, 2 * D, 'Vt'), (3, 3 * D, 'SGt')):
                    xs_, xsj_ = make_xs(pidx_)
                    for b in range(TB):
                        sg_ = stg.next()
                        for half in range(2):
                            bk = psum.next()
                            for kc in range(NJ):
                                op('pe', 'matmul', [xsj_[kc], Win], [bk], sig=(kc == NJ - 1), out=bk[:, :],
                                   lhsT=xs_[:, kc, b * C:(b + 1) * C], rhs=Win[:, kc, cb + half * 512:cb + (half + 1) * 512],
                                   start=(kc == 0), stop=(kc == NJ - 1))
                            op('act', 'activation', [bk], [sg_], out=sg_[:, half * 512:(half + 1) * 512], in_=bk[:, :],
                               func=(AF.Copy if nm == 'Vt' else AF.Silu))
                        dma(STQ, s_tm[nm][t0 + b * C:t0 + (b + 1) * C, :], sg_[:], reads=[sg_])
                bk = psum.next()
                for b in range(TB):
                    for j in range(NJ):
                        op('pe', 'matmul', [o_rkj[j], selb], [bk], sig=(j == NJ - 1), out=bk[:, b * H:(b + 1) * H],
                           lhsT=o_rk[:, j, b * C:(b + 1) * C], rhs=selb[:, j, :], start=(j == 0), stop=(j == NJ - 1))
                op('dve', 'tensor_copy', [bk], [bon], out=bon[:], in_=bk[:, 0:TB * H].rearrange('p (b h) -> p b h', b=TB))
                dma('sp', s_bonus.rearrange('(b p) h -> p b h', p=128)[:, t0 // 128:t0 // 128 + TB, :], bon[:], reads=[bon])
                for j in range(NJ):
                    op('dve', 'tensor_copy', [hTj[j]], [hTj[j]], out=hT[:, j, 0:1], in_=hT[:, j, T:T + 1])
            kb.mute = False
            kb.barrier()

        if LAST_PHASE >= 2:
            kb.mute = 2 in SKIP_PHASES
            arena.reset()
            wkv_pre = arena.alloc([NJ, 2 * D], BF16, 'wkv_pre')
            slot_arena = Arena(wkv_pre[:].rearrange('p a b -> p (a b)'), NJ * 2 * D * 2)
            Wout = arena.alloc([NJ, D], BF16, 'Wout')
            load_w_bf16(Wout, a_w_out.rearrange('(kc p) n -> p kc n', p=128), D)
            ptml = arena.alloc([2, D], F32, 'ptml')
            dma('sp', ptml[:], ptm_d[:, 0:2, :], writes=[ptml])

            def mk_loads(i):
                d_ = {}
                d_['AR'] = arena.alloc([NJ, 2 * C], BF16, 'AR%d' % i)
                d_['KT'] = arena.alloc([NJ, C], BF16, 'KT%d' % i)
                d_['BT'] = arena.alloc([NJ, C], BF16, 'BT%d' % i)
                for n_ in ('At', 'Bbt', 'Kbt', 'Vt', 'SGt'):
                    d_[n_] = arena.alloc([D], BF16, n_ + str(i))
                d_['bon'] = arena.alloc([H], F32, 'bonl%d' % i)
                d_['xin'] = arena.alloc([D], F32, 'xin%d' % i)
                d_['ARz'] = arena.alloc([NJ, 2, 2 * C], BF16, 'ARz%d' % i)
                d_['BTz'] = arena.alloc([NJ, 2, C], BF16, 'BTz%d' % i)
                op('pool', 'memset', [], [d_['ARz']], ap=d_['ARz'][:], constant=0.0)
                op('pool', 'memset', [], [d_['BTz']], ap=d_['BTz'][:], constant=0.0)
                return d_
            lds = Ring([mk_loads(i) for i in range(2)])
            NSL = 4
            slots = []
            for i in range(NSL):
                sl = {}
                sl['S1m'] = slot_arena.alloc([2, 2 * C], BF16, 'S1m%d' % i)
                sl['S2m'] = slot_arena.alloc([2, 2 * C], BF16, 'S2m%d' % i)
                sl['Aoff'] = slot_arena.alloc([2, C], BF16, 'Aoff%d' % i)
                sl['NA'] = [slot_arena.alloc([2, 2 * C], BF16, 'NA%d_%d' % (i, k)) for k in range(2)]
                sl['N'] = [Buf(sl['NA'][k][:, :, 0:C], 'N%d_%d' % (i, k)) for k in range(2)]
                sl['A'] = [Buf(sl['NA'][k][:, :, C:2 * C], 'A%d_%d' % (i, k)) for k in range(2)]
                sl['X'] = [slot_arena.alloc([2, C], BF16, 'X%d_%d' % (i, k)) for k in range(2)]
                sl['Zp'] = slot_arena.alloc([2, C], BF16, 'Zp%d' % i)
                sl['Pp'] = slot_arena.alloc([2, C], BF16, 'Pp%d' % i)
                sl['tmpV'] = slot_arena.alloc([2, N], BF16, 'tmpV%d' % i)
                sl['WU'] = slot_arena.alloc([2, C], BF16, 'WU%d' % i)
                sl['RpT'] = slot_arena.alloc([2, C], BF16, 'RpT%d' % i)
                op('pool', 'memset', [], [sl['RpT']], ap=sl['RpT'][:], constant=0.0)
                slots.append(sl)
            y_sb = arena.alloc([D], F32, 'y_sb')
            yn = arena.alloc([D], F32, 'yn'); bv = arena.alloc([D], F32, 'bv'); ysq = bv
            stt = arena.alloc([4, H], F32, 'stt')
            yfin = arena.alloc([D], BF16, 'yfin')
            yT = arena.alloc([NJ, C], BF16, 'yT')
            t_o = arena.alloc([D], F32, 't_o')
            xr_o = Ring([arena.alloc([D], F32, 'xr_o%d' % i) for i in range(1)])

            bctx2 = ada_setup(nbuf=1)

            def issue_loads(c):
                L = lds.next()
                dma('sp', L['AR'][:], s_ar.rearrange('j p c a t -> p j c (a t)')[:, :, c, :], writes=[L['AR']])
                dma('sp', L['KT'][:], s_kt.rearrange('j p t -> p j t')[:, :, c * C:(c + 1) * C], writes=[L['KT']])
                dma('sp', L['BT'][:], s_bt.rearrange('j p t -> p j t')[:, :, c * C:(c + 1) * C], writes=[L['BT']])
                for n_ in ('At', 'Bbt', 'Kbt', 'Vt', 'SGt'):
                    dma('sp', L[n_][:], s_tm[n_][c * C:(c + 1) * C, :], writes=[L[n_]])
                dma('sp', L['bon'][:], s_bonus[c * C:(c + 1) * C, :], writes=[L['bon']])
                dma('sp', L['xin'][:], x_d[c * C:(c + 1) * C, :], writes=[L['xin']])
                return L
            def stage(k):
                kb.mute = (k > P2STOP) or (2 in SKIP_PHASES)
            L_next = issue_loads(0)
            for c in range(NCH):
                L = L_next
                if c + 1 < NCH:
                    L_next = issue_loads(c + 1)
                AR, KT, BT, At, Bbt, Kbt, Vt, SGt = (L[k_] for k_ in ('AR', 'KT', 'BT', 'At', 'Bbt', 'Kbt', 'Vt', 'SGt'))
                if c < 6:
                    ada_load(bctx2, 'b', c)
                ARz, BTz = L['ARz'], L['BTz']
                for h2 in range(2):
                    pb = 64 * h2
                    op('act', 'activation', [AR], [ARz], out=ARz[pb:pb + 64, :, h2, :], in_=AR[pb:pb + 64, :, :], func=AF.Copy)
                    op('act', 'activation', [BT], [BTz], out=BTz[pb:pb + 64, :, h2, :], in_=BT[pb:pb + 64, :, :], func=AF.Copy)
                ybanks = []
                for half in range(2):
                    pairs = list(range(4 * half, 4 * half + 4))
                    stage(0)
                    for j in pairs:
                        sl = slots[j % NSL]
                        b1, b2, b3 = psum.next(), psum.next(), psum.next()
                        for h2 in range(2):
                            op('pe', 'matmul', [BT, ARz], [b1], sig=(h2 == 1), out=b1[:, h2 * 256:(h2 + 1) * 256], lhsT=BT[:, j, :],
                               rhs=ARz[:, j, h2, :], start=True, stop=True)
                            op('pe', 'matmul', [KT, ARz], [b2], sig=(h2 == 1), out=b2[:, h2 * 256:(h2 + 1) * 256], lhsT=KT[:, j, :],
                               rhs=ARz[:, j, h2, :], start=True, stop=True)
                            op('pe', 'matmul', [AR, BTz], [b3], sig=(h2 == 1), out=b3[:, h2 * 128:(h2 + 1) * 128], lhsT=AR[:, j, 0:C],
                               rhs=BTz[:, j, h2, :], start=True, stop=True)
                        v2 = lambda ap: ap.rearrange('p (h c) -> p h c', h=2)
                        op('dve', 'tensor_tensor', [b1, cst], [sl['S1m']], out=sl['S1m'][:], in0=v2(b1[:, :]), in1=m_sbd_ui, op=ALU.mult)
                        op('dve', 'tensor_tensor', [b3, cst], [sl['N'][0]], out=sl['N'][0][:], in0=v2(b3[:, 0:256]), in1=m_slbd2, op=ALU.mult)
                        op('dve', 'tensor_tensor', [b1, cst], [sl['Aoff']], out=sl['Aoff'][:], in0=v2(b1[:, :])[:, :, 0:C], in1=m_off2, op=ALU.mult)
                        op('dve', 'tensor_tensor', [b2, cst], [sl['S2m']], out=sl['S2m'][:], in0=v2(b2[:, :]), in1=m_su_ui, op=ALU.mult)
                        op('pool', 'tensor_tensor', [sl['S1m'], cbf], [sl['X'][1]], out=sl['X'][1][:], in0=sl['S1m'][:, :, 0:C], in1=ident2b, op=ALU.add)
                    stage(1)
                    for s in range(1, 7):
                        if s == TMP_AFTER + 1:
                            for j in pairs:
                                sl = slots[j % NSL]
                                b8 = psum.next()
                                for h2 in range(2):
                                    h = 2 * j + h2
                                    op('pe', 'matmul', [sl['S2m'], Vt], [b8], sig=(h2 == 1), out=b8[:, h2 * N:(h2 + 1) * N], lhsT=sl['S2m'][:, h2, 0:C],
                                       rhs=Vt[:, h * N:(h + 1) * N], start=True, stop=True)
                                op('act', 'activation', [b8], [sl['tmpV']], out=sl['tmpV'][:], in_=b8[:, 0:2 * N].rearrange('p (h c) -> p h c', h=2), func=AF.Copy)
                        for j in pairs:
                            sl = slots[j % NSL]
                            Np, Nn = sl['N'][(s - 1) % 2], sl['N'][s % 2]
                            Ap_buf = sl['S1m'] if s == 1 else sl['A'][(s - 1) % 2]
                            Ap = (lambda h2, b_=Ap_buf: b_[:, h2, 0:C])
                            An = sl['A'][s % 2]
                            Xp, Xn = sl['X'][(s - 1) % 2], sl['X'][s % 2]
                            v2 = lambda ap: ap.rearrange('p (h c) -> p h c', h=2)
                            if P2_MERGE_NA:
                                bNA = psum.next() if s <= 5 else None
                                bD = psum.next() if s >= 2 else None
                                for h2 in range(2):
                                    if s <= 5:
                                        op('pe', 'matmul', [Ap_buf, Np], [bNA], sig=(h2 == 1), out=bNA[:, h2 * 256:h2 * 256 + C], lhsT=Ap(h2), rhs=Np[:, h2, :],
                                           start=True, stop=True)
                                    if s <= 4:
                                        op('pe', 'matmul', [Ap_buf, Np], [bNA], sig=(h2 == 1), out=bNA[:, h2 * 256 + C:(h2 + 1) * 256], lhsT=Np[:, h2, :], rhs=Ap(h2),
                                           start=True, stop=True)
                                    if s >= 2:
                                        op('pe', 'matmul', [Np, Xp], [bD], sig=(h2 == 1), out=bD[:, h2 * C:(h2 + 1) * C], lhsT=Np[:, h2, :], rhs=Xp[:, h2, :],
                                           start=True, stop=True)
                                if s <= 4:
                                    op('act', 'activation', [bNA], [Nn, An], out=sl['NA'][s % 2][:], in_=v2(bNA[:, :]), func=AF.Copy)
                                elif s == 5:
                                    op('act', 'activation', [bNA], [Nn], out=Nn[:], in_=v2(bNA[:, :])[:, :, 0:C], func=AF.Copy)
                                if s >= 2:
                                    op('dve', 'tensor_tensor', [bD, Xp], [Xn], out=Xn[:], in0=v2(bD[:, 0:256]), in1=Xp[:], op=ALU.add)
                                continue
                            bN, bAD = psum.next(), psum.next()
                            for h2 in range(2):
                                if s <= 5:
                                    op('pe', 'matmul', [Ap_buf, Np], [bN], out=bN[:, h2 * C:(h2 + 1) * C], lhsT=Ap(h2), rhs=Np[:, h2, :],
                                       start=True, stop=True)
                                if s <= 4:
                                    op('pe', 'matmul', [Ap_buf, Np], [bAD], out=bAD[:, h2 * 256:h2 * 256 + C], lhsT=Np[:, h2, :], rhs=Ap(h2),
                                       start=True, stop=True)
                                if s >= 2:
                                    if P2_XADD_PE:
                                        op('pe', 'matmul', [cbf, Xp], [bAD], sig=False, out=bAD[:, h2 * 256 + C:(h2 + 1) * 256], lhsT=identb, rhs=Xp[:, h2, :],
                                           start=True, stop=False)
                                    op('pe', 'matmul', [Np, Xp], [bAD], out=bAD[:, h2 * 256 + C:(h2 + 1) * 256], lhsT=Np[:, h2, :], rhs=Xp[:, h2, :],
                                       start=(not P2_XADD_PE), stop=True)
                            v2 = lambda ap: ap.rearrange('p (h c) -> p h c', h=2)
                            if s <= 5:
                                op('act', 'activation', [bN], [Nn], out=Nn[:], in_=v2(bN[:, 0:256]), func=AF.Copy)
                            if s <= 4:
                                if P2_ACOPY_ACT == 0 or j % P2_ACOPY_ACT == 0:
                                    op('act', 'activation', [bAD], [An], out=An[:], in_=v2(bAD[:, :])[:, :, 0:C], func=AF.Copy)
                                else:
                                    op('dve', 'tensor_copy', [bAD], [An], out=An[:], in_=v2(bAD[:, :])[:, :, 0:C])
                            if s >= 2:
                                if not P2_XADD_PE:
                                    op('dve', 'tensor_tensor', [bAD, Xp], [Xn], out=Xn[:], in0=v2(bAD[:, :])[:, :, C:2 * C], in1=Xp[:], op=ALU.add)
                                elif P2_XADD_PE == 1 and j % 2 == 0:
                                    op('act', 'activation', [bAD], [Xn], out=Xn[:], in_=v2(bAD[:, :])[:, :, C:2 * C], func=AF.Copy)
                                else:
                                    op('dve', 'tensor_copy', [bAD], [Xn], out=Xn[:], in_=v2(bAD[:, :])[:, :, C:2 * C])
                    stage(2)
                    for j in pairs:
                        sl = slots[j % NSL]
                        Xb = sl['X'][0]
                        b9 = psum.next()
                        for h2 in range(2):
                            h = 2 * j + h2
                            op('pe', 'matmul', [Xb, At], [b9], sig=(h2 == 1), out=b9[:, h2 * C:h2 * C + N], lhsT=Xb[:, h2, :],
                               rhs=At[:, h * N:(h + 1) * N], start=True, stop=True)
                            op('pe', 'matmul', [Xb, sl['tmpV']], [b9], sig=(h2 == 1), out=b9[:, h2 * C + N:(h2 + 1) * C], lhsT=Xb[:, h2, :],
                               rhs=sl['tmpV'][:, h2, :], start=True, stop=True)
                        op('act', 'activation', [b9], [sl['Zp']], out=sl['Zp'][:], in_=b9[:, 0:2 * C].rearrange('p (h c) -> p h c', h=2), func=AF.Copy)
                    for j in pairs:
                        sl = slots[j % NSL]
                        b9 = psum.next()
                        for h2 in range(2):
                            op('pe', 'matmul', [sl['Aoff'], sl['Zp']], [b9], sig=(h2 == 1), out=b9[:, h2 * C:(h2 + 1) * C], lhsT=sl['Aoff'][:, h2, :],
                               rhs=sl['Zp'][:, h2, :], start=True, stop=True)
                        op('act', 'activation', [b9], [sl['Pp']], out=sl['Pp'][:], in_=b9[:, 0:2 * C].rearrange('p (h c) -> p h c', h=2), func=AF.Copy)
                    for j in pairs:
                        sl = slots[j % NSL]
                        Xb = sl['X'][0]
                        b9 = psum.next()
                        for h2 in range(2):
                            op('pe', 'matmul', [Xb, sl['Pp']], [b9], sig=(h2 == 1), out=b9[:, h2 * C:(h2 + 1) * C], lhsT=Xb[:, h2, :],
                               rhs=sl['Pp'][:, h2, :], start=True, stop=True)
                        op('dve', 'tensor_tensor', [b9, sl['Zp']], [sl['WU']], out=sl['WU'][:], in0=b9[:, 0:2 * C].rearrange('p (h c) -> p h c', h=2),
                           in1=sl['Zp'][:], op=ALU.add)
                    stage(3)
                    for j in pairs:
                        sl = slots[j % NSL]
                        bR, bG = psum.next(), psum.next()
                        for h2 in range(2):
                            h = 2 * j + h2
                            pb = 64 * h2
                            op('pe', 'matmul', [sl['WU'], sl['S1m']], [bR], sig=(h2 == 1), out=bR[pb:pb + 64, 0:C], lhsT=sl['WU'][:, h2, 0:N],
                               rhs=sl['S1m'][:, h2, C:2 * C], start=True, stop=True)
                            op('pe', 'matmul', [sl['WU'], Bbt], [bG], sig=(h2 == 1), out=bG[pb:pb + 64, 0:N], lhsT=sl['WU'][:, h2, 0:N],
                               rhs=Bbt[:, h * N:(h + 1) * N], start=True, stop=True)
                        for h2 in range(2):
                            pb = 64 * h2
                            op('dve', 'tensor_tensor', [bR, AR], [sl['RpT']], out=sl['RpT'][pb:pb + 64, h2, :], in0=bR[pb:pb + 64, 0:C],
                               in1=AR[pb:pb + 64, j, C:2 * C], op=ALU.add)
                        for h2 in range(2):
                            pb = 64 * h2
                            op('act', 'activation', [bG], [GT[j]], out=GT[j][pb:pb + 64, pb:pb + 64], in_=bG[pb:pb + 64, 0:N], func=AF.Copy)
                    stage(4)
                    bY = psum.next()
                    ybanks.append(bY)
                    for j in pairs:
                        sl = slots[j % NSL]
                        for h2 in range(2):
                            h = 2 * j + h2
                            pb = 64 * h2
                            hc = (h - 8 * half) * N
                            op('pe', 'matmul', [sl['RpT'], STb[j]], [bY], sig=False, out=bY[:, hc:hc + N], lhsT=sl['RpT'][:, h2, :],
                               rhs=STb_all[:, j, :], start=True, stop=False)
                            op('pe', 'matmul', [sl['S1m'], sl['WU']], [bY], sig=False, out=bY[:, hc:hc + N], lhsT=sl['S1m'][:, h2, C:2 * C],
                               rhs=sl['WU'][:, h2, N:2 * N], start=False, stop=False)
                            op('pe', 'matmul', [sl['S2m'], Vt], [bY], out=bY[:, hc:hc + N], lhsT=sl['S2m'][:, h2, C:2 * C],
                               rhs=Vt[:, h * N:(h + 1) * N], start=False, stop=True)
                    for j in pairs:
                        sl = slots[j % NSL]
                        bH = psum.next()
                        for h2 in range(2):
                            h = 2 * j + h2
                            pb = 64 * h2
                            op('pe', 'matmul', [Bbt, sl['WU']], [bH], sig=False, out=bH[pb:pb + 64, 0:N], lhsT=Bbt[:, h * N:(h + 1) * N],
                               rhs=sl['WU'][:, h2, N:2 * N], start=True, stop=False)
                            op('pe', 'matmul', [Kbt, Vt], [bH], sig=False, out=bH[pb:pb + 64, 0:N], lhsT=Kbt[:, h * N:(h + 1) * N],
                               rhs=Vt[:, h * N:(h + 1) * N], start=False, stop=False)
                        op('pe', 'matmul', [GT[j], ST[j]], [bH], out=bH[:, 0:N], lhsT=GT[j][:], rhs=ST_all[:, j, :], start=False, stop=True)
                        op('dve', 'scalar_tensor_tensor', [ST[j], glast, bH], [ST[j]], out=ST_all[:, j, :], in0=ST_all[:, j, :],
                           scalar=glast[:, j, c:c + 1], in1=bH[:, 0:N], op0=ALU.mult, op1=ALU.add)
                        op('act', 'activation', [ST[j]], [STb[j]], out=STb_all[:, j, :], in_=ST_all[:, j, :], func=AF.Copy)
                    op('act', 'activation', [bY], [y_sb], out=y_sb[:, half * 512:(half + 1) * 512], in_=bY[:, :], func=AF.Copy)
                if c == NCH - 1:
                    slot_bufs = [wkv_pre]
                    for sl in slots:
                        for v_ in sl.values():
                            slot_bufs += (v_ if isinstance(v_, list) else [v_])
                    wv_ = w_kv.rearrange('(kc p) n -> p kc n', p=128)
                    for c0_ in range(0, 2 * D, 512):
                        dma('pool', wkv_pre[:, :, c0_:c0_ + 512], wv_[:, :, c0_:c0_ + 512], writes=slot_bufs)
                stage(5)
                y3 = lambda ap: ap.rearrange('p (h n) -> p h n', h=H)
                bc = lambda ap: ap.unsqueeze(2).broadcast_to([128, H, N])
                op('dve', 'tensor_reduce', [y_sb], [stt], out=stt[:, 0, :], in_=y3(y_sb[:]), axis=mybir.AxisListType.X, op=ALU.add)
                op('act', 'activation', [y_sb], [ysq], out=ysq[:], in_=y_sb[:], func=AF.Square)
                op('dve', 'tensor_reduce', [ysq, stt], [stt], out=stt[:, 1, :], in_=y3(ysq[:]), axis=mybir.AxisListType.X, op=ALU.add)
                op('dve', 'tensor_scalar', [stt], [stt], out=stt[:, 0, :], in0=stt[:, 0, :], scalar1=1.0 / N, scalar2=None, op0=ALU.mult)
                op('dve', 'tensor_tensor', [stt], [stt], out=stt[:, 2, :], in0=stt[:, 0, :], in1=stt[:, 0, :], op=ALU.mult)
                op('dve', 'scalar_tensor_tensor', [stt], [stt], out=stt[:, 1, :], in0=stt[:, 1, :], scalar=1.0 / N, in1=stt[:, 2, :],
                   op0=ALU.mult, op1=ALU.subtract)
                op('act', 'activation', [stt, epsb], [stt], out=stt[:, 1, :], in_=stt[:, 1, :], func=AF.Ln, bias=epsb[:, 2:3])
                op('act', 'activation', [stt], [stt], out=stt[:, 1, :], in_=stt[:, 1, :], func=AF.Exp, scale=-0.5)
                op('dve', 'tensor_tensor', [y_sb, stt], [yn], out=y3(yn[:]), in0=y3(y_sb[:]), in1=bc(stt[:, 0, :]), op=ALU.subtract)
                op('dve', 'tensor_tensor', [yn, stt], [yn], out=y3(yn[:]), in0=y3(yn[:]), in1=bc(stt[:, 1, :]), op=ALU.mult)
                op(('pool' if P2_POST_POOL else 'dve'), 'tensor_tensor', [yn, ptml], [yn], out=yn[:], in0=yn[:], in1=ptml[:, 0, :], op=ALU.mult)
                op(('pool' if P2_POST_POOL else 'dve'), 'tensor_tensor', [yn, ptml], [yn], out=yn[:], in0=yn[:], in1=ptml[:, 1, :], op=ALU.add)
                op(('pool' if P2_POST_POOL else 'dve'), 'tensor_tensor', [Vt, L['bon']], [bv], out=y3(bv[:]), in0=y3(Vt[:]), in1=bc(L['bon'][:]), op=ALU.mult)
                op(('pool' if P2_POST_POOL else 'dve'), 'tensor_tensor', [yn, bv], [yn], out=yn[:], in0=yn[:], in1=bv[:], op=ALU.add)
                op(('pool' if P2_POST_POOL else 'dve'), 'tensor_tensor', [yn, SGt], [yfin], out=yfin[:], in0=yn[:], in1=SGt[:], op=ALU.mult)
                stage(6)
                bk = psum.next()
                bkb = bk[:].bitcast(BF16)
                for j in range(NJ):
                    op('pe', 'transpose', [yfin, cbf], [bk], sig=(j == NJ - 1), out=bkb[:, j * 128:(j + 1) * 128],
                       in_=yfin[:, j * 128:(j + 1) * 128], identity=identb)
                op('act', 'activation', [bk], [yT], out=yT[:], in_=bkb.rearrange('p (j t) -> p j t', j=NJ), func=AF.Copy)
                xr = xr_o.next()
                for half in range(2):
                    bk = psum.next()
                    for kc in range(NJ):
                        op('pe', 'matmul', [yT, Wout], [bk], sig=(kc == NJ - 1), out=bk[:, :], lhsT=yT[:, kc, :],
                           rhs=Wout[:, kc, half * 512:(half + 1) * 512], start=(kc == 0), stop=(kc == NJ - 1))
                    hs = slice(half * 512, (half + 1) * 512)
                    op('dve', 'tensor_tensor', [bk, gateA], [t_o], out=t_o[:, hs], in0=bk[:, :], in1=gateA[:, hs], op=ALU.mult)
                    op('dve', 'tensor_tensor', [t_o, L['xin']], [xr], out=xr[:, hs], in0=t_o[:, hs], in1=L['xin'][:, hs], op=ALU.add)
                dma('sp', s_xr1[c * C:(c + 1) * C, :], xr[:], reads=[xr])
                if c < 6:
                    ada_ct(bctx2, 'b', c, load=False)
            kb.mute = False
            kb.barrier()

        s_KT = scr('s_KT', [NJ, 128, S], BF16)
        s_V = scr('s_V', [S, D], BF16)
        s_QT = scr('s_QT', [3 * NJ, 128, S], BF16)
        s_SG = scr('s_SG', [NJ, 128, S], BF16)
        if LAST_PHASE >= 3:
            kb.mute = 3 in SKIP_PHASES
            TB = 4
            T = TB * 128
            for sub in ('kv', 'q'):
                if sub == 'kv':
                    arena.reset()
                    arena.alloc([NJ, 2 * D], BF16, 'Wb')
                    if LAST_PHASE >= 2 and 2 not in SKIP_PHASES:
                        Wb = wkv_pre
                    else:
                        Wb = Buf(wkv_pre.ap if LAST_PHASE >= 2 else arena.base[:, 0:NJ * 2 * D].rearrange('p (a b) -> p a b', a=NJ), 'Wb')
                        load_w_bf16(Wb, w_kv.rearrange('(kc p) n -> p kc n', p=128), 2 * D)
                    Wq = arena.alloc_top([NJ, 4 * D], BF16, 'Wq')
                    load_w_bf16(Wq, b_w_in.rearrange('(kc p) n -> p kc n', p=128), 4 * D)
                else:
                    arena.reset(keep_top=True)
                    Wb = Wq
                qkg = arena.alloc([4], F32, 'qkg')
                TC = arena.alloc([S], F32, 'TC'); TS = arena.alloc([S], F32, 'TS')
                xts = [arena.alloc([TB, D], F32, 'xt3_%d' % i) for i in range(2)]
                rope = xts[1]
                rope_v = xts[1][:].rearrange('p a b -> p (a b)').rearrange('p (r s) -> p r s', r=2)
                bctx = None
                sq = arena.alloc([D], BF16, 'sq3'); ss = arena.alloc([4], F32, 'ss3')
                hT = arena.alloc([NJ, T], BF16, 'hT3'); hTj = split(hT, NJ)
                raw = Ring([arena.alloc([T], BF16, 'raw%d' % i) for i in range(4)])
                sqr = Ring([arena.alloc([T], BF16, 'sqr%d' % i) for i in range(4)])
                rs = Ring([arena.alloc([T], F32, 'rs%d' % i) for i in range(2)])
                t1 = Ring([arena.alloc([T], F32, 't1_%d' % i) for i in range(2)])
                t2 = Ring([arena.alloc([T], F32, 't2_%d' % i) for i in range(2)])
                ofm = Ring([arena.alloc([T], BF16, 'ofm%d' % i) for i in range(2)])
                otm = Ring([arena.alloc([D], BF16, 'otm%d' % i) for i in range(1)])
                load_rows(s_xr1, 0, TB, xts[0])
                dma('sp', rope_v, rope_d, writes=[rope])
                dma('sp', qkg[:], qkg_d, writes=[qkg])
                gi = 2 if sub == 'kv' else 0
                op('dve', 'tensor_scalar', [rope, qkg], [TC], out=TC[:], in0=rope_v[:, 0, :], scalar1=qkg[:, gi:gi + 1],
                   scalar2=(8.0 if sub == 'kv' else 1.0), op0=ALU.mult, op1=ALU.mult)
                op('dve', 'tensor_scalar', [rope, qkg], [TS], out=TS[:], in0=rope_v[:, 1, :], scalar1=qkg[:, gi + 1:gi + 2],
                   scalar2=(8.0 if sub == 'kv' else 1.0), op0=ALU.mult, op1=ALU.mult)
                if sub == 'kv':
                    gs_ap = lambda j: P('kv_norm_g', j)
                    sh_ap = lambda j: 0.0
                    fm_chunks = [(j, j * 128, s_KT, j) for j in range(NJ)]
                else:
                    gs_ap = lambda j: gsh[:, 2, j:j + 1]
                    sh_ap = lambda j: gsh[:, 3, j:j + 1]
                    fm_chunks = [(jq, jq * 128, s_QT, jq) for jq in range(3 * NJ)]
                for tt in range(S // T):
                    t0 = tt * T
                    xt = xts[tt % 2]
                    norm_transpose(TB, xt, sq, ss, xt, hT, hTj, gs_ap, sh_ap, 0)
                    if tt + 1 < S // T:
                        load_rows(s_xr1, t0 + T, TB, xts[(tt + 1) % 2])
                    if bctx is not None and tt < 3:
                        ada_load(bctx, 'b', 2 * tt)
                    cx = {}

                    def st0(i):
                        (ci, c0_, dst, di) = fm_chunks[i]
                        bk = psum.next()
                        for kc in range(NJ):
                            op('pe', 'matmul', [Wb, hTj[kc]], [bk], sig=(kc == NJ - 1), out=bk[:, :], lhsT=Wb[:, kc, c0_:c0_ + 128],
                               rhs=hT[:, kc, :], start=(kc == 0), stop=(kc == NJ - 1))
                        cx[i] = dict(bk=bk)

                    def st1(i):
                        c_ = cx[i]
                        raw_, sq_ = raw.next(), sqr.next()
                        op('act', 'activation', [c_['bk']], [raw_], out=raw_[:], in_=c_['bk'][:, :], func=AF.Copy)
                        op('act', 'activation', [c_['bk']], [sq_], out=sq_[:], in_=c_['bk'][:, :], func=AF.Square)
                        c_.update(raw=raw_, sq=sq_)

                    def st2(i):
                        c_ = cx[i]
                        b_ss, b_rot = psum.next(), psum.next()
                        t1_ = t1.next()
                        op('pe', 'matmul', [cbf, c_['sq']], [b_ss], out=b_ss[:, :], lhsT=blockones, rhs=c_['sq'][:], start=True, stop=True)
                        op('pe', 'matmul', [cbf, c_['raw']], [b_rot], out=b_rot[:, :], lhsT=rotp, rhs=c_['raw'][:], start=True, stop=True)
                        op('pool', 'tensor_tensor', [c_['raw'], TC], [t1_], out=t1_[:], in0=c_['raw'][:], in1=TC[:, t0:t0 + T], op=ALU.mult)
                        c_.update(t1=t1_, b_ss=b_ss, b_rot=b_rot)

                    def st3(i):
                        c_ = cx[i]
                        rs_ = rs.next()
                        op('act', 'activation', [c_['b_ss'], epsb], [rs_], out=rs_[:], in_=c_['b_ss'][:, :], func=AF.Ln, bias=epsb[:, 3:4])
                        op('act', 'activation', [rs_], [rs_], out=rs_[:], in_=rs_[:], func=AF.Exp, scale=-0.5)
                        t2_ = t2.next()
                        op('dve', 'tensor_tensor', [c_['b_rot'], TS], [t2_], out=t2_[:], in0=c_['b_rot'][:, :], in1=TS[:, t0:t0 + T], op=ALU.mult)
                        op('dve', 'tensor_tensor', [c_['t1'], t2_], [t2_], out=t2_[:], in0=c_['t1'][:], in1=t2_[:], op=ALU.add)
                        c_.update(rs=rs_, t2=t2_)

                    def st4(i):
                        (ci, c0_, dst, di) = fm_chunks[i]
                        c_ = cx.pop(i)
                        o_ = ofm.next()
                        op('dve', 'tensor_tensor', [c_['t2'], c_['rs']], [o_], out=o_[:], in0=c_['t2'][:], in1=c_['rs'][:], op=ALU.mult)
                        dma('sp', dst[di, :, t0:t0 + T], o_[:], reads=[o_])
                    wavefront(len(fm_chunks), [st0, st1, st2, st3, st4], order=(list(P3_ORDER) if P3_ORDER else None))
                    if bctx is not None and tt < 3:
                        ada_ct(bctx, 'b', 2 * tt, load=False)
                        ada_load(bctx, 'b', 2 * tt + 1)
                    if sub == 'kv':
                        for b in range(TB):
                            o_ = otm.next()
                            for half in range(2):
                                bk = psum.next()
                                for kc in range(NJ):
                                    op('pe', 'matmul', [hTj[kc], Wb], [bk], sig=(kc == NJ - 1), out=bk[:, :], lhsT=hT[:, kc, b * 128:(b + 1) * 128],
                                       rhs=Wb[:, kc, D + half * 512:D + (half + 1) * 512], start=(kc == 0), stop=(kc == NJ - 1))
                                op('act', 'activation', [bk], [o_], out=o_[:, half * 512:(half + 1) * 512], in_=bk[:, :], func=AF.Copy)
                            dma('sp', s_V[t0 + b * 128:t0 + (b + 1) * 128, :], o_[:], reads=[o_])
                        if bctx is not None and tt < 3:
                            ada_ct(bctx, 'b', 2 * tt + 1, load=False)
                    else:
                        for j in range(NJ):
                            bk = psum.next()
                            for kc in range(NJ):
                                op('pe', 'matmul', [Wb, hTj[kc]], [bk], sig=(kc == NJ - 1), out=bk[:, :],
                                   lhsT=Wb[:, kc, 3 * D + j * 128:3 * D + (j + 1) * 128], rhs=hT[:, kc, :], start=(kc == 0), stop=(kc == NJ - 1))
                            o_ = ofm.next()
                            op('act', 'activation', [bk], [o_], out=o_[:], in_=bk[:, :], func=AF.Silu)
                            dma('sp', s_SG[j, :, t0:t0 + T], o_[:], reads=[o_])
                kb.mute2 = kb.mute
                kb.mute = False
                kb.barrier()
                kb.mute = kb.mute2

        if LAST_PHASE >= 4:
            kb.mute = False
            arena.reset()
            yT = arena.alloc([NJ, S], BF16, 'yTall'); yTj = split(yT, NJ)

            def mk_pl(i):
                d_ = dict(KT=arena.alloc([S], BF16, 'KTp%d' % i), SG=arena.alloc([S], BF16, 'SGp%d' % i),
                          QTz=arena.alloc([2, 3, S], BF16, 'QTz%d' % i))
                op('pool', 'memset', [], [d_['QTz']], ap=d_['QTz'][:], constant=0.0)
                d_['VL'] = [[arena.alloc([16, 128], BF16, 'VL%d_%d_%d' % (i, g, h2)) for h2 in range(2)] for g in range(3)]
                for g in range(3):
                    for h2 in range(2):
                        op('pool', 'memset', [], [d_['VL'][g][h2]], ap=d_['VL'][g][h2][:], constant=1.0)
                return d_
            pls = [mk_pl(i) for i in range(2)]
            GR = ((1, 16), (4, 4), (16, 1))

            def pair_loads(j):
                L = pls[j % 2]
                dma('sp', L['KT'][:], s_KT[j], writes=[L['KT']])
                qv = s_QT.rearrange('(g j) p t -> j p g t', g=3)[j]
                for h2 in range(2):
                    pb = 64 * h2
                    dma('sp', L['QTz'][pb:pb + 64, h2, :, :], qv[pb:pb + 64, :, :], writes=[L['QTz']])
                dma('sp', L['SG'][:], s_SG[j], writes=[L['SG']])
                for g, (dil, nblk) in enumerate(GR):
                    for h2 in range(2):
                        dma('sp', L['VL'][g][h2][:, :, 64 * h2:64 * h2 + 64].rearrange('p (r nb) d -> p r nb d', r=dil),
                            s_V.rearrange('(nb p r) d -> p r nb d', p=128, r=dil)[:, :, :, j * 128 + 64 * h2:j * 128 + 64 * h2 + 64],
                            writes=[L['VL'][g][h2]])
                return L
            accO = arena.alloc([S], F32, 'accA'); accL = arena.alloc([S], F32, 'accB')
            rec = arena.alloc([S], F32, 'rec')
            Pm = Ring([arena.alloc([2, 256], BF16, 'Pm%d' % i) for i in range(P4_NPM)])
            mbias = cbf[:, 2944:3200]
            sw1 = cst[:, 128:256]; sw2 = cst[:, 256:384]
            bOr = [banks[0], banks[1]]
            bLr = [banks[2], banks[3]]
            sring = Ring(banks[4:8])
            L_next = pair_loads(0)
            for j in range(NJ):
                L_ = L_next
                if j + 1 < NJ:
                    L_next = pair_loads(j + 1)
                KT, QTz, SG, VL = L_['KT'], L_['QTz'], L_['SG'], L_['VL']
                items = []
                for g, (dil, nblk) in enumerate(GR):
                    for r in range(dil):
                        for kbi in range(nblk):
                            nq = 2 if kbi + 1 < nblk else 1
                            ncol = 128 * nq
                            st_ = dil * 128 * kbi + r
                            items.append(dict(g=g, dil=dil, nblk=nblk, r=r, kbi=kbi, nq=nq, ncol=ncol,
                                              kcols=slice(st_, st_ + dil * 127 + 1, dil),
                                              qcols=slice(st_, st_ + dil * (ncol - 1) + 1, dil)))
                v2 = lambda ap: ap.rearrange('p (h c) -> p h c', h=2)

                def emit_scores(it):
                    bS = sring.next()
                    ncol = it['ncol']
                    for h2 in range(2):
                        op('pe', 'matmul', [KT, QTz], [bS], sig=False, out=bS[:, h2 * 256:h2 * 256 + ncol], lhsT=KT[:, it['kcols']],
                           rhs=QTz[:, h2, it['g'], it['qcols']], start=True, stop=False)
                        op('pe', 'matmul', [cbf], [bS], sig=(h2 == 1), out=bS[:, h2 * 256:h2 * 256 + ncol], lhsT=identb,
                           rhs=mbias[:, 0:ncol], start=False, stop=True)
                    pm = Pm.next()
                    op('act', 'activation', [bS], [pm], out=pm[:, :, 0:ncol], in_=v2(bS[:, :])[:, :, 0:ncol], func=AF.Exp)
                    it['pm'] = pm

                def emit_pv(it):
                    g, dil, r, kbi, pm = it['g'], it['dil'], it['r'], it['kbi'], it['pm']
                    vblk = r * it['nblk'] + kbi
                    for qt in range(it['nq']):
                        nb = kbi + qt
                        reg = nb % 2
                        first = (qt == 1) or (nb == 0)
                        last = (qt == 0)
                        for h2, bq in ((0, bOr[reg]), (1, bLr[reg])):
                            op('pe', 'matmul', [VL[g][h2], pm], [bq], sig=(h2 == 1), out=bq[:, 0:128],
                               lhsT=VL[g][h2][:, vblk, :], rhs=pm[:, h2, qt * 128:(qt + 1) * 128], start=first, stop=last)
                        if last:
                            q0 = dil * 128 * nb + r
                            tcols = slice(q0, q0 + dil * 127 + 1, dil)
                            if g == 0:
                                op('dve', 'tensor_copy', [bOr[reg]], [accO], out=accO[:, tcols], in_=bOr[reg][:, 0:128])
                                op('act', 'activation', [bLr[reg]], [accL], out=accL[:, tcols], in_=bLr[reg][:, 0:128], func=AF.Copy)
                            else:
                                op('dve', 'tensor_tensor', [bOr[reg], accO], [accO], out=accO[:, tcols], in0=bOr[reg][:, 0:128],
                                   in1=accO[:, tcols], op=ALU.add)
                                op('dve', 'tensor_tensor', [bLr[reg], accL], [accL], out=accL[:, tcols], in0=bLr[reg][:, 0:128],
                                   in1=accL[:, tcols], op=ALU.add)
                LOOK = P4_LOOK
                for i in range(len(items) + LOOK):
                    if i < len(items):
                        emit_scores(items[i])
                    if i - LOOK >= 0:
                        emit_pv(items[i - LOOK])
                for q4 in range(S // 512):
                    cs_ = slice(q4 * 512, (q4 + 1) * 512)
                    bk = sring.next()
                    op('pe', 'matmul', [cst, accO], [bk], sig=False, out=bk[:, :], lhsT=sw1, rhs=accO[:, cs_], start=True, stop=False)
                    op('pe', 'matmul', [cst, accL], [bk], out=bk[:, :], lhsT=sw2, rhs=accL[:, cs_], start=False, stop=True)
                    op('act', 'activation', [bk], [rec], out=rec[:, cs_], in_=bk[:, :], func=AF.Ln)
                op('act', 'activation', [rec], [rec], out=rec[:], in_=rec[:], func=AF.Exp, scale=-1.0)
                for (pb, src) in ((0, accO), (64, accL)):
                    op('dve', 'tensor_tensor', [src, rec], [src], out=src[pb:pb + 64, :], in0=src[pb:pb + 64, :], in1=rec[pb:pb + 64, :], op=ALU.mult)
                    op('dve', 'tensor_tensor', [src, SG], [yTj[j]], out=yT[pb:pb + 64, j, :], in0=src[pb:pb + 64, :], in1=SG[pb:pb + 64, :], op=ALU.mult)
                if j == NJ - 2:
                    Wout = Buf(pls[0]['QTz'][:].rearrange('p a b c -> p (a b c)')[:, 0:NJ * D].rearrange('p (k n) -> p k n', k=NJ), 'WoutB')
                    wv_ = b_w_out.rearrange('(kc p) n -> p kc n', p=128)
                    for c0_ in range(0, D, 512):
                        dma('pool', Wout[:, :, c0_:c0_ + 512], wv_[:, :, c0_:c0_ + 512], writes=[Wout, pls[0]['QTz']])
            kb.barrier()
            xr_in = Ring([Buf(accO[:, 0:D], 'xr_in0'), Buf(accO[:, D:2 * D], 'xr_in1')])
            t_o = Buf(accL[:, 0:D], 't_o5')
            o5 = Ring([Buf(rec[:, 0:D], 'o5_0'), Buf(rec[:, D:2 * D], 'o5_1')])
            allbanks = Ring(banks)
            xi_next = xr_in.next()
            dma('sp', xi_next[:], s_xr1[0:128, :], writes=[xi_next])
            for b in range(S // 128):
                xi = xi_next
                if b + 1 < S // 128:
                    xi_next = xr_in.next()
                    dma('sp', xi_next[:], s_xr1[(b + 1) * 128:(b + 2) * 128, :], writes=[xi_next])
                oo = o5.next()
                for half in range(2):
                    bk = allbanks.next()
                    for kc in range(NJ):
                        op('pe', 'matmul', [yTj[kc], Wout], [bk], sig=(kc == NJ - 1), out=bk[:, :], lhsT=yT[:, kc, b * 128:(b + 1) * 128],
                           rhs=Wout[:, kc, half * 512:(half + 1) * 512], start=(kc == 0), stop=(kc == NJ - 1))
                    hs = slice(half * 512, (half + 1) * 512)
                    op('dve', 'tensor_tensor', [bk, gateB], [t_o], out=t_o[:, hs], in0=bk[:, :], in1=gateB[:, hs], op=ALU.mult)
                    op('dve', 'tensor_tensor', [t_o, xi], [oo], out=oo[:, hs], in0=t_o[:, hs], in1=xi[:, hs], op=ALU.add)
                dma('sp', out_d[b * 128:(b + 1) * 128, :], oo[:], reads=[oo])

        kb.barrier()
        kb.emit()
    return nc


def _host_layout(inputs, b):
    f32 = np.float32
    fm = lambda v: np.ascontiguousarray(np.asarray(v, f32).reshape(NJ, 128).T)
    d = {}
    pf = {}
    mu = inputs['a_mix_mu'][0]
    for p in range(6):
        pf['mu%d' % p] = fm(mu[p])
    pf['a_norm_g'] = fm(inputs['a_norm_g'][0]); pf['w0'] = fm(inputs['a_w0'][0]); pf['a0'] = fm(inputs['a_a0'][0])
    pf['k_k'] = fm(inputs['a_k_k'][0]); pf['k_a'] = fm(inputs['a_k_a'][0]); pf['r_k'] = fm(inputs['a_r_k'][0].reshape(-1))
    pf['kv_norm_g'] = fm(inputs['kv_norm_g']); pf['b_norm_g'] = fm(inputs['b_norm_g'][0])
    ab, bb = inputs['a_ada_b'][0], inputs['b_ada_b'][0]
    pf['a_ada_b_shift'] = fm(ab[:D]); pf['a_ada_b_scale'] = fm(ab[D:2 * D])
    pf['b_ada_b_shift'] = fm(bb[:D]); pf['b_ada_b_scale'] = fm(bb[D:2 * D])
    pf['c'] = fm(inputs['c'][b])
    d['pfm'] = np.ascontiguousarray(np.stack([pf[n] for n in PFM], axis=1))
    rep = lambda v: np.broadcast_to(np.asarray(v, f32)[None, :], (128, D))
    d['ptm'] = np.ascontiguousarray(np.stack([rep(inputs['a_ln_g'][0]), rep(inputs['a_ln_b'][0]),
                                              rep(ab[2 * D:]), rep(bb[2 * D:])], axis=1))
    return d


def _consts():
    f32 = np.float32
    ti = np.arange(128)
    ident = np.eye(128, dtype=f32)
    m_su = (ti[:, None] < ti[None, :]).astype(f32)
    m_ui = (ti[:, None] <= ti[None, :]).astype(f32)
    m_sl = (ti[:, None] > ti[None, :]).astype(f32)
    m2 = np.concatenate([m_su, m_ui], 1)
    m_su_ui = np.concatenate([m2, m2], 1)
    blockones = np.kron(np.eye(2, dtype=f32), np.ones((64, 64), f32))
    rot = np.zeros((128, 128), f32)
    for po in range(128):
        d_ = po % 64
        pi = po + 32 if d_ < 32 else po - 32
        rot[pi, po] = 1.0
    m_li = (ti[:, None] >= ti[None, :]).astype(f32)
    bd = ((ti[:, None] // 64) == (ti[None, :] // 64)).astype(f32)
    sw1 = (ti[:, None] == ti[None, :] + 64).astype(f32)
    sw2 = (ti[:, None] + 64 == ti[None, :]).astype(f32)
    m_off = ((ti[:, None] < 64) & (ti[None, :] >= 64)).astype(f32)
    cst = np.concatenate([ident, m_su_ui, m_sl, m_sl, ident, ident, blockones, rot, m_ui, m_li, m_ui, m_li,
                          m_su * bd, m_ui, m_su * bd, m_ui, m_sl * bd, m_sl * bd, m_off, m_off,
                          (np.concatenate([m_ui, m_li], 1) - 1.0) * 30000.0, sw1, sw2], 1).astype(f32)
    assert cst.shape[1] == CW
    sel = np.zeros((128, NJ, H), f32)
    for p in range(128):
        for j in range(NJ):
            sel[p, j, 2 * j + p // 64] = 1.0
    pos = np.arange(S, dtype=f32)
    inv = (np.float32(10000.0) ** (-np.arange(0, 64, 2, dtype=f32) / np.float32(64))).astype(f32)
    ang = pos[:, None] * inv[None, :]
    cos, sin = np.cos(ang).astype(f32), np.sin(ang).astype(f32)
    rope = np.zeros((128, 2, S), f32)
    for p in range(128):
        d_ = p % 64
        rope[p, 0] = cos[:, d_ % 32]
        rope[p, 1] = sin[:, d_ % 32] * (-1.0 if d_ < 32 else 1.0)
    return {'cst': cst, 'sel': sel, 'rope': rope}


def _qkg(qg, kg):
    idx = np.arange(128) % 64
    par = (idx + 32) % 64
    qg = np.asarray(qg, np.float32).reshape(64); kg = np.asarray(kg, np.float32).reshape(64)
    return np.ascontiguousarray(np.stack([qg[idx], qg[par], kg[idx], kg[par]], 1).astype(np.float32))


def kernel(**inputs):
    inputs = {k: np.asarray(v) for k, v in inputs.items()}
    n = 8
    nc = build_nc()
    consts = _consts()
    shared = {k: np.ascontiguousarray(inputs[k][0], dtype=np.float32) for k in
              ('a_ada_w', 'b_ada_w', 'a_w_in', 'b_w_in', 'a_w1', 'a_w2', 'a_a1', 'a_a2', 'a_w_out', 'b_w_out')}
    shared['w_kv'] = np.ascontiguousarray(inputs['w_kv'], dtype=np.float32)
    shared['qkg'] = _qkg(inputs['b_q_norm_g'][0], inputs['k_norm_g'])
    shared.update(consts)
    in_maps = []
    for b in range(n):
        m = dict(shared)
        m['x'] = np.ascontiguousarray(inputs['x'][b], dtype=np.float32)
        m.update(_host_layout(inputs, b))
        in_maps.append(m)
    res = run_bass_kernel_spmd(nc, in_maps, core_ids=list(range(n)))
    return np.stack([r['out'] for r in res.results], axis=0).astype(np.float32)
```

```python
import math
from contextlib import ExitStack
import numpy as np
import concourse.bass as bass
import concourse.mybir as mybir
from concourse.bass_utils import run_bass_kernel_spmd

F32, BF16 = mybir.dt.float32, mybir.dt.bfloat16
ALU = mybir.AluOpType
AF = mybir.ActivationFunctionType
D, S, H, N = 1024, 2048, 16, 64
NJ = 8
C = 128
NCH = S // C
ENG = ('pe', 'act', 'dve', 'pool', 'sp')
NDS = 24
C0 = math.exp(-0.5)
DEBUG = False
LAST_PHASE = 9
SAME_ENGINE_SYNC = True
P2STOP = 99
SKIP_PHASES = ()
EMBED_WAIT = True
NSTG = 6
TMP_AFTER = 1
TRANSITIVE = True
PE_EMBED = True
P2_ACOPY_ACT = 2
STQ = 'act'
P2_POST_POOL = 0
P1_ORDER = (0, 3, 1, 2)
P3_ORDER = None
P4_LOOK = 2
P4_NPM = 4
P2_MERGE_NA = 1
P2_XADD_PE = 0

PFM = ['mu0', 'mu1', 'mu2', 'mu3', 'mu4', 'mu5', 'a_norm_g', 'w0', 'a0', 'k_k', 'k_a', 'r_k',
       'kv_norm_g', 'b_norm_g', 'a_ada_b_shift', 'a_ada_b_scale', 'b_ada_b_shift', 'b_ada_b_scale', 'c']
PTM = ['ln_g', 'ln_b', 'a_gate_b', 'b_gate_b']
CW = 3456


class Buf:
    __slots__ = ('ap', 'w', 'r', 'name', 'excl')

    def __init__(self, ap, name='', excl=False):
        self.ap, self.w, self.r, self.name, self.excl = ap, None, {}, name, excl

    def __getitem__(self, idx):
        return self.ap[idx]


def split(buf, n):
    return [Buf(buf.ap[:, i], '%s[%d]' % (buf.name, i)) for i in range(n)]


class KB:
    def __init__(self, nc, stack):
        self.nc = nc
        self.q = {e: [] for e in ENG}
        self.sem = {e: stack.enter_context(nc.semaphore('s_' + e)) for e in ENG}
        self.cnt = {e: 0 for e in ENG}
        self.seen = {e: {} for e in ENG}
        self.dsems = [stack.enter_context(nc.semaphore('d%d' % i)) for i in range(NDS)]
        self.dval = [0] * NDS
        self.dnext = 0
        self.mute = False
        self.simq = {e: [] for e in ENG}
        self.know = {}

    def _semof(self, key):
        return self.sem[key] if isinstance(key, str) else self.dsems[key[1]]

    def _deps(self, eng, reads, writes, extra=()):
        waits = {}

        def need(key, val):
            if key == eng and (eng == 'pe' or not SAME_ENGINE_SYNC):
                return
            if self.seen[eng].get(key, 0) >= val:
                return
            if waits.get(key, 0) < val:
                waits[key] = val
        for b in reads:
            if b.w:
                need(*b.w)
        self.read_keys = set(waits)
        for b in writes:
            if b.w:
                need(*b.w)
            for k, v in b.r.items():
                need(k, v)
        for k, v in extra:
            need(k, v)
        if TRANSITIVE and waits:
            waits = {k: v for k, v in waits.items() if not any(
                k != k3 and self.know.get((k3, v3), {}).get(k, 0) >= v for k3, v3 in waits.items())}
            sn = self.seen[eng]
            for k, v in waits.items():
                for k2, v2 in self.know.get((k, v), {}).items():
                    if sn.get(k2, 0) < v2:
                        sn[k2] = v2
        for k, v in waits.items():
            self.seen[eng][k] = v
        self.last_wk = list(waits.items())
        return [(self._semof(k), v) for k, v in waits.items()]

    def op(self, eng, name, reads, writes, sig=True, **kw):
        if self.mute:
            return
        writes = list(writes) + [b for b in reads if b.excl]
        reads = [b for b in reads if not b.excl]
        wl = self._deps(eng, reads, writes)
        if sig:
            self.cnt[eng] += 1
            seq = self.cnt[eng]
        else:
            seq = self.cnt[eng] + 1
        if eng == 'pe' and PE_EMBED and name in ('matmul', 'transpose'):
            wo = [i_ for i_, (k_, v_) in enumerate(self.last_wk) if k_ not in self.read_keys]
            if wo:
                i_ = wo[0]
                wl = wl[:i_] + wl[i_ + 1:] + [wl[i_]]
                kw = dict(kw, _embed_last=True)
        self.q[eng].append((wl, name, kw, self.sem[eng] if sig else None, 1))
        if sig and TRANSITIVE:
            self.know[(eng, seq)] = dict(self.seen[eng])
        self.simq[eng].append((self.last_wk, name, kw, (eng if sig else None), eng))
        for b in reads:
            b.r[eng] = max(b.r.get(eng, 0), seq)
        for b in writes:
            b.w = (eng, seq)
            b.r = {}

    def dma(self, eng, out, in_, reads=(), writes=(), **kw):
        if self.mute:
            return
        i = self.dnext
        self.dnext = (i + 1) % NDS
        prev = self.dval[i]
        key = ('d', i)
        wl = self._deps(eng, reads, writes, extra=((key, prev),) if prev else ())
        val = prev + 16
        self.dval[i] = val
        kw = dict(kw, out=out, in_=in_)
        self.q[eng].append((wl, 'dma_start', kw, self.dsems[i], 16))
        if TRANSITIVE:
            self.know[(key, val)] = dict(self.seen[eng])
        self.simq[eng].append((self.last_wk, 'dma_start', kw, key, eng))
        for b in reads:
            b.r[key] = val
        for b in writes:
            b.w = (key, val)
            b.r = {}

    def barrier(self):
        targets = [(e, self.cnt[e]) for e in ENG if self.cnt[e]] + \
                  [(('d', i), self.dval[i]) for i in range(NDS) if self.dval[i]]
        for eng in ENG:
            wl = self._deps(eng, (), (), extra=targets)
            self.q[eng].append((wl, None, None, None, 0))
            self.simq[eng].append((self.last_wk, None, None, None, eng))

    def emit(self):
        nc = self.nc

        def play(e, lst, ename=''):
            for (wl, name, kw, sem, inc) in lst:
                pe_embed = False
                if name is not None and kw is not None and kw.get('_embed_last'):
                    kw = {k_: v_ for k_, v_ in kw.items() if k_ != '_embed_last'}
                    pe_embed = True
                embed = (EMBED_WAIT and len(wl) > 0 and name in ('activation', 'tensor_tensor', 'tensor_scalar', 'tensor_copy', 'scalar_tensor_tensor', 'tensor_reduce', 'tensor_tensor_scan', 'memset')
                         and kw.get('accum_out') is None and ename != 'pe')
                for s, v in (wl[1:] if embed else (wl[:-1] if pe_embed else wl)):
                    e.wait_ge(s, v)
                if name is None:
                    continue
                ins = getattr(e, name)(**kw)
                if embed:
                    ins._wait_ge(wl[0][0], wl[0][1])
                elif pe_embed:
                    ins._wait_ge(wl[-1][0], wl[-1][1])
                if sem is not None:
                    ins.then_inc(sem, inc)
        with nc.Block() as block:
            @block.tensor
            def _(e):
                play(e, self.q['pe'], 'pe')

            @block.scalar
            def _(e):
                play(e, self.q['act'])

            @block.vector
            def _(e):
                play(e, self.q['dve'])

            @block.gpsimd
            def _(e):
                play(e, self.q['pool'])

            @block.sync
            def _(e):
                play(e, self.q['sp'])


class Arena:
    def __init__(self, ap_bf16, nbytes):
        self.base, self.cap, self.off, self.top = ap_bf16, nbytes, 0, nbytes

    def reset(self, keep_top=False):
        self.off = 0
        if not keep_top:
            self.top = self.cap

    def alloc_top(self, shape, dt, name=''):
        n = int(np.prod(shape))
        nb = (n * (4 if dt == F32 else 2) + 63) // 64 * 64
        self.top -= nb
        assert self.off <= self.top, ('arena overflow (top)', name, self.off, self.top)
        v = self.base[:, self.top // 2:(self.top + nb) // 2]
        if dt == F32:
            v = v.bitcast(F32)
        v = v[:, 0:n]
        if len(shape) == 2:
            v = v.rearrange('p (a b) -> p a b', a=shape[0])
        return Buf(v, name)

    def alloc(self, shape, dt, name=''):
        n = int(np.prod(shape))
        nb = n * (4 if dt == F32 else 2)
        nb = (nb + 63) // 64 * 64
        assert self.off + nb <= self.top, ('arena overflow', name, self.off, nb, self.top)
        v = self.base[:, self.off // 2:(self.off + nb) // 2]
        self.off += nb
        if dt == F32:
            v = v.bitcast(F32)
        v = v[:, 0:n]
        if len(shape) == 2:
            v = v.rearrange('p (a b) -> p a b', a=shape[0])
        elif len(shape) == 3:
            v = v.rearrange('p (a b c) -> p a b c', a=shape[0], b=shape[1])
        return Buf(v, name)


class Ring:
    def __init__(self, bufs):
        self.bufs, self.i = bufs, 0

    def next(self):
        b = self.bufs[self.i]
        self.i = (self.i + 1) % len(self.bufs)
        return b


def wavefront(n, stages, order=None):
    ns = len(stages)
    order = list(reversed(range(ns))) if order is None else order
    for w in range(n + ns - 1):
        for s_ in order:
            i = w - s_
            if 0 <= i < n:
                stages[s_](i)


def build_nc():
    nc = bass.Bass("TRN2", target_bir_lowering=False)
    dram_in = lambda name, shape: nc.dram_tensor(name, list(shape), F32, kind="ExternalInput").ap()
    skind = "ExternalOutput" if DEBUG else "Internal"
    scr = lambda name, shape, dt: nc.dram_tensor(name, list(shape), dt, kind=skind).ap()

    x_d = dram_in('x', [S, D])
    pfm_d = dram_in('pfm', [128, len(PFM), NJ])
    ptm_d = dram_in('ptm', [128, len(PTM), D])
    cst_d = dram_in('cst', [128, CW])
    sel_d = dram_in('sel', [128, NJ, H])
    rope_d = dram_in('rope', [128, 2, S])
    qkg_d = dram_in('qkg', [128, 4])
    a_ada_w = dram_in('a_ada_w', [D, 3 * D]); b_ada_w = dram_in('b_ada_w', [D, 3 * D])
    a_w_in = dram_in('a_w_in', [D, 4 * D]); b_w_in = dram_in('b_w_in', [D, 4 * D])
    a_w1 = dram_in('a_w1', [D, 64]); a_w2 = dram_in('a_w2', [64, D])
    a_a1 = dram_in('a_a1', [D, 64]); a_a2 = dram_in('a_a2', [64, D])
    a_w_out = dram_in('a_w_out', [D, D]); b_w_out = dram_in('b_w_out', [D, D])
    w_kv = dram_in('w_kv', [D, 2 * D])
    out_d = nc.dram_tensor('out', [S, D], F32, kind="ExternalOutput").ap()

    s_ar = scr('s_ar', [NJ, 128, NCH, 2, C], BF16)
    s_kt = scr('s_kt', [NJ, 128, S], BF16)
    s_bt = scr('s_bt', [NJ, 128, S], BF16)
    s_tm = {n: scr('s_' + n, [S, D], BF16) for n in ('At', 'Bbt', 'Kbt', 'Vt', 'SGt')}
    s_bonus = scr('s_bonus', [S, H], F32)
    s_xr1 = scr('s_xr1', [S, D], F32)

    with ExitStack() as st:
        sb = lambda name, shape, dt: st.enter_context(nc.sbuf_tensor('sb_' + name, list(shape), dt))
        kb = KB(nc, st)
        op, dma = kb.op, kb.dma
        pfm = Buf(sb('pfm', [128, len(PFM), NJ], F32)[:], 'pfm')
        cst = Buf(sb('cst', [128, 384], F32)[:], 'cst')
        cbf = Buf(sb('cbf', [128, CW], BF16)[:], 'cbf')
        sel = Buf(sb('sel', [128, NJ, H], F32)[:], 'sel')
        selb = Buf(sb('selb', [128, NJ, H], BF16)[:], 'selb')
        modA = Buf(sb('modA', [128, 16], F32)[:], 'modA')
        modB = Buf(sb('modB', [128, 16], F32)[:], 'modB')
        gsh = Buf(sb('gsh', [128, 4, NJ], F32)[:], 'gsh')
        gateA = Buf(sb('gateA', [128, D], F32)[:], 'gateA')
        gateB = Buf(sb('gateB', [128, D], F32)[:], 'gateB')
        glast = Buf(sb('glast', [128, NJ, NCH], F32)[:], 'glast')
        ST_all = Buf(sb('ST', [128, NJ, N], F32)[:], 'ST')
        STb_all = Buf(sb('STb', [128, NJ, N], BF16)[:], 'STb')
        ST, STb = split(ST_all, NJ), split(STb_all, NJ)
        GT = [Buf(sb('GT%d' % j, [128, 128], F32)[:], 'GT%d' % j) for j in range(NJ)]
        zeros = Buf(sb('zeros', [128, 512], F32)[:], 'zeros')
        epsb = Buf(sb('epsb', [128, 8], F32)[:], 'epsb')
        npar = Buf(sb('npar', [128, 2, NJ], F32)[:], 'npar')
        ARENA_BYTES = 180 * 1024
        arena = Arena(sb('arena', [128, ARENA_BYTES // 2], BF16)[:], ARENA_BYTES)
        banks = [Buf(st.enter_context(nc.psum_tensor('bank%d' % i, [128, 512], F32))[:], 'bank%d' % i, excl=True)
                 for i in range(8)]
        psum = Ring(banks)

        pidx = {n: i for i, n in enumerate(PFM)}
        P = lambda name, j: pfm[:, pidx[name], j:j + 1]
        ident = cst[:, 0:128]
        identb = cbf[:, 0:128]
        m_su_ui = cbf[:, 128:640].rearrange('p (h c) -> p h c', h=2)
        m_sbd_ui = cbf[:, 1920:2432].rearrange('p (h c) -> p h c', h=2)
        m_slbd2 = cbf[:, 2432:2688].rearrange('p (h c) -> p h c', h=2)
        m_off2 = cbf[:, 2688:2944].rearrange('p (h c) -> p h c', h=2)
        ident2b = cbf[:, 896:1152].rearrange('p (h c) -> p h c', h=2)
        blockones = cbf[:, 1152:1280]
        rotp = cbf[:, 1280:1408]

        dma('sp', pfm[:], pfm_d, writes=[pfm])
        dma('sp', sel[:], sel_d, writes=[sel])
        op('dve', 'tensor_copy', [sel], [selb], out=selb[:], in_=sel[:])
        op('dve', 'tensor_scalar', [pfm], [npar], out=npar[:, 0, :], in0=pfm[:, pidx['w0'], :], scalar1=-1.0, scalar2=None, op0=ALU.mult)
        op('dve', 'tensor_scalar', [pfm, npar], [npar], out=npar[:, 1, :], in0=pfm[:, pidx['a0'], :], scalar1=-1.0, scalar2=None, op0=ALU.mult)
        op('pool', 'memset', [], [zeros], ap=zeros[:], constant=0.0)
        for i_, v_ in enumerate((1e-6, 1e-24, 64e-5, 64e-6, 1.0)):
            op('pool', 'memset', [epsb], [epsb], ap=epsb[:, i_:i_ + 1], constant=v_)
        op('pool', 'memset', [], ST, ap=ST_all[:], constant=0.0)
        op('pool', 'memset', [], STb, ap=STb_all[:], constant=0.0)
        for j in range(NJ):
            op('pool', 'memset', [], [GT[j]], ap=GT[j][:], constant=0.0)

        LAYERS = {'a': (a_ada_w, modA, gateA, 'a_ada_b_shift', 'a_ada_b_scale', 0, 0, 'a_norm_g'),
                  'b': (b_ada_w, modB, gateB, 'b_ada_b_shift', 'b_ada_b_scale', 1, 2, 'b_norm_g')}

        def ada_setup(nbuf=2):
            silc = arena.alloc([NJ], F32, 'silc')
            ptmg = arena.alloc([2, D], F32, 'ptmg')
            dma('sp', ptmg[:], ptm_d[:, 2:4, :], writes=[ptmg])
            silrep = arena.alloc([NJ, 128], F32, 'silrep')
            adaw_p = Ring([arena.alloc([NJ, 512], F32, 'adaw%d' % i) for i in range(nbuf)])
            op('act', 'activation', [pfm], [silc], out=silc[:], in_=pfm[:, pidx['c'], :], func=AF.Silu)
            for kc in range(NJ):
                op('dve', 'tensor_scalar', [silc, zeros], [silrep], out=silrep[:, kc, :], in0=zeros[:, 0:128],
                   scalar1=silc[:, kc:kc + 1], scalar2=None, op0=ALU.add)
            return dict(silc=silc, ptmg=ptmg, silrep=silrep, adaw_p=adaw_p)

        def ada_load(ctx, layer, ct):
            ada_w = LAYERS[layer][0]
            wv = ada_w.rearrange('(kc p) n -> p kc n', p=128)
            wt = ctx['adaw_p'].next()
            dma('sp', wt[:], wv[:, :, ct * 512:(ct + 1) * 512], writes=[wt])
            ctx['wt'] = wt

        def ada_ct(ctx, layer, ct, load=True):
            (ada_w, mod, gate, bshift, bscale, gi_, li, ng) = LAYERS[layer]
            silc, ptmg, silrep = ctx['silc'], ctx['ptmg'], ctx['silrep']
            if load:
                ada_load(ctx, layer, ct)
            wt = ctx['wt']
            bk = psum.next()
            if ct < 4:
                for fc in range(4):
                    col = ct * 4 + fc
                    for kc in range(NJ):
                        op('pe', 'matmul', [wt, silc], [bk], sig=(kc == NJ - 1), out=bk[:, col:col + 1],
                           lhsT=wt[:, kc, fc * 128:(fc + 1) * 128], rhs=silc[:, kc:kc + 1],
                           start=(kc == 0), stop=(kc == NJ - 1))
                op('dve', 'tensor_copy', [bk], [mod], out=mod[:, ct * 4:ct * 4 + 4], in_=bk[:, ct * 4:ct * 4 + 4])
            else:
                for kc in range(NJ):
                    op('pe', 'matmul', [wt, silrep], [bk], sig=(kc == NJ - 1), out=bk[:, :],
                       lhsT=silrep[:, kc, :], rhs=wt[:, kc, :], start=(kc == 0), stop=(kc == NJ - 1))
                c0_ = (ct - 4) * 512
                op('dve', 'tensor_tensor', [bk, ptmg], [gate], out=gate[:, c0_:c0_ + 512], in0=bk[:, :],
                   in1=ptmg[:, gi_, c0_:c0_ + 512], op=ALU.add)
            if ct == 5:
                op('dve', 'tensor_tensor', [mod, pfm, gsh], [gsh], out=gsh[:, li + 1, :], in0=mod[:, 0:8],
                   in1=pfm[:, pidx[bshift], :], op=ALU.add)
                op('dve', 'tensor_tensor', [mod, pfm, gsh], [gsh], out=gsh[:, li, :], in0=mod[:, 8:16],
                   in1=pfm[:, pidx[bscale], :], op=ALU.add)
                op('dve', 'scalar_tensor_tensor', [gsh, pfm], [gsh], out=gsh[:, li, :], in0=gsh[:, li, :], scalar=1.0,
                   in1=pfm[:, pidx[ng], :], op0=ALU.add, op1=ALU.mult)

        def load_w_bf16(dst, src_view, ncols, cw=512):
            for c0_ in range(0, ncols, cw):
                dma('pool', dst[:, :, c0_:c0_ + cw], src_view[:, :, c0_:c0_ + cw], writes=[dst])

        arena.reset()
        Win = arena.alloc_top([NJ, 4 * D], BF16, 'Win')
        load_w_bf16(Win, a_w_in.rearrange('(kc p) n -> p kc n', p=128), 4 * D)
        cstage = arena.alloc([CW], F32, 'cstage')
        dma('sp', cstage[:], cst_d, writes=[cstage])
        op('dve', 'tensor_copy', [cstage], [cbf], out=cbf[:], in_=cstage[:])
        op('act', 'activation', [cstage], [cst], out=cst[:, 0:128], in_=cstage[:, 0:128], func=AF.Copy)
        op('act', 'activation', [cstage, cst], [cst], out=cst[:, 128:384], in_=cstage[:, 3200:3456], func=AF.Copy)
        actx = ada_setup()
        for ct in range(6):
            ada_ct(actx, 'a', ct)
        kb.barrier()

        def load_rows(src_dram, t0, TB, xt):
            dma('sp', xt[:, 0:TB, :], src_dram.rearrange('(b p) d -> p b d', p=128)[:, t0 // 128:t0 // 128 + TB, :],
                writes=[xt])

        def norm_transpose(TB, xt, sq, ss, xn, hT, hTj, gs_ap, sh_ap, col0):
            for b in range(TB):
                op('act', 'activation', [xt], [sq, ss], out=sq[:], in_=xt[:, b, :], func=AF.Square,
                   accum_out=ss[:, b:b + 1])
            op('act', 'activation', [ss, epsb], [ss], out=ss[:, 0:TB], in_=ss[:, 0:TB], func=AF.Ln, scale=1.0 / D, bias=epsb[:, 0:1])
            op('act', 'activation', [ss], [ss], out=ss[:, 0:TB], in_=ss[:, 0:TB], func=AF.Exp, scale=-0.5)
            for b in range(TB):
                op('act', 'activation', [xt, ss], [xn], out=xn[:, b, :], in_=xt[:, b, :], func=AF.Copy,
                   scale=ss[:, b:b + 1])
            for j in range(NJ):
                bk = psum.next()
                for b in range(TB):
                    op('pe', 'transpose', [xn, cst], [bk], sig=(b == TB - 1), out=bk[:, b * 128:(b + 1) * 128],
                       in_=xn[:, b, j * 128:(j + 1) * 128], identity=ident)
                if j % 2 == 0:
                    op('dve', 'tensor_scalar', [bk, gsh], [hTj[j]], out=hT[:, j, col0:col0 + TB * 128],
                       in0=bk[:, 0:TB * 128], scalar1=gs_ap(j), scalar2=sh_ap(j), op0=ALU.mult, op1=ALU.add)
                else:
                    op('act', 'activation', [bk, gsh], [hTj[j]], out=hT[:, j, col0:col0 + TB * 128],
                       in_=bk[:, 0:TB * 128], func=AF.Identity, scale=gs_ap(j), bias=sh_ap(j))

        if LAST_PHASE >= 1:
            kb.mute = 1 in SKIP_PHASES
            TB = 2
            T = TB * 128
            arena.reset(keep_top=True)
            W1 = arena.alloc([NJ, 64], BF16, 'W1'); A1 = arena.alloc([NJ, 64], BF16, 'A1')
            W2 = arena.alloc([D], BF16, 'W2'); A2 = arena.alloc([D], BF16, 'A2')
            xt_r = Ring([arena.alloc([TB, D], F32, 'xt%d' % i) for i in range(1)])
            ss = arena.alloc([4], F32, 'ss')
            hT = arena.alloc([NJ, T + 1], F32, 'hT'); hTj = split(hT, NJ)
            xx = arena.alloc([NJ, T], F32, 'xx'); xxj = split(xx, NJ)
            _xsb = [arena.alloc([NJ, T], BF16, 'xs%d' % i) for i in range(2)]
            xs_r_ = Ring([(b_, split(b_, NJ)) for b_ in _xsb])
            lt = Ring([arena.alloc([T], BF16, 'lt%d' % i) for i in range(2)])
            _tsz = {'rk': (3, [2, T]), 'sg': (2, [2, T]), 'cs': (2, [T]), 'tqa': (2, [T]), 'tqb': (1, [T]), 'kkr': (2, [T]), 'k2': (3, [T]),
                    'gam': (2, [T]), 'ginv': (2, [T]), 'gprev': (2, [T]), 'ginvl': (2, [T]), 'rn': (1, [T]), 'kkn': (1, [T]), 'b_': (1, [T])}
            tmp = {n: Ring([arena.alloc(sh_, F32, n + str(i)) for i in range(k_)]) for n, (k_, sh_) in _tsz.items()}
            sqb = Ring([arena.alloc([T], BF16, 'sqb%d' % i) for i in range(2)])
            o_ar = arena.alloc([NJ, TB, 2 * C], BF16, 'o_ar'); o_arj = split(o_ar, NJ)
            o_k = arena.alloc([NJ, T], BF16, 'o_k'); o_kj = split(o_k, NJ)
            o_b = arena.alloc([NJ, T], BF16, 'o_b'); o_bj = split(o_b, NJ)
            o_kb = arena.alloc([NJ, T], BF16, 'o_kb'); o_kbj = split(o_kb, NJ)
            o_bb = arena.alloc([NJ, T], BF16, 'o_bb'); o_bbj = split(o_bb, NJ)
            o_rk = arena.alloc([NJ, T], BF16, 'o_rk'); o_rkj = split(o_rk, NJ)
            stg = Ring([arena.alloc([D], BF16, 'stg%d' % i) for i in range(NSTG)])
            sq = stg.bufs[2]
            bon = arena.alloc([TB, H], F32, 'bon')
            nbias = Ring([arena.alloc([TB], F32, 'nbias%d' % i) for i in range(2)])

            dma('pool', W1[:], a_w1.rearrange('(kc p) n -> p kc n', p=128), writes=[W1])
            dma('pool', A1[:], a_a1.rearrange('(kc p) n -> p kc n', p=128), writes=[A1])
            dma('pool', W2[0:64, :], a_w2, writes=[W2])
            dma('pool', A2[0:64, :], a_a2, writes=[A2])
            op('dve', 'memset', [], hTj, ap=hT[:, :, 0:1], constant=0.0)
            nT = S // T
            xt = xt_r.next()
            load_rows(x_d, 0, TB, xt)
            for tt in range(nT):
                t0 = tt * T
                norm_transpose(TB, xt, sq, ss, xt, hT, hTj, lambda j: gsh[:, 0, j:j + 1], lambda j: gsh[:, 1, j:j + 1], 1)
                if tt + 1 < nT:
                    load_rows(x_d, t0 + T, TB, xt)
                op('dve', 'tensor_tensor', hTj, xxj, out=xx[:], in0=hT[:, :, 0:T], in1=hT[:, :, 1:T + 1], op=ALU.subtract)

                def make_xs(p):
                    xs, xsj = xs_r_.next()
                    for j in range(NJ):
                        op('dve', 'scalar_tensor_tensor', [xxj[j], hTj[j], pfm], [xsj[j]], out=xs[:, j, :], in0=xx[:, j, :],
                           scalar=P('mu%d' % p, j), in1=hT[:, j, 1:T + 1], op0=ALU.mult, op1=ALU.add)
                    return xs, xsj

                def lora_mid(xs, xsj, Wl, func):
                    l_ = lt.next()
                    bk = psum.next()
                    for kc in range(NJ):
                        op('pe', 'matmul', [Wl, xsj[kc]], [bk], sig=(kc == NJ - 1), out=bk[0:64, 0:T], lhsT=Wl[:, kc, :],
                           rhs=xs[:, kc, :], start=(kc == 0), stop=(kc == NJ - 1))
                    op('act', 'activation', [bk], [l_], out=l_[0:64, :], in_=bk[0:64, 0:T], func=func)
                    return l_
                xs_w, xs_wj = make_xs(4)
                ltw = lora_mid(xs_w, xs_wj, W1, AF.Tanh)
                xs_a, xs_aj = make_xs(5)
                lta = lora_mid(xs_a, xs_aj, A1, AF.Copy)
                xs_r, xs_rj = make_xs(0)
                xs_k, xs_kj = make_xs(1)

                jx = {}
                v3 = lambda ap: ap.rearrange('p (b t) -> p b t', b=TB)

                def sP(j):
                    fs = slice(j * 128, (j + 1) * 128)
                    b_rk, b_z = psum.next(), psum.next()
                    for (c0b, xs_, xsj_, cb) in ((0, xs_r, xs_rj, 0), (T, xs_k, xs_kj, D)):
                        for kc in range(NJ):
                            op('pe', 'matmul', [Win, xsj_[kc]], [b_rk], sig=(kc == NJ - 1), out=b_rk[:, c0b:c0b + T],
                               lhsT=Win[:, kc, cb + j * 128:cb + (j + 1) * 128], rhs=xs_[:, kc, :],
                               start=(kc == 0), stop=(kc == NJ - 1))
                    op('pe', 'matmul', [W2, ltw], [b_z], out=b_z[:, 0:T], lhsT=W2[0:64, fs], rhs=ltw[0:64, :], start=True, stop=True)
                    op('pe', 'matmul', [A2, lta], [b_z], out=b_z[:, T:2 * T], lhsT=A2[0:64, fs], rhs=lta[0:64, :], start=True, stop=True)
                    jx[j] = dict(b_rk=b_rk, b_z=b_z)

                def sA(j):
                    b_rk, b_z = jx[j]['b_rk'], jx[j]['b_z']
                    t_ = {n: tmp[n].next() for n in ('rk', 'sg', 'cs', 'tqa', 'tqb', 'kkr', 'k2')}
                    rk_, sg_, cs, tqa, tqb, kkr, k2 = (t_[n] for n in ('rk', 'sg', 'cs', 'tqa', 'tqb', 'kkr', 'k2'))
                    r_, k_, s1, ic = rk_[:, 0, :], rk_[:, 1, :], sg_[:, 0, :], sg_[:, 1, :]
                    nb_ = nbias.next()
                    op('act', 'activation', [b_rk], [rk_], out=rk_[:], in_=b_rk[:, 0:2 * T].rearrange('p (a t) -> p a t', a=2), func=AF.Copy)
                    for pi_ in range(2):
                        op('act', 'activation', [b_z, npar], [sg_], out=sg_[:, pi_, :], in_=b_z[:, pi_ * T:(pi_ + 1) * T], func=AF.Exp, scale=-1.0,
                           bias=npar[:, pi_, j:j + 1])
                    op('act', 'activation', [sg_, epsb], [sg_], out=sg_[:], in_=sg_[:], func=AF.Ln, bias=epsb[:, 4:5])
                    op('act', 'activation', [sg_], [sg_], out=sg_[:], in_=sg_[:], func=AF.Exp, scale=-1.0)
                    for b in range(TB):
                        bs = slice(b * C, (b + 1) * C)
                        op('dve', 'tensor_tensor_scan', [sg_, zeros], [cs], out=cs[:, bs], data0=s1[:, bs], data1=zeros[:, 0:C],
                           initial=0.0, op0=ALU.add, op1=ALU.add)
                    op('dve', 'tensor_scalar', [cs], [nb_], out=nb_[:, 0:TB], in0=cs[:, C - 1::C], scalar1=-C0, scalar2=None, op0=ALU.mult)
                    op('dve', 'tensor_tensor', [cs, sg_], [tqa], out=tqa[:], in0=cs[:], in1=s1, op=ALU.subtract)
                    op('dve', 'tensor_scalar', [rk_, pfm], [kkr], out=kkr[:], in0=k_, scalar1=P('k_k', j), scalar2=None, op0=ALU.mult)
                    op('dve', 'tensor_scalar', [sg_, pfm], [tqb], out=tqb[:], in0=ic, scalar1=1.0, scalar2=P('k_a', j),
                       op0=ALU.subtract, op1=ALU.mult)
                    op('dve', 'scalar_tensor_tensor', [tqb, rk_], [k2], out=k2[:], in0=tqb[:], scalar=1.0, in1=k_, op0=ALU.add, op1=ALU.mult)
                    jx[j] = dict(rk=rk_, sg=sg_, cs=cs, tqa=tqa, kkr=kkr, k2=k2, nb=nb_)

                def sB(j):
                    c_ = jx[j]
                    cs, nb_, kkr = c_['cs'], c_['nb'], c_['kkr']
                    t_ = {n: tmp[n].next() for n in ('gam', 'ginv', 'gprev', 'ginvl')}
                    gam, ginv, gprev, ginvl = (t_[n] for n in ('gam', 'ginv', 'gprev', 'ginvl'))
                    op('act', 'activation', [cs], [gam], out=gam[:], in_=cs[:], func=AF.Exp, scale=-C0)
                    op('act', 'activation', [cs], [ginv], out=ginv[:], in_=cs[:], func=AF.Exp, scale=C0)
                    op('act', 'activation', [c_['tqa']], [gprev], out=gprev[:], in_=c_['tqa'][:], func=AF.Exp, scale=-C0)
                    for b in range(TB):
                        bs = slice(b * C, (b + 1) * C)
                        op('act', 'activation', [cs, nb_], [ginvl], out=ginvl[:, bs], in_=cs[:, bs], func=AF.Exp, scale=C0, bias=nb_[:, b:b + 1])
                    op('act', 'activation', [gam], [glast], out=glast[:, j, tt * TB:(tt + 1) * TB], in_=gam[:, C - 1::C], func=AF.Copy)
                    sq_ = sqb.next()
                    op('act', 'activation', [kkr], [sq_], out=sq_[:], in_=kkr[:], func=AF.Square)
                    b_ss = psum.next()
                    op('pe', 'matmul', [cbf, sq_], [b_ss], out=b_ss[:, 0:T], lhsT=blockones, rhs=sq_[:], start=True, stop=True)
                    c_.update(gam=gam, ginv=ginv, gprev=gprev, ginvl=ginvl, b_ss=b_ss)

                def sC(j):
                    c_ = jx.pop(j)
                    rk_, sg_, kkr, k2 = c_['rk'], c_['sg'], c_['kkr'], c_['k2']
                    gam, ginv, gprev, ginvl, b_ss = c_['gam'], c_['ginv'], c_['gprev'], c_['ginvl'], c_['b_ss']
                    r_, ic = rk_[:, 0, :], sg_[:, 1, :]
                    rn, kkn, b__ = tmp['rn'].next(), tmp['kkn'].next(), tmp['b_'].next()
                    op('act', 'activation', [b_ss, epsb], [rn], out=rn[:], in_=b_ss[:, 0:T], func=AF.Ln, bias=epsb[:, 1:2])
                    op('act', 'activation', [rn], [rn], out=rn[:], in_=rn[:], func=AF.Exp, scale=-0.5)
                    op('dve', 'tensor_tensor', [kkr, rn], [kkn], out=kkn[:], in0=kkr[:], in1=rn[:], op=ALU.mult)
                    op('dve', 'tensor_tensor', [kkn, sg_], [b__], out=b__[:], in0=kkn[:], in1=ic, op=ALU.mult)
                    op('pool', 'tensor_tensor', [rk_, gam], [o_arj[j]], out=o_ar[:, j, :, C:2 * C], in0=v3(r_), in1=v3(gam[:]), op=ALU.mult)
                    for b in range(TB):
                        bs = slice(b * C, (b + 1) * C)
                        op('dve', 'scalar_tensor_tensor', [kkn, gprev, o_arj[j]], [o_arj[j]], out=o_ar[:, j, b, 0:C], in0=kkn[:, bs],
                           scalar=-1.0, in1=gprev[:, bs], op0=ALU.mult, op1=ALU.mult)
                    op('dve', 'tensor_tensor', [k2, ginv], [o_kj[j]], out=o_k[:, j, :], in0=k2[:], in1=ginv[:], op=ALU.mult)
                    op('pool', 'tensor_tensor', [b__, ginv], [o_bj[j]], out=o_b[:, j, :], in0=b__[:], in1=ginv[:], op=ALU.mult)
                    op('dve', 'tensor_tensor', [k2, ginvl], [o_kbj[j]], out=o_kb[:, j, :], in0=k2[:], in1=ginvl[:], op=ALU.mult)
                    op('pool', 'tensor_tensor', [b__, ginvl], [o_bbj[j]], out=o_bb[:, j, :], in0=b__[:], in1=ginvl[:], op=ALU.mult)
                    op('dve', 'scalar_tensor_tensor', [rk_, k2, pfm], [o_rkj[j]], out=o_rk[:, j, :], in0=r_, scalar=P('r_k', j), in1=k2[:],
                       op0=ALU.mult, op1=ALU.mult)
                wavefront(NJ, [sP, sA, sB, sC], order=list(P1_ORDER))
                ch0 = t0 // C
                dma('sp', s_ar.rearrange('j p c a t -> p j c (a t)')[:, :, ch0:ch0 + TB, :], o_ar[:], reads=o_arj)
                dma('sp', s_kt.rearrange('j p t -> p j t')[:, :, t0:t0 + T], o_k[:], reads=o_kj)
                dma('sp', s_bt.rearrange('j p t -> p j t')[:, :, t0:t0 + T], o_b[:], reads=o_bj)
                for (srcf, srcj, nm) in ((lambda j, b: o_ar[:, j, b, 0:C], o_arj, 'At'),
                                         (lambda j, b: o_kb[:, j, b * C:(b + 1) * C], o_kbj, 'Kbt'),
                                         (lambda j, b: o_bb[:, j, b * C:(b + 1) * C], o_bbj, 'Bbt')):
                    for b in range(TB):
                        bk = psum.next()
                        bkb = bk[:].bitcast(BF16)
                        for j in range(NJ):
                            op('pe', 'transpose', [srcj[j], cbf], [bk], sig=(j == NJ - 1), out=bkb[:, j * 128:(j + 1) * 128],
                               in_=srcf(j, b), identity=identb)
                        sg_ = stg.next()
                        if b % 2 == 0:
                            op('act', 'activation', [bk], [sg_], out=sg_[:], in_=bkb, func=AF.Copy)
                            dma(STQ, s_tm[nm][t0 + b * C:t0 + (b + 1) * C, :], sg_[:], reads=[sg_])
                        else:
                            op('dve', 'tensor_copy', [bk], [sg_], out=sg_[:], in_=bkb)
                            dma('sp', s_tm[nm][t0 + b * C:t0 + (b + 1) * C, :], sg_[:], reads=[sg_])
                for (pidx_, cb, nm) in ((2, 2 * D, 'Vt'), (3, 3 * D, 'SGt')):
                    xs_, xsj_ = make_xs(pidx_)
                    for b in range(TB):
                        sg_ = stg.next()
                        for half in range(2):
                            bk = psum.next()
                            for kc in range(NJ):
                                op('pe', 'matmul', [xsj_[kc], Win], [bk], sig=(kc == NJ - 1), out=bk[:, :],
                                   lhsT=xs_[:, kc, b * C:(b + 1) * C], rhs=Win[:, kc, cb + half * 512:cb + (half + 1) * 512],
                                   start=(kc == 0), stop=(kc == NJ - 1))
                            op('act', 'activation', [bk], [sg_], out=sg_[:, half * 512:(half + 1) * 512], in_=bk[:, :],
                               func=(AF.Copy if nm == 'Vt' else AF.Silu))
                        dma(STQ, s_tm[nm][t0 + b * C:t0 + (b + 1) * C, :], sg_[:], reads=[sg_])
                bk = psum.next()
                for b in range(TB):
                    for j in range(NJ):
                        op('pe', 'matmul', [o_rkj[j], selb], [bk], sig=(j == NJ - 1), out=bk[:, b * H:(b + 1) * H],
                           lhsT=o_rk[:, j, b * C:(b + 1) * C], rhs=selb[:, j, :], start=(j == 0), stop=(j == NJ - 1))
                op('dve', 'tensor_copy', [bk], [bon], out=bon[:], in_=bk[:, 0:TB * H].rearrange('p (b h) -> p b h', b=TB))
                dma('sp', s_bonus.rearrange('(b p) h -> p b h', p=128)[:, t0 // 128:t0 // 128 + TB, :], bon[:], reads=[bon])
                for j in range(NJ):
                    op('dve', 'tensor_copy', [hTj[j]], [hTj[j]], out=hT[:, j, 0:1], in_=hT[:, j, T:T + 1])
            kb.mute = False
            kb.barrier()

        if LAST_PHASE >= 2:
            kb.mute = 2 in SKIP_PHASES
            arena.reset()
            wkv_pre = arena.alloc([NJ, 2 * D], BF16, 'wkv_pre')
            slot_arena = Arena(wkv_pre[:].rearrange('p a b -> p (a b)'), NJ * 2 * D * 2)
            Wout = arena.alloc([NJ, D], BF16, 'Wout')
            load_w_bf16(Wout, a_w_out.rearrange('(kc p) n -> p kc n', p=128), D)
            ptml = arena.alloc([2, D], F32, 'ptml')
            dma('sp', ptml[:], ptm_d[:, 0:2, :], writes=[ptml])

            def mk_loads(i):
                d_ = {}
                d_['AR'] = arena.alloc([NJ, 2 * C], BF16, 'AR%d' % i)
                d_['KT'] = arena.alloc([NJ, C], BF16, 'KT%d' % i)
                d_['BT'] = arena.alloc([NJ, C], BF16, 'BT%d' % i)
                for n_ in ('At', 'Bbt', 'Kbt', 'Vt', 'SGt'):
                    d_[n_] = arena.alloc([D], BF16, n_ + str(i))
                d_['bon'] = arena.alloc([H], F32, 'bonl%d' % i)
                d_['xin'] = arena.alloc([D], F32, 'xin%d' % i)
                d_['ARz'] = arena.alloc([NJ, 2, 2 * C], BF16, 'ARz%d' % i)
                d_['BTz'] = arena.alloc([NJ, 2, C], BF16, 'BTz%d' % i)
                op('pool', 'memset', [], [d_['ARz']], ap=d_['ARz'][:], constant=0.0)
                op('pool', 'memset', [], [d_['BTz']], ap=d_['BTz'][:], constant=0.0)
                return d_
            lds = Ring([mk_loads(i) for i in range(2)])
            NSL = 4
            slots = []
            for i in range(NSL):
                sl = {}
                sl['S1m'] = slot_arena.alloc([2, 2 * C], BF16, 'S1m%d' % i)
                sl['S2m'] = slot_arena.alloc([2, 2 * C], BF16, 'S2m%d' % i)
                sl['Aoff'] = slot_arena.alloc([2, C], BF16, 'Aoff%d' % i)
                sl['NA'] = [slot_arena.alloc([2, 2 * C], BF16, 'NA%d_%d' % (i, k)) for k in range(2)]
                sl['N'] = [Buf(sl['NA'][k][:, :, 0:C], 'N%d_%d' % (i, k)) for k in range(2)]
                sl['A'] = [Buf(sl['NA'][k][:, :, C:2 * C], 'A%d_%d' % (i, k)) for k in range(2)]
                sl['X'] = [slot_arena.alloc([2, C], BF16, 'X%d_%d' % (i, k)) for k in range(2)]
                sl['Zp'] = slot_arena.alloc([2, C], BF16, 'Zp%d' % i)
                sl['Pp'] = slot_arena.alloc([2, C], BF16, 'Pp%d' % i)
                sl['tmpV'] = slot_arena.alloc([2, N], BF16, 'tmpV%d' % i)
                sl['WU'] = slot_arena.alloc([2, C], BF16, 'WU%d' % i)
                sl['RpT'] = slot_arena.alloc([2, C], BF16, 'RpT%d' % i)
                op('pool', 'memset', [], [sl['RpT']], ap=sl['RpT'][:], constant=0.0)
                slots.append(sl)
            y_sb = arena.alloc([D], F32, 'y_sb')
            yn = arena.alloc([D], F32, 'yn'); bv = arena.alloc([D], F32, 'bv'); ysq = bv
            stt = arena.alloc([4, H], F32, 'stt')
            yfin = arena.alloc([D], BF16, 'yfin')
            yT = arena.alloc([NJ, C], BF16, 'yT')
            t_o = arena.alloc([D], F32, 't_o')
            xr_o = Ring([arena.alloc([D], F32, 'xr_o%d' % i) for i in range(1)])

            bctx2 = ada_setup(nbuf=1)

            def issue_loads(c):
                L = lds.next()
                dma('sp', L['AR'][:], s_ar.rearrange('j p c a t -> p j c (a t)')[:, :, c, :], writes=[L['AR']])
                dma('sp', L['KT'][:], s_kt.rearrange('j p t -> p j t')[:, :, c * C:(c + 1) * C], writes=[L['KT']])
                dma('sp', L['BT'][:], s_bt.rearrange('j p t -> p j t')[:, :, c * C:(c + 1) * C], writes=[L['BT']])
                for n_ in ('At', 'Bbt', 'Kbt', 'Vt', 'SGt'):
                    dma('sp', L[n_][:], s_tm[n_][c * C:(c + 1) * C, :], writes=[L[n_]])
                dma('sp', L['bon'][:], s_bonus[c * C:(c + 1) * C, :], writes=[L['bon']])
                dma('sp', L['xin'][:], x_d[c * C:(c + 1) * C, :], writes=[L['xin']])
                return L
            def stage(k):
                kb.mute = (k > P2STOP) or (2 in SKIP_PHASES)
            L_next = issue_loads(0)
            for c in range(NCH):
                L = L_next
                if c + 1 < NCH:
                    L_next = issue_loads(c + 1)
                AR, KT, BT, At, Bbt, Kbt, Vt, SGt = (L[k_] for k_ in ('AR', 'KT', 'BT', 'At', 'Bbt', 'Kbt', 'Vt', 'SGt'))
                if c < 6:
                    ada_load(bctx2, 'b', c)
                ARz, BTz = L['ARz'], L['BTz']

                def zpad(Lx):
                    for h2 in range(2):
                        pb = 64 * h2
                        op('act', 'activation', [Lx['AR']], [Lx['ARz']], out=Lx['ARz'][pb:pb + 64, :, h2, :], in_=Lx['AR'][pb:pb + 64, :, :], func=AF.Copy)
                        op('act', 'activation', [Lx['BT']], [Lx['BTz']], out=Lx['BTz'][pb:pb + 64, :, h2, :], in_=Lx['BT'][pb:pb + 64, :, :], func=AF.Copy)
                if c == 0:
                    zpad(L)
                ybanks = []
                for half in range(2):
                    pairs = list(range(4 * half, 4 * half + 4))
                    stage(0)
                    for j in pairs:
                        sl = slots[j % NSL]
                        b1, b2, b3 = psum.next(), psum.next(), psum.next()
                        for h2 in range(2):
                            op('pe', 'matmul', [BT, ARz], [b1], out=b1[:, h2 * 256:(h2 + 1) * 256], lhsT=BT[:, j, :],
                               rhs=ARz[:, j, h2, :], start=True, stop=True)
                            op('pe', 'matmul', [KT, ARz], [b2], out=b2[:, h2 * 256:(h2 + 1) * 256], lhsT=KT[:, j, :],
                               rhs=ARz[:, j, h2, :], start=True, stop=True)
                            op('pe', 'matmul', [AR, BTz], [b3], out=b3[:, h2 * 128:(h2 + 1) * 128], lhsT=AR[:, j, 0:C],
                               rhs=BTz[:, j, h2, :], start=True, stop=True)
                        v2 = lambda ap: ap.rearrange('p (h c) -> p h c', h=2)
                        op('dve', 'tensor_tensor', [b1, cst], [sl['S1m']], out=sl['S1m'][:], in0=v2(b1[:, :]), in1=m_sbd_ui, op=ALU.mult)
                        op('dve', 'tensor_tensor', [b3, cst], [sl['N'][0]], out=sl['N'][0][:], in0=v2(b3[:, 0:256]), in1=m_slbd2, op=ALU.mult)
                        op('dve', 'tensor_tensor', [b1, cst], [sl['Aoff']], out=sl['Aoff'][:], in0=v2(b1[:, :])[:, :, 0:C], in1=m_off2, op=ALU.mult)
                        op('dve', 'tensor_tensor', [b2, cst], [sl['S2m']], out=sl['S2m'][:], in0=v2(b2[:, :]), in1=m_su_ui, op=ALU.mult)
                        op('pool', 'tensor_tensor', [sl['S1m'], cbf], [sl['X'][1]], out=sl['X'][1][:], in0=sl['S1m'][:, :, 0:C], in1=ident2b, op=ALU.add)
                    stage(1)
                    for s in range(1, 7):
                        if s == TMP_AFTER + 1:
                            for j in pairs:
                                sl = slots[j % NSL]
                                b8 = psum.next()
                                for h2 in range(2):
                                    h = 2 * j + h2
                                    op('pe', 'matmul', [sl['S2m'], Vt], [b8], out=b8[:, h2 * N:(h2 + 1) * N], lhsT=sl['S2m'][:, h2, 0:C],
                                       rhs=Vt[:, h * N:(h + 1) * N], start=True, stop=True)
                                op('act', 'activation', [b8], [sl['tmpV']], out=sl['tmpV'][:], in_=b8[:, 0:2 * N].rearrange('p (h c) -> p h c', h=2), func=AF.Copy)
                        for j in pairs:
                            sl = slots[j % NSL]
                            Np, Nn = sl['N'][(s - 1) % 2], sl['N'][s % 2]
                            Ap_buf = sl['S1m'] if s == 1 else sl['A'][(s - 1) % 2]
                            Ap = (lambda h2, b_=Ap_buf: b_[:, h2, 0:C])
                            An = sl['A'][s % 2]
                            Xp, Xn = sl['X'][(s - 1) % 2], sl['X'][s % 2]
                            v2 = lambda ap: ap.rearrange('p (h c) -> p h c', h=2)
                            if P2_MERGE_NA:
                                bNA = psum.next() if s <= 5 else None
                                bD = psum.next() if s >= 2 else None
                                for h2 in range(2):
                                    if s <= 5:
                                        op('pe', 'matmul', [Ap_buf, Np], [bNA], out=bNA[:, h2 * 256:h2 * 256 + C], lhsT=Ap(h2), rhs=Np[:, h2, :],
                                           start=True, stop=True)
                                    if s <= 4:
                                        op('pe', 'matmul', [Ap_buf, Np], [bNA], out=bNA[:, h2 * 256 + C:(h2 + 1) * 256], lhsT=Np[:, h2, :], rhs=Ap(h2),
                                           start=True, stop=True)
                                    if s >= 2:
                                        op('pe', 'matmul', [Np, Xp], [bD], out=bD[:, h2 * C:(h2 + 1) * C], lhsT=Np[:, h2, :], rhs=Xp[:, h2, :],
                                           start=True, stop=True)
                                if s <= 4:
                                    op('act', 'activation', [bNA], [Nn, An], out=sl['NA'][s % 2][:], in_=v2(bNA[:, :]), func=AF.Copy)
                                elif s == 5:
                                    op('act', 'activation', [bNA], [Nn], out=Nn[:], in_=v2(bNA[:, :])[:, :, 0:C], func=AF.Copy)
                                if s >= 2:
                                    op('dve', 'tensor_tensor', [bD, Xp], [Xn], out=Xn[:], in0=v2(bD[:, 0:256]), in1=Xp[:], op=ALU.add)
                                continue
                            bN, bAD = psum.next(), psum.next()
                            for h2 in range(2):
                                if s <= 5:
                                    op('pe', 'matmul', [Ap_buf, Np], [bN], out=bN[:, h2 * C:(h2 + 1) * C], lhsT=Ap(h2), rhs=Np[:, h2, :],
                                       start=True, stop=True)
                                if s <= 4:
                                    op('pe', 'matmul', [Ap_buf, Np], [bAD], out=bAD[:, h2 * 256:h2 * 256 + C], lhsT=Np[:, h2, :], rhs=Ap(h2),
                                       start=True, stop=True)
                                if s >= 2:
                                    if P2_XADD_PE:
                                        op('pe', 'matmul', [cbf, Xp], [bAD], sig=False, out=bAD[:, h2 * 256 + C:(h2 + 1) * 256], lhsT=identb, rhs=Xp[:, h2, :],
                                           start=True, stop=False)
                                    op('pe', 'matmul', [Np, Xp], [bAD], out=bAD[:, h2 * 256 + C:(h2 + 1) * 256], lhsT=Np[:, h2, :], rhs=Xp[:, h2, :],
                                       start=(not P2_XADD_PE), stop=True)
                            v2 = lambda ap: ap.rearrange('p (h c) -> p h c', h=2)
                            if s <= 5:
                                op('act', 'activation', [bN], [Nn], out=Nn[:], in_=v2(bN[:, 0:256]), func=AF.Copy)
                            if s <= 4:
                                if P2_ACOPY_ACT == 0 or j % P2_ACOPY_ACT == 0:
                                    op('act', 'activation', [bAD], [An], out=An[:], in_=v2(bAD[:, :])[:, :, 0:C], func=AF.Copy)
                                else:
                                    op('dve', 'tensor_copy', [bAD], [An], out=An[:], in_=v2(bAD[:, :])[:, :, 0:C])
                            if s >= 2:
                                if not P2_XADD_PE:
                                    op('dve', 'tensor_tensor', [bAD, Xp], [Xn], out=Xn[:], in0=v2(bAD[:, :])[:, :, C:2 * C], in1=Xp[:], op=ALU.add)
                                elif P2_XADD_PE == 1 and j % 2 == 0:
                                    op('act', 'activation', [bAD], [Xn], out=Xn[:], in_=v2(bAD[:, :])[:, :, C:2 * C], func=AF.Copy)
                                else:
                                    op('dve', 'tensor_copy', [bAD], [Xn], out=Xn[:], in_=v2(bAD[:, :])[:, :, C:2 * C])
                    stage(2)
                    for j in pairs:
                        sl = slots[j % NSL]
                        Xb = sl['X'][0]
                        b9 = psum.next()
                        for h2 in range(2):
                            h = 2 * j + h2
                            op('pe', 'matmul', [Xb, At], [b9], out=b9[:, h2 * C:h2 * C + N], lhsT=Xb[:, h2, :],
                               rhs=At[:, h * N:(h + 1) * N], start=True, stop=True)
                            op('pe', 'matmul', [Xb, sl['tmpV']], [b9], out=b9[:, h2 * C + N:(h2 + 1) * C], lhsT=Xb[:, h2, :],
                               rhs=sl['tmpV'][:, h2, :], start=True, stop=True)
                        op('act', 'activation', [b9], [sl['Zp']], out=sl['Zp'][:], in_=b9[:, 0:2 * C].rearrange('p (h c) -> p h c', h=2), func=AF.Copy)
                    for j in pairs:
                        sl = slots[j % NSL]
                        b9 = psum.next()
                        for h2 in range(2):
                            op('pe', 'matmul', [sl['Aoff'], sl['Zp']], [b9], out=b9[:, h2 * C:(h2 + 1) * C], lhsT=sl['Aoff'][:, h2, :],
                               rhs=sl['Zp'][:, h2, :], start=True, stop=True)
                        op('act', 'activation', [b9], [sl['Pp']], out=sl['Pp'][:], in_=b9[:, 0:2 * C].rearrange('p (h c) -> p h c', h=2), func=AF.Copy)
                    for j in pairs:
                        sl = slots[j % NSL]
                        Xb = sl['X'][0]
                        b9 = psum.next()
                        for h2 in range(2):
                            op('pe', 'matmul', [Xb, sl['Pp']], [b9], out=b9[:, h2 * C:(h2 + 1) * C], lhsT=Xb[:, h2, :],
                               rhs=sl['Pp'][:, h2, :], start=True, stop=True)
                        op('dve', 'tensor_tensor', [b9, sl['Zp']], [sl['WU']], out=sl['WU'][:], in0=b9[:, 0:2 * C].rearrange('p (h c) -> p h c', h=2),
                           in1=sl['Zp'][:], op=ALU.add)
                    stage(3)
                    for j in pairs:
                        sl = slots[j % NSL]
                        bR, bG = psum.next(), psum.next()
                        for h2 in range(2):
                            h = 2 * j + h2
                            pb = 64 * h2
                            op('pe', 'matmul', [sl['WU'], sl['S1m']], [bR], out=bR[pb:pb + 64, 0:C], lhsT=sl['WU'][:, h2, 0:N],
                               rhs=sl['S1m'][:, h2, C:2 * C], start=True, stop=True)
                            op('pe', 'matmul', [sl['WU'], Bbt], [bG], out=bG[pb:pb + 64, 0:N], lhsT=sl['WU'][:, h2, 0:N],
                               rhs=Bbt[:, h * N:(h + 1) * N], start=True, stop=True)
                        for h2 in range(2):
                            pb = 64 * h2
                            op('dve', 'tensor_tensor', [bR, AR], [sl['RpT']], out=sl['RpT'][pb:pb + 64, h2, :], in0=bR[pb:pb + 64, 0:C],
                               in1=AR[pb:pb + 64, j, C:2 * C], op=ALU.add)
                        for h2 in range(2):
                            pb = 64 * h2
                            op('act', 'activation', [bG], [GT[j]], out=GT[j][pb:pb + 64, pb:pb + 64], in_=bG[pb:pb + 64, 0:N], func=AF.Copy)
                    stage(4)
                    bY = psum.next()
                    ybanks.append(bY)
                    for j in pairs:
                        sl = slots[j % NSL]
                        for h2 in range(2):
                            h = 2 * j + h2
                            pb = 64 * h2
                            hc = (h - 8 * half) * N
                            op('pe', 'matmul', [sl['RpT'], STb[j]], [bY], sig=False, out=bY[:, hc:hc + N], lhsT=sl['RpT'][:, h2, :],
                               rhs=STb_all[:, j, :], start=True, stop=False)
                            op('pe', 'matmul', [sl['S1m'], sl['WU']], [bY], sig=False, out=bY[:, hc:hc + N], lhsT=sl['S1m'][:, h2, C:2 * C],
                               rhs=sl['WU'][:, h2, N:2 * N], start=False, stop=False)
                            op('pe', 'matmul', [sl['S2m'], Vt], [bY], out=bY[:, hc:hc + N], lhsT=sl['S2m'][:, h2, C:2 * C],
                               rhs=Vt[:, h * N:(h + 1) * N], start=False, stop=True)
                    for j in pairs:
                        sl = slots[j % NSL]
                        bH = psum.next()
                        for h2 in range(2):
                            h = 2 * j + h2
                            pb = 64 * h2
                            op('pe', 'matmul', [Bbt, sl['WU']], [bH], sig=False, out=bH[pb:pb + 64, 0:N], lhsT=Bbt[:, h * N:(h + 1) * N],
                               rhs=sl['WU'][:, h2, N:2 * N], start=True, stop=False)
                            op('pe', 'matmul', [Kbt, Vt], [bH], sig=False, out=bH[pb:pb + 64, 0:N], lhsT=Kbt[:, h * N:(h + 1) * N],
                               rhs=Vt[:, h * N:(h + 1) * N], start=False, stop=False)
                        op('pe', 'matmul', [GT[j], ST[j]], [bH], out=bH[:, 0:N], lhsT=GT[j][:], rhs=ST_all[:, j, :], start=False, stop=True)
                        op('dve', 'scalar_tensor_tensor', [ST[j], glast, bH], [ST[j]], out=ST_all[:, j, :], in0=ST_all[:, j, :],
                           scalar=glast[:, j, c:c + 1], in1=bH[:, 0:N], op0=ALU.mult, op1=ALU.add)
                        op('act', 'activation', [ST[j]], [STb[j]], out=STb_all[:, j, :], in_=ST_all[:, j, :], func=AF.Copy)
                    op('act', 'activation', [bY], [y_sb], out=y_sb[:, half * 512:(half + 1) * 512], in_=bY[:, :], func=AF.Copy)
                    if half == 0 and c + 1 < NCH:
                        zpad(L_next)
                if c == NCH - 1:
                    slot_bufs = [wkv_pre]
                    for sl in slots:
                        for v_ in sl.values():
                            slot_bufs += (v_ if isinstance(v_, list) else [v_])
                    wv_ = w_kv.rearrange('(kc p) n -> p kc n', p=128)
                    for c0_ in range(0, 2 * D, 512):
                        dma('pool', wkv_pre[:, :, c0_:c0_ + 512], wv_[:, :, c0_:c0_ + 512], writes=slot_bufs)
                stage(5)
                y3 = lambda ap: ap.rearrange('p (h n) -> p h n', h=H)
                bc = lambda ap: ap.unsqueeze(2).broadcast_to([128, H, N])
                op('dve', 'tensor_reduce', [y_sb], [stt], out=stt[:, 0, :], in_=y3(y_sb[:]), axis=mybir.AxisListType.X, op=ALU.add)
                op('act', 'activation', [y_sb], [ysq], out=ysq[:], in_=y_sb[:], func=AF.Square)
                op('dve', 'tensor_reduce', [ysq, stt], [stt], out=stt[:, 1, :], in_=y3(ysq[:]), axis=mybir.AxisListType.X, op=ALU.add)
                op('dve', 'tensor_scalar', [stt], [stt], out=stt[:, 0, :], in0=stt[:, 0, :], scalar1=1.0 / N, scalar2=None, op0=ALU.mult)
                op('dve', 'tensor_tensor', [stt], [stt], out=stt[:, 2, :], in0=stt[:, 0, :], in1=stt[:, 0, :], op=ALU.mult)
                op('dve', 'scalar_tensor_tensor', [stt], [stt], out=stt[:, 1, :], in0=stt[:, 1, :], scalar=1.0 / N, in1=stt[:, 2, :],
                   op0=ALU.mult, op1=ALU.subtract)
                op('act', 'activation', [stt, epsb], [stt], out=stt[:, 1, :], in_=stt[:, 1, :], func=AF.Ln, bias=epsb[:, 2:3])
                op('act', 'activation', [stt], [stt], out=stt[:, 1, :], in_=stt[:, 1, :], func=AF.Exp, scale=-0.5)
                op('dve', 'tensor_tensor', [y_sb, stt], [yn], out=y3(yn[:]), in0=y3(y_sb[:]), in1=bc(stt[:, 0, :]), op=ALU.subtract)
                op('dve', 'tensor_tensor', [yn, stt], [yn], out=y3(yn[:]), in0=y3(yn[:]), in1=bc(stt[:, 1, :]), op=ALU.mult)
                op(('pool' if P2_POST_POOL else 'dve'), 'tensor_tensor', [yn, ptml], [yn], out=yn[:], in0=yn[:], in1=ptml[:, 0, :], op=ALU.mult)
                op(('pool' if P2_POST_POOL else 'dve'), 'tensor_tensor', [yn, ptml], [yn], out=yn[:], in0=yn[:], in1=ptml[:, 1, :], op=ALU.add)
                op(('pool' if P2_POST_POOL else 'dve'), 'tensor_tensor', [Vt, L['bon']], [bv], out=y3(bv[:]), in0=y3(Vt[:]), in1=bc(L['bon'][:]), op=ALU.mult)
                op(('pool' if P2_POST_POOL else 'dve'), 'tensor_tensor', [yn, bv], [yn], out=yn[:], in0=yn[:], in1=bv[:], op=ALU.add)
                op(('pool' if P2_POST_POOL else 'dve'), 'tensor_tensor', [yn, SGt], [yfin], out=yfin[:], in0=yn[:], in1=SGt[:], op=ALU.mult)
                stage(6)
                bk = psum.next()
                bkb = bk[:].bitcast(BF16)
                for j in range(NJ):
                    op('pe', 'transpose', [yfin, cbf], [bk], sig=(j == NJ - 1), out=bkb[:, j * 128:(j + 1) * 128],
                       in_=yfin[:, j * 128:(j + 1) * 128], identity=identb)
                op('act', 'activation', [bk], [yT], out=yT[:], in_=bkb.rearrange('p (j t) -> p j t', j=NJ), func=AF.Copy)
                xr = xr_o.next()
                for half in range(2):
                    bk = psum.next()
                    for kc in range(NJ):
                        op('pe', 'matmul', [yT, Wout], [bk], sig=(kc == NJ - 1), out=bk[:, :], lhsT=yT[:, kc, :],
                           rhs=Wout[:, kc, half * 512:(half + 1) * 512], start=(kc == 0), stop=(kc == NJ - 1))
                    hs = slice(half * 512, (half + 1) * 512)
                    op('dve', 'tensor_tensor', [bk, gateA], [t_o], out=t_o[:, hs], in0=bk[:, :], in1=gateA[:, hs], op=ALU.mult)
                    op('dve', 'tensor_tensor', [t_o, L['xin']], [xr], out=xr[:, hs], in0=t_o[:, hs], in1=L['xin'][:, hs], op=ALU.add)
                dma('sp', s_xr1[c * C:(c + 1) * C, :], xr[:], reads=[xr])
                if c < 6:
                    ada_ct(bctx2, 'b', c, load=False)
            kb.mute = False
            kb.barrier()

        s_KT = scr('s_KT', [NJ, 128, S], BF16)
        s_V = scr('s_V', [S, D], BF16)
        s_QT = scr('s_QT', [3 * NJ, 128, S], BF16)
        s_SG = scr('s_SG', [NJ, 128, S], BF16)
        if LAST_PHASE >= 3:
            kb.mute = 3 in SKIP_PHASES
            TB = 4
            T = TB * 128
            for sub in ('kv', 'q'):
                if sub == 'kv':
                    arena.reset()
                    arena.alloc([NJ, 2 * D], BF16, 'Wb')
                    if LAST_PHASE >= 2 and 2 not in SKIP_PHASES:
                        Wb = wkv_pre
                    else:
                        Wb = Buf(wkv_pre.ap if LAST_PHASE >= 2 else arena.base[:, 0:NJ * 2 * D].rearrange('p (a b) -> p a b', a=NJ), 'Wb')
                        load_w_bf16(Wb, w_kv.rearrange('(kc p) n -> p kc n', p=128), 2 * D)
                    Wq = arena.alloc_top([NJ, 4 * D], BF16, 'Wq')
                    load_w_bf16(Wq, b_w_in.rearrange('(kc p) n -> p kc n', p=128), 4 * D)
                else:
                    arena.reset(keep_top=True)
                    Wb = Wq
                qkg = arena.alloc([4], F32, 'qkg')
                TC = arena.alloc([S], F32, 'TC'); TS = arena.alloc([S], F32, 'TS')
                xts = [arena.alloc([TB, D], F32, 'xt3_%d' % i) for i in range(2)]
                rope = xts[1]
                rope_v = xts[1][:].rearrange('p a b -> p (a b)').rearrange('p (r s) -> p r s', r=2)
                bctx = None
                sq = arena.alloc([D], BF16, 'sq3'); ss = arena.alloc([4], F32, 'ss3')
                hT = arena.alloc([NJ, T], BF16, 'hT3'); hTj = split(hT, NJ)
                raw = Ring([arena.alloc([T], BF16, 'raw%d' % i) for i in range(4)])
                sqr = Ring([arena.alloc([T], BF16, 'sqr%d' % i) for i in range(4)])
                rs = Ring([arena.alloc([T], F32, 'rs%d' % i) for i in range(2)])
                t1 = Ring([arena.alloc([T], F32, 't1_%d' % i) for i in range(2)])
                t2 = Ring([arena.alloc([T], F32, 't2_%d' % i) for i in range(2)])
                ofm = Ring([arena.alloc([T], BF16, 'ofm%d' % i) for i in range(2)])
                otm = Ring([arena.alloc([D], BF16, 'otm%d' % i) for i in range(1)])
                load_rows(s_xr1, 0, TB, xts[0])
                dma('sp', rope_v, rope_d, writes=[rope])
                dma('sp', qkg[:], qkg_d, writes=[qkg])
                gi = 2 if sub == 'kv' else 0
                op('dve', 'tensor_scalar', [rope, qkg], [TC], out=TC[:], in0=rope_v[:, 0, :], scalar1=qkg[:, gi:gi + 1],
                   scalar2=(8.0 if sub == 'kv' else 1.0), op0=ALU.mult, op1=ALU.mult)
                op('dve', 'tensor_scalar', [rope, qkg], [TS], out=TS[:], in0=rope_v[:, 1, :], scalar1=qkg[:, gi + 1:gi + 2],
                   scalar2=(8.0 if sub == 'kv' else 1.0), op0=ALU.mult, op1=ALU.mult)
                if sub == 'kv':
                    gs_ap = lambda j: P('kv_norm_g', j)
                    sh_ap = lambda j: 0.0
                    fm_chunks = [(j, j * 128, s_KT, j) for j in range(NJ)]
                else:
                    gs_ap = lambda j: gsh[:, 2, j:j + 1]
                    sh_ap = lambda j: gsh[:, 3, j:j + 1]
                    fm_chunks = [(jq, jq * 128, s_QT, jq) for jq in range(3 * NJ)]
                for tt in range(S // T):
                    t0 = tt * T
                    xt = xts[tt % 2]
                    norm_transpose(TB, xt, sq, ss, xt, hT, hTj, gs_ap, sh_ap, 0)
                    if tt + 1 < S // T:
                        load_rows(s_xr1, t0 + T, TB, xts[(tt + 1) % 2])
                    if bctx is not None and tt < 3:
                        ada_load(bctx, 'b', 2 * tt)
                    cx = {}

                    def st0(i):
                        (ci, c0_, dst, di) = fm_chunks[i]
                        bk = psum.next()
                        for kc in range(NJ):
                            op('pe', 'matmul', [Wb, hTj[kc]], [bk], sig=(kc == NJ - 1), out=bk[:, :], lhsT=Wb[:, kc, c0_:c0_ + 128],
                               rhs=hT[:, kc, :], start=(kc == 0), stop=(kc == NJ - 1))
                        cx[i] = dict(bk=bk)

                    def st1(i):
                        c_ = cx[i]
                        raw_, sq_ = raw.next(), sqr.next()
                        op('act', 'activation', [c_['bk']], [raw_], out=raw_[:], in_=c_['bk'][:, :], func=AF.Copy)
                        op('act', 'activation', [c_['bk']], [sq_], out=sq_[:], in_=c_['bk'][:, :], func=AF.Square)
                        c_.update(raw=raw_, sq=sq_)

                    def st2(i):
                        c_ = cx[i]
                        b_ss, b_rot = psum.next(), psum.next()
                        t1_ = t1.next()
                        op('pe', 'matmul', [cbf, c_['sq']], [b_ss], out=b_ss[:, :], lhsT=blockones, rhs=c_['sq'][:], start=True, stop=True)
                        op('pe', 'matmul', [cbf, c_['raw']], [b_rot], out=b_rot[:, :], lhsT=rotp, rhs=c_['raw'][:], start=True, stop=True)
                        op('pool', 'tensor_tensor', [c_['raw'], TC], [t1_], out=t1_[:], in0=c_['raw'][:], in1=TC[:, t0:t0 + T], op=ALU.mult)
                        c_.update(t1=t1_, b_ss=b_ss, b_rot=b_rot)

                    def st3(i):
                        c_ = cx[i]
                        rs_ = rs.next()
                        op('act', 'activation', [c_['b_ss'], epsb], [rs_], out=rs_[:], in_=c_['b_ss'][:, :], func=AF.Ln, bias=epsb[:, 3:4])
                        op('act', 'activation', [rs_], [rs_], out=rs_[:], in_=rs_[:], func=AF.Exp, scale=-0.5)
                        t2_ = t2.next()
                        op('dve', 'tensor_tensor', [c_['b_rot'], TS], [t2_], out=t2_[:], in0=c_['b_rot'][:, :], in1=TS[:, t0:t0 + T], op=ALU.mult)
                        op('dve', 'tensor_tensor', [c_['t1'], t2_], [t2_], out=t2_[:], in0=c_['t1'][:], in1=t2_[:], op=ALU.add)
                        c_.update(rs=rs_, t2=t2_)

                    def st4(i):
                        (ci, c0_, dst, di) = fm_chunks[i]
                        c_ = cx.pop(i)
                        o_ = ofm.next()
                        op('dve', 'tensor_tensor', [c_['t2'], c_['rs']], [o_], out=o_[:], in0=c_['t2'][:], in1=c_['rs'][:], op=ALU.mult)
                        dma('sp', dst[di, :, t0:t0 + T], o_[:], reads=[o_])
                    wavefront(len(fm_chunks), [st0, st1, st2, st3, st4], order=(list(P3_ORDER) if P3_ORDER else None))
                    if bctx is not None and tt < 3:
                        ada_ct(bctx, 'b', 2 * tt, load=False)
                        ada_load(bctx, 'b', 2 * tt + 1)
                    if sub == 'kv':
                        for b in range(TB):
                            o_ = otm.next()
                            for half in range(2):
                                bk = psum.next()
                                for kc in range(NJ):
                                    op('pe', 'matmul', [hTj[kc], Wb], [bk], sig=(kc == NJ - 1), out=bk[:, :], lhsT=hT[:, kc, b * 128:(b + 1) * 128],
                                       rhs=Wb[:, kc, D + half * 512:D + (half + 1) * 512], start=(kc == 0), stop=(kc == NJ - 1))
                                op('act', 'activation', [bk], [o_], out=o_[:, half * 512:(half + 1) * 512], in_=bk[:, :], func=AF.Copy)
                            dma('sp', s_V[t0 + b * 128:t0 + (b + 1) * 128, :], o_[:], reads=[o_])
                        if bctx is not None and tt < 3:
                            ada_ct(bctx, 'b', 2 * tt + 1, load=False)
                    else:
                        for j in range(NJ):
                            bk = psum.next()
                            for kc in range(NJ):
                                op('pe', 'matmul', [Wb, hTj[kc]], [bk], sig=(kc == NJ - 1), out=bk[:, :],
                                   lhsT=Wb[:, kc, 3 * D + j * 128:3 * D + (j + 1) * 128], rhs=hT[:, kc, :], start=(kc == 0), stop=(kc == NJ - 1))
                            o_ = ofm.next()
                            op('act', 'activation', [bk], [o_], out=o_[:], in_=bk[:, :], func=AF.Silu)
                            dma('sp', s_SG[j, :, t0:t0 + T], o_[:], reads=[o_])
                kb.mute2 = kb.mute
                kb.mute = False
                kb.barrier()
                kb.mute = kb.mute2

        if LAST_PHASE >= 4:
            kb.mute = False
            arena.reset()
            yT = arena.alloc([NJ, S], BF16, 'yTall'); yTj = split(yT, NJ)

            def mk_pl(i):
                d_ = dict(KT=arena.alloc([S], BF16, 'KTp%d' % i), SG=arena.alloc([S], BF16, 'SGp%d' % i),
                          QTz=arena.alloc([2, 3, S], BF16, 'QTz%d' % i))
                op('pool', 'memset', [], [d_['QTz']], ap=d_['QTz'][:], constant=0.0)
                d_['VL'] = [[arena.alloc([16, 128], BF16, 'VL%d_%d_%d' % (i, g, h2)) for h2 in range(2)] for g in range(3)]
                for g in range(3):
                    for h2 in range(2):
                        op('pool', 'memset', [], [d_['VL'][g][h2]], ap=d_['VL'][g][h2][:], constant=1.0)
                return d_
            pls = [mk_pl(i) for i in range(2)]
            GR = ((1, 16), (4, 4), (16, 1))

            def pair_loads(j):
                L = pls[j % 2]
                dma('sp', L['KT'][:], s_KT[j], writes=[L['KT']])
                qv = s_QT.rearrange('(g j) p t -> j p g t', g=3)[j]
                for h2 in range(2):
                    pb = 64 * h2
                    dma('sp', L['QTz'][pb:pb + 64, h2, :, :], qv[pb:pb + 64, :, :], writes=[L['QTz']])
                dma('sp', L['SG'][:], s_SG[j], writes=[L['SG']])
                for g, (dil, nblk) in enumerate(GR):
                    for h2 in range(2):
                        dma('sp', L['VL'][g][h2][:, :, 64 * h2:64 * h2 + 64].rearrange('p (r nb) d -> p r nb d', r=dil),
                            s_V.rearrange('(nb p r) d -> p r nb d', p=128, r=dil)[:, :, :, j * 128 + 64 * h2:j * 128 + 64 * h2 + 64],
                            writes=[L['VL'][g][h2]])
                return L
            accO = arena.alloc([S], F32, 'accA'); accL = arena.alloc([S], F32, 'accB')
            rec = arena.alloc([S], F32, 'rec')
            Pm = Ring([arena.alloc([2, 256], BF16, 'Pm%d' % i) for i in range(P4_NPM)])
            mbias = cbf[:, 2944:3200]
            sw1 = cst[:, 128:256]; sw2 = cst[:, 256:384]
            bOr = [banks[0], banks[1]]
            bLr = [banks[2], banks[3]]
            sring = Ring(banks[4:8])
            L_next = pair_loads(0)
            for j in range(NJ):
                L_ = L_next
                if j + 1 < NJ:
                    L_next = pair_loads(j + 1)
                KT, QTz, SG, VL = L_['KT'], L_['QTz'], L_['SG'], L_['VL']
                items = []
                for g, (dil, nblk) in enumerate(GR):
                    for r in range(dil):
                        for kbi in range(nblk):
                            nq = 2 if kbi + 1 < nblk else 1
                            ncol = 128 * nq
                            st_ = dil * 128 * kbi + r
                            items.append(dict(g=g, dil=dil, nblk=nblk, r=r, kbi=kbi, nq=nq, ncol=ncol,
                                              kcols=slice(st_, st_ + dil * 127 + 1, dil),
                                              qcols=slice(st_, st_ + dil * (ncol - 1) + 1, dil)))
                v2 = lambda ap: ap.rearrange('p (h c) -> p h c', h=2)

                def emit_scores(it):
                    bS = sring.next()
                    ncol = it['ncol']
                    for h2 in range(2):
                        op('pe', 'matmul', [KT, QTz], [bS], sig=False, out=bS[:, h2 * 256:h2 * 256 + ncol], lhsT=KT[:, it['kcols']],
                           rhs=QTz[:, h2, it['g'], it['qcols']], start=True, stop=False)
                        op('pe', 'matmul', [cbf], [bS], out=bS[:, h2 * 256:h2 * 256 + ncol], lhsT=identb,
                           rhs=mbias[:, 0:ncol], start=False, stop=True)
                    pm = Pm.next()
                    op('act', 'activation', [bS], [pm], out=pm[:, :, 0:ncol], in_=v2(bS[:, :])[:, :, 0:ncol], func=AF.Exp)
                    it['pm'] = pm

                def emit_pv(it):
                    g, dil, r, kbi, pm = it['g'], it['dil'], it['r'], it['kbi'], it['pm']
                    vblk = r * it['nblk'] + kbi
                    for qt in range(it['nq']):
                        nb = kbi + qt
                        reg = nb % 2
                        first = (qt == 1) or (nb == 0)
                        last = (qt == 0)
                        for h2, bq in ((0, bOr[reg]), (1, bLr[reg])):
                            op('pe', 'matmul', [VL[g][h2], pm], [bq], out=bq[:, 0:128],
                               lhsT=VL[g][h2][:, vblk, :], rhs=pm[:, h2, qt * 128:(qt + 1) * 128], start=first, stop=last)
                        if last:
                            q0 = dil * 128 * nb + r
                            tcols = slice(q0, q0 + dil * 127 + 1, dil)
                            if g == 0:
                                op('dve', 'tensor_copy', [bOr[reg]], [accO], out=accO[:, tcols], in_=bOr[reg][:, 0:128])
                                op('act', 'activation', [bLr[reg]], [accL], out=accL[:, tcols], in_=bLr[reg][:, 0:128], func=AF.Copy)
                            else:
                                op('dve', 'tensor_tensor', [bOr[reg], accO], [accO], out=accO[:, tcols], in0=bOr[reg][:, 0:128],
                                   in1=accO[:, tcols], op=ALU.add)
                                op('dve', 'tensor_tensor', [bLr[reg], accL], [accL], out=accL[:, tcols], in0=bLr[reg][:, 0:128],
                                   in1=accL[:, tcols], op=ALU.add)
                LOOK = P4_LOOK
                for i in range(len(items) + LOOK):
                    if i < len(items):
                        emit_scores(items[i])
                    if i - LOOK >= 0:
                        emit_pv(items[i - LOOK])
                for q4 in range(S // 512):
                    cs_ = slice(q4 * 512, (q4 + 1) * 512)
                    bk = sring.next()
                    op('pe', 'matmul', [cst, accO], [bk], sig=False, out=bk[:, :], lhsT=sw1, rhs=accO[:, cs_], start=True, stop=False)
                    op('pe', 'matmul', [cst, accL], [bk], out=bk[:, :], lhsT=sw2, rhs=accL[:, cs_], start=False, stop=True)
                    op('act', 'activation', [bk], [rec], out=rec[:, cs_], in_=bk[:, :], func=AF.Ln)
                op('act', 'activation', [rec], [rec], out=rec[:], in_=rec[:], func=AF.Exp, scale=-1.0)
                for (pb, src) in ((0, accO), (64, accL)):
                    op('dve', 'tensor_tensor', [src, rec], [src], out=src[pb:pb + 64, :], in0=src[pb:pb + 64, :], in1=rec[pb:pb + 64, :], op=ALU.mult)
                    op('dve', 'tensor_tensor', [src, SG], [yTj[j]], out=yT[pb:pb + 64, j, :], in0=src[pb:pb + 64, :], in1=SG[pb:pb + 64, :], op=ALU.mult)
                if j == NJ - 2:
                    Wout = Buf(pls[0]['QTz'][:].rearrange('p a b c -> p (a b c)')[:, 0:NJ * D].rearrange('p (k n) -> p k n', k=NJ), 'WoutB')
                    wv_ = b_w_out.rearrange('(kc p) n -> p kc n', p=128)
                    for c0_ in range(0, D, 512):
                        dma('pool', Wout[:, :, c0_:c0_ + 512], wv_[:, :, c0_:c0_ + 512], writes=[Wout, pls[0]['QTz']])
            kb.barrier()
            xr_in = Ring([Buf(accO[:, 0:D], 'xr_in0'), Buf(accO[:, D:2 * D], 'xr_in1')])
            t_o = Buf(accL[:, 0:D], 't_o5')
            o5 = Ring([Buf(rec[:, 0:D], 'o5_0'), Buf(rec[:, D:2 * D], 'o5_1')])
            allbanks = Ring(banks)
            xi_next = xr_in.next()
            dma('sp', xi_next[:], s_xr1[0:128, :], writes=[xi_next])
            for b in range(S // 128):
                xi = xi_next
                if b + 1 < S // 128:
                    xi_next = xr_in.next()
                    dma('sp', xi_next[:], s_xr1[(b + 1) * 128:(b + 2) * 128, :], writes=[xi_next])
                oo = o5.next()
                for half in range(2):
                    bk = allbanks.next()
                    for kc in range(NJ):
                        op('pe', 'matmul', [yTj[kc], Wout], [bk], sig=(kc == NJ - 1), out=bk[:, :], lhsT=yT[:, kc, b * 128:(b + 1) * 128],
                           rhs=Wout[:, kc, half * 512:(half + 1) * 512], start=(kc == 0), stop=(kc == NJ - 1))
                    hs = slice(half * 512, (half + 1) * 512)
                    op('dve', 'tensor_tensor', [bk, gateB], [t_o], out=t_o[:, hs], in0=bk[:, :], in1=gateB[:, hs], op=ALU.mult)
                    op('dve', 'tensor_tensor', [t_o, xi], [oo], out=oo[:, hs], in0=t_o[:, hs], in1=xi[:, hs], op=ALU.add)
                dma('sp', out_d[b * 128:(b + 1) * 128, :], oo[:], reads=[oo])

        kb.barrier()
        kb.emit()
    return nc


def _host_layout(inputs, b):
    f32 = np.float32
    fm = lambda v: np.ascontiguousarray(np.asarray(v, f32).reshape(NJ, 128).T)
    d = {}
    pf = {}
    mu = inputs['a_mix_mu'][0]
    for p in range(6):
        pf['mu%d' % p] = fm(mu[p])
    pf['a_norm_g'] = fm(inputs['a_norm_g'][0]); pf['w0'] = fm(inputs['a_w0'][0]); pf['a0'] = fm(inputs['a_a0'][0])
    pf['k_k'] = fm(inputs['a_k_k'][0]); pf['k_a'] = fm(inputs['a_k_a'][0]); pf['r_k'] = fm(inputs['a_r_k'][0].reshape(-1))
    pf['kv_norm_g'] = fm(inputs['kv_norm_g']); pf['b_norm_g'] = fm(inputs['b_norm_g'][0])
    ab, bb = inputs['a_ada_b'][0], inputs['b_ada_b'][0]
    pf['a_ada_b_shift'] = fm(ab[:D]); pf['a_ada_b_scale'] = fm(ab[D:2 * D])
    pf['b_ada_b_shift'] = fm(bb[:D]); pf['b_ada_b_scale'] = fm(bb[D:2 * D])
    pf['c'] = fm(inputs['c'][b])
    d['pfm'] = np.ascontiguousarray(np.stack([pf[n] for n in PFM], axis=1))
    rep = lambda v: np.broadcast_to(np.asarray(v, f32)[None, :], (128, D))
    d['ptm'] = np.ascontiguousarray(np.stack([rep(inputs['a_ln_g'][0]), rep(inputs['a_ln_b'][0]),
                                              rep(ab[2 * D:]), rep(bb[2 * D:])], axis=1))
    return d


def _consts():
    f32 = np.float32
    ti = np.arange(128)
    ident = np.eye(128, dtype=f32)
    m_su = (ti[:, None] < ti[None, :]).astype(f32)
    m_ui = (ti[:, None] <= ti[None, :]).astype(f32)
    m_sl = (ti[:, None] > ti[None, :]).astype(f32)
    m2 = np.concatenate([m_su, m_ui], 1)
    m_su_ui = np.concatenate([m2, m2], 1)
    blockones = np.kron(np.eye(2, dtype=f32), np.ones((64, 64), f32))
    rot = np.zeros((128, 128), f32)
    for po in range(128):
        d_ = po % 64
        pi = po + 32 if d_ < 32 else po - 32
        rot[pi, po] = 1.0
    m_li = (ti[:, None] >= ti[None, :]).astype(f32)
    bd = ((ti[:, None] // 64) == (ti[None, :] // 64)).astype(f32)
    sw1 = (ti[:, None] == ti[None, :] + 64).astype(f32)
    sw2 = (ti[:, None] + 64 == ti[None, :]).astype(f32)
    m_off = ((ti[:, None] < 64) & (ti[None, :] >= 64)).astype(f32)
    cst = np.concatenate([ident, m_su_ui, m_sl, m_sl, ident, ident, blockones, rot, m_ui, m_li, m_ui, m_li,
                          m_su * bd, m_ui, m_su * bd, m_ui, m_sl * bd, m_sl * bd, m_off, m_off,
                          (np.concatenate([m_ui, m_li], 1) - 1.0) * 30000.0, sw1, sw2], 1).astype(f32)
    assert cst.shape[1] == CW
    sel = np.zeros((128, NJ, H), f32)
    for p in range(128):
        for j in range(NJ):
            sel[p, j, 2 * j + p // 64] = 1.0
    pos = np.arange(S, dtype=f32)
    inv = (np.float32(10000.0) ** (-np.arange(0, 64, 2, dtype=f32) / np.float32(64))).astype(f32)
    ang = pos[:, None] * inv[None, :]
    cos, sin = np.cos(ang).astype(f32), np.sin(ang).astype(f32)
    rope = np.zeros((128, 2, S), f32)
    for p in range(128):
        d_ = p % 64
        rope[p, 0] = cos[:, d_ % 32]
        rope[p, 1] = sin[:, d_ % 32] * (-1.0 if d_ < 32 else 1.0)
    return {'cst': cst, 'sel': sel, 'rope': rope}


def _qkg(qg, kg):
    idx = np.arange(128) % 64
    par = (idx + 32) % 64
    qg = np.asarray(qg, np.float32).reshape(64); kg = np.asarray(kg, np.float32).reshape(64)
    return np.ascontiguousarray(np.stack([qg[idx], qg[par], kg[idx], kg[par]], 1).astype(np.float32))


def kernel(**inputs):
    inputs = {k: np.asarray(v) for k, v in inputs.items()}
    n = 8
    nc = build_nc()
    consts = _consts()
    shared = {k: np.ascontiguousarray(inputs[k][0], dtype=np.float32) for k in
              ('a_ada_w', 'b_ada_w', 'a_w_in', 'b_w_in', 'a_w1', 'a_w2', 'a_a1', 'a_a2', 'a_w_out', 'b_w_out')}
    shared['w_kv'] = np.ascontiguousarray(inputs['w_kv'], dtype=np.float32)
    shared['qkg'] = _qkg(inputs['b_q_norm_g'][0], inputs['k_norm_g'])
    shared.update(consts)
    in_maps = []
    for b in range(n):
        m = dict(shared)
        m['x'] = np.ascontiguousarray(inputs['x'][b], dtype=np.float32)
        m.update(_host_layout(inputs, b))
        in_maps.append(m)
    res = run_bass_kernel_spmd(nc, in_maps, core_ids=list(range(n)))
    return np.stack([r['out'] for r in res.results], axis=0).astype(np.float32)
```

```python
import math
from contextlib import ExitStack
import numpy as np
import concourse.bass as bass
import concourse.mybir as mybir
from concourse.bass_utils import run_bass_kernel_spmd

F32, BF16 = mybir.dt.float32, mybir.dt.bfloat16
ALU = mybir.AluOpType
AF = mybir.ActivationFunctionType
D, S, H, N = 1024, 2048, 16, 64
NJ = 8
C = 128
NCH = S // C
ENG = ('pe', 'act', 'dve', 'pool', 'sp')
NDS = 24
C0 = math.exp(-0.5)
DEBUG = False
LAST_PHASE = 9
SAME_ENGINE_SYNC = True
P2STOP = 99
SKIP_PHASES = ()
EMBED_WAIT = True
NSTG = 6
TMP_AFTER = 1
TRANSITIVE = True
PE_EMBED = True
P2_ACOPY_ACT = 2
STQ = 'act'
P2_POST_POOL = 0
P1_ORDER = (0, 3, 1, 2)
P3_ORDER = None
P4_LOOK = 2
P4_NPM = 4
P2_MERGE_NA = 1
P2_XADD_PE = 0

PFM = ['mu0', 'mu1', 'mu2', 'mu3', 'mu4', 'mu5', 'a_norm_g', 'w0', 'a0', 'k_k', 'k_a', 'r_k',
       'kv_norm_g', 'b_norm_g', 'a_ada_b_shift', 'a_ada_b_scale', 'b_ada_b_shift', 'b_ada_b_scale', 'c']
PTM = ['ln_g', 'ln_b', 'a_gate_b', 'b_gate_b']
CW = 3456


class Buf:
    __slots__ = ('ap', 'w', 'r', 'name', 'excl')

    def __init__(self, ap, name='', excl=False):
        self.ap, self.w, self.r, self.name, self.excl = ap, None, {}, name, excl

    def __getitem__(self, idx):
        return self.ap[idx]


def split(buf, n):
    return [Buf(buf.ap[:, i], '%s[%d]' % (buf.name, i)) for i in range(n)]


class KB:
    def __init__(self, nc, stack):
        self.nc = nc
        self.q = {e: [] for e in ENG}
        self.sem = {e: stack.enter_context(nc.semaphore('s_' + e)) for e in ENG}
        self.cnt = {e: 0 for e in ENG}
        self.seen = {e: {} for e in ENG}
        self.dsems = [stack.enter_context(nc.semaphore('d%d' % i)) for i in range(NDS)]
        self.dval = [0] * NDS
        self.dnext = 0
        self.mute = False
        self.simq = {e: [] for e in ENG}
        self.know = {}

    def _semof(self, key):
        return self.sem[key] if isinstance(key, str) else self.dsems[key[1]]

    def _deps(self, eng, reads, writes, extra=()):
        waits = {}

        def need(key, val):
            if key == eng and (eng == 'pe' or not SAME_ENGINE_SYNC):
                return
            if self.seen[eng].get(key, 0) >= val:
                return
            if waits.get(key, 0) < val:
                waits[key] = val
        for b in reads:
            if b.w:
                need(*b.w)
        self.read_keys = set(waits)
        for b in writes:
            if b.w:
                need(*b.w)
            for k, v in b.r.items():
                need(k, v)
        for k, v in extra:
            need(k, v)
        if TRANSITIVE and waits:
            waits = {k: v for k, v in waits.items() if not any(
                k != k3 and self.know.get((k3, v3), {}).get(k, 0) >= v for k3, v3 in waits.items())}
            sn = self.seen[eng]
            for k, v in waits.items():
                for k2, v2 in self.know.get((k, v), {}).items():
                    if sn.get(k2, 0) < v2:
                        sn[k2] = v2
        for k, v in waits.items():
            self.seen[eng][k] = v
        self.last_wk = list(waits.items())
        return [(self._semof(k), v) for k, v in waits.items()]

    def op(self, eng, name, reads, writes, sig=True, **kw):
        if self.mute:
            return
        writes = list(writes) + [b for b in reads if b.excl]
        reads = [b for b in reads if not b.excl]
        wl = self._deps(eng, reads, writes)
        if sig:
            self.cnt[eng] += 1
            seq = self.cnt[eng]
        else:
            seq = self.cnt[eng] + 1
        if eng == 'pe' and PE_EMBED and name in ('matmul', 'transpose'):
            wo = [i_ for i_, (k_, v_) in enumerate(self.last_wk) if k_ not in self.read_keys]
            if wo:
                i_ = wo[0]
                wl = wl[:i_] + wl[i_ + 1:] + [wl[i_]]
                kw = dict(kw, _embed_last=True)
        self.q[eng].append((wl, name, kw, self.sem[eng] if sig else None, 1))
        if sig and TRANSITIVE:
            self.know[(eng, seq)] = dict(self.seen[eng])
        self.simq[eng].append((self.last_wk, name, kw, (eng if sig else None), eng))
        for b in reads:
            b.r[eng] = max(b.r.get(eng, 0), seq)
        for b in writes:
            b.w = (eng, seq)
            b.r = {}

    def dma(self, eng, out, in_, reads=(), writes=(), **kw):
        if self.mute:
            return
        i = self.dnext
        self.dnext = (i + 1) % NDS
        prev = self.dval[i]
        key = ('d', i)
        wl = self._deps(eng, reads, writes, extra=((key, prev),) if prev else ())
        val = prev + 16
        self.dval[i] = val
        kw = dict(kw, out=out, in_=in_)
        self.q[eng].append((wl, 'dma_start', kw, self.dsems[i], 16))
        if TRANSITIVE:
            self.know[(key, val)] = dict(self.seen[eng])
        self.simq[eng].append((self.last_wk, 'dma_start', kw, key, eng))
        for b in reads:
            b.r[key] = val
        for b in writes:
            b.w = (key, val)
            b.r = {}

    def barrier(self):
        targets = [(e, self.cnt[e]) for e in ENG if self.cnt[e]] + \
                  [(('d', i), self.dval[i]) for i in range(NDS) if self.dval[i]]
        for eng in ENG:
            wl = self._deps(eng, (), (), extra=targets)
            self.q[eng].append((wl, None, None, None, 0))
            self.simq[eng].append((self.last_wk, None, None, None, eng))

    def emit(self):
        nc = self.nc

        def play(e, lst, ename=''):
            for (wl, name, kw, sem, inc) in lst:
                pe_embed = False
                if name is not None and kw is not None and kw.get('_embed_last'):
                    kw = {k_: v_ for k_, v_ in kw.items() if k_ != '_embed_last'}
                    pe_embed = True
                embed = (EMBED_WAIT and len(wl) > 0 and name in ('activation', 'tensor_tensor', 'tensor_scalar', 'tensor_copy', 'scalar_tensor_tensor', 'tensor_reduce', 'tensor_tensor_scan', 'memset')
                         and kw.get('accum_out') is None and ename != 'pe')
                for s, v in (wl[1:] if embed else (wl[:-1] if pe_embed else wl)):
                    e.wait_ge(s, v)
                if name is None:
                    continue
                ins = getattr(e, name)(**kw)
                if embed:
                    ins._wait_ge(wl[0][0], wl[0][1])
                elif pe_embed:
                    ins._wait_ge(wl[-1][0], wl[-1][1])
                if sem is not None:
                    ins.then_inc(sem, inc)
        with nc.Block() as block:
            @block.tensor
            def _(e):
                play(e, self.q['pe'], 'pe')

            @block.scalar
            def _(e):
                play(e, self.q['act'])

            @block.vector
            def _(e):
                play(e, self.q['dve'])

            @block.gpsimd
            def _(e):
                play(e, self.q['pool'])

            @block.sync
            def _(e):
                play(e, self.q['sp'])


class Arena:
    def __init__(self, ap_bf16, nbytes):
        self.base, self.cap, self.off, self.top = ap_bf16, nbytes, 0, nbytes

    def reset(self, keep_top=False):
        self.off = 0
        if not keep_top:
            self.top = self.cap

    def alloc_top(self, shape, dt, name=''):
        n = int(np.prod(shape))
        nb = (n * (4 if dt == F32 else 2) + 63) // 64 * 64
        self.top -= nb
        assert self.off <= self.top, ('arena overflow (top)', name, self.off, self.top)
        v = self.base[:, self.top // 2:(self.top + nb) // 2]
        if dt == F32:
            v = v.bitcast(F32)
        v = v[:, 0:n]
        if len(shape) == 2:
            v = v.rearrange('p (a b) -> p a b', a=shape[0])
        return Buf(v, name)

    def alloc(self, shape, dt, name=''):
        n = int(np.prod(shape))
        nb = n * (4 if dt == F32 else 2)
        nb = (nb + 63) // 64 * 64
        assert self.off + nb <= self.top, ('arena overflow', name, self.off, nb, self.top)
        v = self.base[:, self.off // 2:(self.off + nb) // 2]
        self.off += nb
        if dt == F32:
            v = v.bitcast(F32)
        v = v[:, 0:n]
        if len(shape) == 2:
            v = v.rearrange('p (a b) -> p a b', a=shape[0])
        elif len(shape) == 3:
            v = v.rearrange('p (a b c) -> p a b c', a=shape[0], b=shape[1])
        return Buf(v, name)


class Ring:
    def __init__(self, bufs):
        self.bufs, self.i = bufs, 0

    def next(self):
        b = self.bufs[self.i]
        self.i = (self.i + 1) % len(self.bufs)
        return b


def wavefront(n, stages, order=None):
    ns = len(stages)
    order = list(reversed(range(ns))) if order is None else order
    for w in range(n + ns - 1):
        for s_ in order:
            i = w - s_
            if 0 <= i < n:
                stages[s_](i)


def build_nc():
    nc = bass.Bass("TRN2", target_bir_lowering=False)
    dram_in = lambda name, shape: nc.dram_tensor(name, list(shape), F32, kind="ExternalInput").ap()
    skind = "ExternalOutput" if DEBUG else "Internal"
    scr = lambda name, shape, dt: nc.dram_tensor(name, list(shape), dt, kind=skind).ap()

    x_d = dram_in('x', [S, D])
    pfm_d = dram_in('pfm', [128, len(PFM), NJ])
    ptm_d = dram_in('ptm', [128, len(PTM), D])
    cst_d = dram_in('cst', [128, CW])
    sel_d = dram_in('sel', [128, NJ, H])
    rope_d = dram_in('rope', [128, 2, S])
    qkg_d = dram_in('qkg', [128, 4])
    a_ada_w = dram_in('a_ada_w', [D, 3 * D]); b_ada_w = dram_in('b_ada_w', [D, 3 * D])
    a_w_in = dram_in('a_w_in', [D, 4 * D]); b_w_in = dram_in('b_w_in', [D, 4 * D])
    a_w1 = dram_in('a_w1', [D, 64]); a_w2 = dram_in('a_w2', [64, D])
    a_a1 = dram_in('a_a1', [D, 64]); a_a2 = dram_in('a_a2', [64, D])
    a_w_out = dram_in('a_w_out', [D, D]); b_w_out = dram_in('b_w_out', [D, D])
    w_kv = dram_in('w_kv', [D, 2 * D])
    out_d = nc.dram_tensor('out', [S, D], F32, kind="ExternalOutput").ap()

    s_ar = scr('s_ar', [NJ, 128, NCH, 2, C], BF16)
    s_kt = scr('s_kt', [NJ, 128, S], BF16)
    s_bt = scr('s_bt', [NJ, 128, S], BF16)
    s_tm = {n: scr('s_' + n, [S, D], BF16) for n in ('At', 'Bbt', 'Kbt', 'Vt', 'SGt')}
    s_bonus = scr('s_bonus', [S, H], F32)
    s_xr1 = scr('s_xr1', [S, D], F32)

    with ExitStack() as st:
        sb = lambda name, shape, dt: st.enter_context(nc.sbuf_tensor('sb_' + name, list(shape), dt))
        kb = KB(nc, st)
        op, dma = kb.op, kb.dma
        pfm = Buf(sb('pfm', [128, len(PFM), NJ], F32)[:], 'pfm')
        cst = Buf(sb('cst', [128, 384], F32)[:], 'cst')
        cbf = Buf(sb('cbf', [128, CW], BF16)[:], 'cbf')
        sel = Buf(sb('sel', [128, NJ, H], F32)[:], 'sel')
        selb = Buf(sb('selb', [128, NJ, H], BF16)[:], 'selb')
        modA = Buf(sb('modA', [128, 16], F32)[:], 'modA')
        modB = Buf(sb('modB', [128, 16], F32)[:], 'modB')
        gsh = Buf(sb('gsh', [128, 4, NJ], F32)[:], 'gsh')
        gateA = Buf(sb('gateA', [128, D], F32)[:], 'gateA')
        gateB = Buf(sb('gateB', [128, D], F32)[:], 'gateB')
        glast = Buf(sb('glast', [128, NJ, NCH], F32)[:], 'glast')
        ST_all = Buf(sb('ST', [128, NJ, N], F32)[:], 'ST')
        STb_all = Buf(sb('STb', [128, NJ, N], BF16)[:], 'STb')
        ST, STb = split(ST_all, NJ), split(STb_all, NJ)
        GT = [Buf(sb('GT%d' % j, [128, 128], F32)[:], 'GT%d' % j) for j in range(NJ)]
        zeros = Buf(sb('zeros', [128, 512], F32)[:], 'zeros')
        epsb = Buf(sb('epsb', [128, 8], F32)[:], 'epsb')
        npar = Buf(sb('npar', [128, 2, NJ], F32)[:], 'npar')
        ARENA_BYTES = 180 * 1024
        arena = Arena(sb('arena', [128, ARENA_BYTES // 2], BF16)[:], ARENA_BYTES)
        banks = [Buf(st.enter_context(nc.psum_tensor('bank%d' % i, [128, 512], F32))[:], 'bank%d' % i, excl=True)
                 for i in range(8)]
        psum = Ring(banks)

        pidx = {n: i for i, n in enumerate(PFM)}
        P = lambda name, j: pfm[:, pidx[name], j:j + 1]
        ident = cst[:, 0:128]
        identb = cbf[:, 0:128]
        m_su_ui = cbf[:, 128:640].rearrange('p (h c) -> p h c', h=2)
        m_sbd_ui = cbf[:, 1920:2432].rearrange('p (h c) -> p h c', h=2)
        m_slbd2 = cbf[:, 2432:2688].rearrange('p (h c) -> p h c', h=2)
        m_off2 = cbf[:, 2688:2944].rearrange('p (h c) -> p h c', h=2)
        ident2b = cbf[:, 896:1152].rearrange('p (h c) -> p h c', h=2)
        blockones = cbf[:, 1152:1280]
        rotp = cbf[:, 1280:1408]

        dma('sp', pfm[:], pfm_d, writes=[pfm])
        dma('sp', sel[:], sel_d, writes=[sel])
        op('dve', 'tensor_copy', [sel], [selb], out=selb[:], in_=sel[:])
        op('dve', 'tensor_scalar', [pfm], [npar], out=npar[:, 0, :], in0=pfm[:, pidx['w0'], :], scalar1=-1.0, scalar2=None, op0=ALU.mult)
        op('dve', 'tensor_scalar', [pfm, npar], [npar], out=npar[:, 1, :], in0=pfm[:, pidx['a0'], :], scalar1=-1.0, scalar2=None, op0=ALU.mult)
        op('pool', 'memset', [], [zeros], ap=zeros[:], constant=0.0)
        for i_, v_ in enumerate((1e-6, 1e-24, 64e-5, 64e-6, 1.0)):
            op('pool', 'memset', [epsb], [epsb], ap=epsb[:, i_:i_ + 1], constant=v_)
        op('pool', 'memset', [], ST, ap=ST_all[:], constant=0.0)
        op('pool', 'memset', [], STb, ap=STb_all[:], constant=0.0)
        for j in range(NJ):
            op('pool', 'memset', [], [GT[j]], ap=GT[j][:], constant=0.0)

        LAYERS = {'a': (a_ada_w, modA, gateA, 'a_ada_b_shift', 'a_ada_b_scale', 0, 0, 'a_norm_g'),
                  'b': (b_ada_w, modB, gateB, 'b_ada_b_shift', 'b_ada_b_scale', 1, 2, 'b_norm_g')}

        def ada_setup(nbuf=2):
            silc = arena.alloc([NJ], F32, 'silc')
            ptmg = arena.alloc([2, D], F32, 'ptmg')
            dma('sp', ptmg[:], ptm_d[:, 2:4, :], writes=[ptmg])
            silrep = arena.alloc([NJ, 128], F32, 'silrep')
            adaw_p = Ring([arena.alloc([NJ, 512], F32, 'adaw%d' % i) for i in range(nbuf)])
            op('act', 'activation', [pfm], [silc], out=silc[:], in_=pfm[:, pidx['c'], :], func=AF.Silu)
            for kc in range(NJ):
                op('dve', 'tensor_scalar', [silc, zeros], [silrep], out=silrep[:, kc, :], in0=zeros[:, 0:128],
                   scalar1=silc[:, kc:kc + 1], scalar2=None, op0=ALU.add)
            return dict(silc=silc, ptmg=ptmg, silrep=silrep, adaw_p=adaw_p)

        def ada_load(ctx, layer, ct):
            ada_w = LAYERS[layer][0]
            wv = ada_w.rearrange('(kc p) n -> p kc n', p=128)
            wt = ctx['adaw_p'].next()
            dma('sp', wt[:], wv[:, :, ct * 512:(ct + 1) * 512], writes=[wt])
            ctx['wt'] = wt

        def ada_ct(ctx, layer, ct, load=True):
            (ada_w, mod, gate, bshift, bscale, gi_, li, ng) = LAYERS[layer]
            silc, ptmg, silrep = ctx['silc'], ctx['ptmg'], ctx['silrep']
            if load:
                ada_load(ctx, layer, ct)
            wt = ctx['wt']
            bk = psum.next()
            if ct < 4:
                for fc in range(4):
                    col = ct * 4 + fc
                    for kc in range(NJ):
                        op('pe', 'matmul', [wt, silc], [bk], sig=(kc == NJ - 1), out=bk[:, col:col + 1],
                           lhsT=wt[:, kc, fc * 128:(fc + 1) * 128], rhs=silc[:, kc:kc + 1],
                           start=(kc == 0), stop=(kc == NJ - 1))
                op('dve', 'tensor_copy', [bk], [mod], out=mod[:, ct * 4:ct * 4 + 4], in_=bk[:, ct * 4:ct * 4 + 4])
            else:
                for kc in range(NJ):
                    op('pe', 'matmul', [wt, silrep], [bk], sig=(kc == NJ - 1), out=bk[:, :],
                       lhsT=silrep[:, kc, :], rhs=wt[:, kc, :], start=(kc == 0), stop=(kc == NJ - 1))
                c0_ = (ct - 4) * 512
                op('dve', 'tensor_tensor', [bk, ptmg], [gate], out=gate[:, c0_:c0_ + 512], in0=bk[:, :],
                   in1=ptmg[:, gi_, c0_:c0_ + 512], op=ALU.add)
            if ct == 5:
                op('dve', 'tensor_tensor', [mod, pfm, gsh], [gsh], out=gsh[:, li + 1, :], in0=mod[:, 0:8],
                   in1=pfm[:, pidx[bshift], :], op=ALU.add)
                op('dve', 'tensor_tensor', [mod, pfm, gsh], [gsh], out=gsh[:, li, :], in0=mod[:, 8:16],
                   in1=pfm[:, pidx[bscale], :], op=ALU.add)
                op('dve', 'scalar_tensor_tensor', [gsh, pfm], [gsh], out=gsh[:, li, :], in0=gsh[:, li, :], scalar=1.0,
                   in1=pfm[:, pidx[ng], :], op0=ALU.add, op1=ALU.mult)

        def load_w_bf16(dst, src_view, ncols, cw=512):
            for c0_ in range(0, ncols, cw):
                dma('pool', dst[:, :, c0_:c0_ + cw], src_view[:, :, c0_:c0_ + cw], writes=[dst])

        arena.reset()
        Win = arena.alloc_top([NJ, 4 * D], BF16, 'Win')
        load_w_bf16(Win, a_w_in.rearrange('(kc p) n -> p kc n', p=128), 4 * D)
        cstage = arena.alloc([CW], F32, 'cstage')
        dma('sp', cstage[:], cst_d, writes=[cstage])
        op('dve', 'tensor_copy', [cstage], [cbf], out=cbf[:], in_=cstage[:])
        op('act', 'activation', [cstage], [cst], out=cst[:, 0:128], in_=cstage[:, 0:128], func=AF.Copy)
        op('act', 'activation', [cstage, cst], [cst], out=cst[:, 128:384], in_=cstage[:, 3200:3456], func=AF.Copy)
        actx = ada_setup()
        for ct in range(6):
            ada_ct(actx, 'a', ct)
        kb.barrier()

        def load_rows(src_dram, t0, TB, xt):
            dma('sp', xt[:, 0:TB, :], src_dram.rearrange('(b p) d -> p b d', p=128)[:, t0 // 128:t0 // 128 + TB, :],
                writes=[xt])

        def norm_transpose(TB, xt, sq, ss, xn, hT, hTj, gs_ap, sh_ap, col0):
            for b in range(TB):
                op('act', 'activation', [xt], [sq, ss], out=sq[:], in_=xt[:, b, :], func=AF.Square,
                   accum_out=ss[:, b:b + 1])
            op('act', 'activation', [ss, epsb], [ss], out=ss[:, 0:TB], in_=ss[:, 0:TB], func=AF.Ln, scale=1.0 / D, bias=epsb[:, 0:1])
            op('act', 'activation', [ss], [ss], out=ss[:, 0:TB], in_=ss[:, 0:TB], func=AF.Exp, scale=-0.5)
            for b in range(TB):
                op('act', 'activation', [xt, ss], [xn], out=xn[:, b, :], in_=xt[:, b, :], func=AF.Copy,
                   scale=ss[:, b:b + 1])
            for j in range(NJ):
                bk = psum.next()
                for b in range(TB):
                    op('pe', 'transpose', [xn, cst], [bk], sig=(b == TB - 1), out=bk[:, b * 128:(b + 1) * 128],
                       in_=xn[:, b, j * 128:(j + 1) * 128], identity=ident)
                if j % 2 == 0:
                    op('dve', 'tensor_scalar', [bk, gsh], [hTj[j]], out=hT[:, j, col0:col0 + TB * 128],
                       in0=bk[:, 0:TB * 128], scalar1=gs_ap(j), scalar2=sh_ap(j), op0=ALU.mult, op1=ALU.add)
                else:
                    op('act', 'activation', [bk, gsh], [hTj[j]], out=hT[:, j, col0:col0 + TB * 128],
                       in_=bk[:, 0:TB * 128], func=AF.Identity, scale=gs_ap(j), bias=sh_ap(j))

        if LAST_PHASE >= 1:
            kb.mute = 1 in SKIP_PHASES
            TB = 2
            T = TB * 128
            arena.reset(keep_top=True)
            W1 = arena.alloc([NJ, 64], BF16, 'W1'); A1 = arena.alloc([NJ, 64], BF16, 'A1')
            W2 = arena.alloc([D], BF16, 'W2'); A2 = arena.alloc([D], BF16, 'A2')
            xt_r = Ring([arena.alloc([TB, D], F32, 'xt%d' % i) for i in range(1)])
            ss = arena.alloc([4], F32, 'ss')
            hT = arena.alloc([NJ, T + 1], F32, 'hT'); hTj = split(hT, NJ)
            xx = arena.alloc([NJ, T], F32, 'xx'); xxj = split(xx, NJ)
            _xsb = [arena.alloc([NJ, T], BF16, 'xs%d' % i) for i in range(2)]
            xs_r_ = Ring([(b_, split(b_, NJ)) for b_ in _xsb])
            lt = Ring([arena.alloc([T], BF16, 'lt%d' % i) for i in range(2)])
            _tsz = {'rk': (3, [2, T]), 'sg': (2, [2, T]), 'cs': (2, [T]), 'tqa': (2, [T]), 'tqb': (1, [T]), 'kkr': (2, [T]), 'k2': (3, [T]),
                    'gam': (2, [T]), 'ginv': (2, [T]), 'gprev': (2, [T]), 'ginvl': (2, [T]), 'rn': (1, [T]), 'kkn': (1, [T]), 'b_': (1, [T])}
            tmp = {n: Ring([arena.alloc(sh_, F32, n + str(i)) for i in range(k_)]) for n, (k_, sh_) in _tsz.items()}
            sqb = Ring([arena.alloc([T], BF16, 'sqb%d' % i) for i in range(2)])
            o_ar = arena.alloc([NJ, TB, 2 * C], BF16, 'o_ar'); o_arj = split(o_ar, NJ)
            o_k = arena.alloc([NJ, T], BF16, 'o_k'); o_kj = split(o_k, NJ)
            o_b = arena.alloc([NJ, T], BF16, 'o_b'); o_bj = split(o_b, NJ)
            o_kb = arena.alloc([NJ, T], BF16, 'o_kb'); o_kbj = split(o_kb, NJ)
            o_bb = arena.alloc([NJ, T], BF16, 'o_bb'); o_bbj = split(o_bb, NJ)
            o_rk = arena.alloc([NJ, T], BF16, 'o_rk'); o_rkj = split(o_rk, NJ)
            stg = Ring([arena.alloc([D], BF16, 'stg%d' % i) for i in range(NSTG)])
            sq = stg.bufs[2]
            bon = arena.alloc([TB, H], F32, 'bon')
            nbias = Ring([arena.alloc([TB], F32, 'nbias%d' % i) for i in range(2)])

            dma('pool', W1[:], a_w1.rearrange('(kc p) n -> p kc n', p=128), writes=[W1])
            dma('pool', A1[:], a_a1.rearrange('(kc p) n -> p kc n', p=128), writes=[A1])
            dma('pool', W2[0:64, :], a_w2, writes=[W2])
            dma('pool', A2[0:64, :], a_a2, writes=[A2])
            op('dve', 'memset', [], hTj, ap=hT[:, :, 0:1], constant=0.0)
            nT = S // T
            xt = xt_r.next()
            load_rows(x_d, 0, TB, xt)
            for tt in range(nT):
                t0 = tt * T
                norm_transpose(TB, xt, sq, ss, xt, hT, hTj, lambda j: gsh[:, 0, j:j + 1], lambda j: gsh[:, 1, j:j + 1], 1)
                if tt + 1 < nT:
                    load_rows(x_d, t0 + T, TB, xt)
                op('dve', 'tensor_tensor', hTj, xxj, out=xx[:], in0=hT[:, :, 0:T], in1=hT[:, :, 1:T + 1], op=ALU.subtract)

                def make_xs(p):
                    xs, xsj = xs_r_.next()
                    for j in range(NJ):
                        op('dve', 'scalar_tensor_tensor', [xxj[j], hTj[j], pfm], [xsj[j]], out=xs[:, j, :], in0=xx[:, j, :],
                           scalar=P('mu%d' % p, j), in1=hT[:, j, 1:T + 1], op0=ALU.mult, op1=ALU.add)
                    return xs, xsj

                def lora_mid(xs, xsj, Wl, func):
                    l_ = lt.next()
                    bk = psum.next()
                    for kc in range(NJ):
                        op('pe', 'matmul', [Wl, xsj[kc]], [bk], sig=(kc == NJ - 1), out=bk[0:64, 0:T], lhsT=Wl[:, kc, :],
                           rhs=xs[:, kc, :], start=(kc == 0), stop=(kc == NJ - 1))
                    op('act', 'activation', [bk], [l_], out=l_[0:64, :], in_=bk[0:64, 0:T], func=func)
                    return l_
                xs_w, xs_wj = make_xs(4)
                ltw = lora_mid(xs_w, xs_wj, W1, AF.Tanh)
                xs_a, xs_aj = make_xs(5)
                lta = lora_mid(xs_a, xs_aj, A1, AF.Copy)
                xs_r, xs_rj = make_xs(0)
                xs_k, xs_kj = make_xs(1)

                jx = {}
                v3 = lambda ap: ap.rearrange('p (b t) -> p b t', b=TB)

                def sP(j):
                    fs = slice(j * 128, (j + 1) * 128)
                    b_rk, b_z = psum.next(), psum.next()
                    for (c0b, xs_, xsj_, cb) in ((0, xs_r, xs_rj, 0), (T, xs_k, xs_kj, D)):
                        for kc in range(NJ):
                            op('pe', 'matmul', [Win, xsj_[kc]], [b_rk], sig=(kc == NJ - 1), out=b_rk[:, c0b:c0b + T],
                               lhsT=Win[:, kc, cb + j * 128:cb + (j + 1) * 128], rhs=xs_[:, kc, :],
                               start=(kc == 0), stop=(kc == NJ - 1))
                    op('pe', 'matmul', [W2, ltw], [b_z], out=b_z[:, 0:T], lhsT=W2[0:64, fs], rhs=ltw[0:64, :], start=True, stop=True)
                    op('pe', 'matmul', [A2, lta], [b_z], out=b_z[:, T:2 * T], lhsT=A2[0:64, fs], rhs=lta[0:64, :], start=True, stop=True)
                    jx[j] = dict(b_rk=b_rk, b_z=b_z)

                def sA(j):
                    b_rk, b_z = jx[j]['b_rk'], jx[j]['b_z']
                    t_ = {n: tmp[n].next() for n in ('rk', 'sg', 'cs', 'tqa', 'tqb', 'kkr', 'k2')}
                    rk_, sg_, cs, tqa, tqb, kkr, k2 = (t_[n] for n in ('rk', 'sg', 'cs', 'tqa', 'tqb', 'kkr', 'k2'))
                    r_, k_, s1, ic = rk_[:, 0, :], rk_[:, 1, :], sg_[:, 0, :], sg_[:, 1, :]
                    nb_ = nbias.next()
                    op('act', 'activation', [b_rk], [rk_], out=rk_[:], in_=b_rk[:, 0:2 * T].rearrange('p (a t) -> p a t', a=2), func=AF.Copy)
                    for pi_ in range(2):
                        op('act', 'activation', [b_z, npar], [sg_], out=sg_[:, pi_, :], in_=b_z[:, pi_ * T:(pi_ + 1) * T], func=AF.Exp, scale=-1.0,
                           bias=npar[:, pi_, j:j + 1])
                    op('act', 'activation', [sg_, epsb], [sg_], out=sg_[:], in_=sg_[:], func=AF.Ln, bias=epsb[:, 4:5])
                    op('act', 'activation', [sg_], [sg_], out=sg_[:], in_=sg_[:], func=AF.Exp, scale=-1.0)
                    for b in range(TB):
                        bs = slice(b * C, (b + 1) * C)
                        op('dve', 'tensor_tensor_scan', [sg_, zeros], [cs], out=cs[:, bs], data0=s1[:, bs], data1=zeros[:, 0:C],
                           initial=0.0, op0=ALU.add, op1=ALU.add)
                    op('dve', 'tensor_scalar', [cs], [nb_], out=nb_[:, 0:TB], in0=cs[:, C - 1::C], scalar1=-C0, scalar2=None, op0=ALU.mult)
                    op('dve', 'tensor_tensor', [cs, sg_], [tqa], out=tqa[:], in0=cs[:], in1=s1, op=ALU.subtract)
                    op('dve', 'tensor_scalar', [rk_, pfm], [kkr], out=kkr[:], in0=k_, scalar1=P('k_k', j), scalar2=None, op0=ALU.mult)
                    op('dve', 'tensor_scalar', [sg_, pfm], [tqb], out=tqb[:], in0=ic, scalar1=1.0, scalar2=P('k_a', j),
                       op0=ALU.subtract, op1=ALU.mult)
                    op('dve', 'scalar_tensor_tensor', [tqb, rk_], [k2], out=k2[:], in0=tqb[:], scalar=1.0, in1=k_, op0=ALU.add, op1=ALU.mult)
                    jx[j] = dict(rk=rk_, sg=sg_, cs=cs, tqa=tqa, kkr=kkr, k2=k2, nb=nb_)

                def sB(j):
                    c_ = jx[j]
                    cs, nb_, kkr = c_['cs'], c_['nb'], c_['kkr']
                    t_ = {n: tmp[n].next() for n in ('gam', 'ginv', 'gprev', 'ginvl')}
                    gam, ginv, gprev, ginvl = (t_[n] for n in ('gam', 'ginv', 'gprev', 'ginvl'))
                    sq_ = sqb.next()
                    op('act', 'activation', [kkr], [sq_], out=sq_[:], in_=kkr[:], func=AF.Square)
                    b_ss = psum.next()
                    op('pe', 'matmul', [cbf, sq_], [b_ss], out=b_ss[:, 0:T], lhsT=blockones, rhs=sq_[:], start=True, stop=True)
                    op('act', 'activation', [cs], [gam], out=gam[:], in_=cs[:], func=AF.Exp, scale=-C0)
                    op('act', 'activation', [cs], [ginv], out=ginv[:], in_=cs[:], func=AF.Exp, scale=C0)
                    op('act', 'activation', [c_['tqa']], [gprev], out=gprev[:], in_=c_['tqa'][:], func=AF.Exp, scale=-C0)
                    for b in range(TB):
                        bs = slice(b * C, (b + 1) * C)
                        op('act', 'activation', [cs, nb_], [ginvl], out=ginvl[:, bs], in_=cs[:, bs], func=AF.Exp, scale=C0, bias=nb_[:, b:b + 1])
                    op('act', 'activation', [gam], [glast], out=glast[:, j, tt * TB:(tt + 1) * TB], in_=gam[:, C - 1::C], func=AF.Copy)
                    c_.update(gam=gam, ginv=ginv, gprev=gprev, ginvl=ginvl, b_ss=b_ss)

                def sC(j):
                    c_ = jx.pop(j)
                    rk_, sg_, kkr, k2 = c_['rk'], c_['sg'], c_['kkr'], c_['k2']
                    gam, ginv, gprev, ginvl, b_ss = c_['gam'], c_['ginv'], c_['gprev'], c_['ginvl'], c_['b_ss']
                    r_, ic = rk_[:, 0, :], sg_[:, 1, :]
                    rn, kkn, b__ = tmp['rn'].next(), tmp['kkn'].next(), tmp['b_'].next()
                    op('act', 'activation', [b_ss, epsb], [rn], out=rn[:], in_=b_ss[:, 0:T], func=AF.Ln, bias=epsb[:, 1:2])
                    op('act', 'activation', [rn], [rn], out=rn[:], in_=rn[:], func=AF.Exp, scale=-0.5)
                    op('dve', 'tensor_tensor', [kkr, rn], [kkn], out=kkn[:], in0=kkr[:], in1=rn[:], op=ALU.mult)
                    op('dve', 'tensor_tensor', [kkn, sg_], [b__], out=b__[:], in0=kkn[:], in1=ic, op=ALU.mult)
                    op('pool', 'tensor_tensor', [rk_, gam], [o_arj[j]], out=o_ar[:, j, :, C:2 * C], in0=v3(r_), in1=v3(gam[:]), op=ALU.mult)
                    for b in range(TB):
                        bs = slice(b * C, (b + 1) * C)
                        op('dve', 'scalar_tensor_tensor', [kkn, gprev, o_arj[j]], [o_arj[j]], out=o_ar[:, j, b, 0:C], in0=kkn[:, bs],
                           scalar=-1.0, in1=gprev[:, bs], op0=ALU.mult, op1=ALU.mult)
                    op('dve', 'tensor_tensor', [k2, ginv], [o_kj[j]], out=o_k[:, j, :], in0=k2[:], in1=ginv[:], op=ALU.mult)
                    op('pool', 'tensor_tensor', [b__, ginv], [o_bj[j]], out=o_b[:, j, :], in0=b__[:], in1=ginv[:], op=ALU.mult)
                    op('dve', 'tensor_tensor', [k2, ginvl], [o_kbj[j]], out=o_kb[:, j, :], in0=k2[:], in1=ginvl[:], op=ALU.mult)
                    op('pool', 'tensor_tensor', [b__, ginvl], [o_bbj[j]], out=o_bb[:, j, :], in0=b__[:], in1=ginvl[:], op=ALU.mult)
                    op('dve', 'scalar_tensor_tensor', [rk_, k2, pfm], [o_rkj[j]], out=o_rk[:, j, :], in0=r_, scalar=P('r_k', j), in1=k2[:],
                       op0=ALU.mult, op1=ALU.mult)
                wavefront(NJ, [sP, sA, sB, sC], order=list(P1_ORDER))
                ch0 = t0 // C
                dma('sp', s_ar.rearrange('j p c a t -> p j c (a t)')[:, :, ch0:ch0 + TB, :], o_ar[:], reads=o_arj)
                dma('sp', s_kt.rearrange('j p t -> p j t')[:, :, t0:t0 + T], o_k[:], reads=o_kj)
                dma('sp', s_bt.rearrange('j p t -> p j t')[:, :, t0:t0 + T], o_b[:], reads=o_bj)
                for (srcf, srcj, nm) in ((lambda j, b: o_ar[:, j, b, 0:C], o_arj, 'At'),
                                         (lambda j, b: o_kb[:, j, b * C:(b + 1) * C], o_kbj, 'Kbt'),
                                         (lambda j, b: o_bb[:, j, b * C:(b + 1) * C], o_bbj, 'Bbt')):
                    for b in range(TB):
                        bk = psum.next()
                        bkb = bk[:].bitcast(BF16)
                        for j in range(NJ):
                            op('pe', 'transpose', [srcj[j], cbf], [bk], sig=(j == NJ - 1), out=bkb[:, j * 128:(j + 1) * 128],
                               in_=srcf(j, b), identity=identb)
                        sg_ = stg.next()
                        if b % 2 == 0:
                            op('act', 'activation', [bk], [sg_], out=sg_[:], in_=bkb, func=AF.Copy)
                            dma(STQ, s_tm[nm][t0 + b * C:t0 + (b + 1) * C, :], sg_[:], reads=[sg_])
                        else:
                            op('dve', 'tensor_copy', [bk], [sg_], out=sg_[:], in_=bkb)
                            dma('sp', s_tm[nm][t0 + b * C:t0 + (b + 1) * C, :], sg_[:], reads=[sg_])
                for (pidx_, cb, nm) in ((2, 2 * D, 'Vt'), (3, 3 * D, 'SGt')):
                    xs_, xsj_ = make_xs(pidx_)
                    for b in range(TB):
                        sg_ = stg.next()
                        for half in range(2):
                            bk = psum.next()
                            for kc in range(NJ):
                                op('pe', 'matmul', [xsj_[kc], Win], [bk], sig=(kc == NJ - 1), out=bk[:, :],
                                   lhsT=xs_[:, kc, b * C:(b + 1) * C], rhs=Win[:, kc, cb + half * 512:cb + (half + 1) * 512],
                                   start=(kc == 0), stop=(kc == NJ - 1))
                            op('act', 'activation', [bk], [sg_], out=sg_[:, half * 512:(half + 1) * 512], in_=bk[:, :],
                               func=(AF.Copy if nm == 'Vt' else AF.Silu))
                        dma(STQ, s_tm[nm][t0 + b * C:t0 + (b + 1) * C, :], sg_[:], reads=[sg_])
                bk = psum.next()
                for b in range(TB):
                    for j in range(NJ):
                        op('pe', 'matmul', [o_rkj[j], selb], [bk], sig=(j == NJ - 1), out=bk[:, b * H:(b + 1) * H],
                           lhsT=o_rk[:, j, b * C:(b + 1) * C], rhs=selb[:, j, :], start=(j == 0), stop=(j == NJ - 1))
                op('dve', 'tensor_copy', [bk], [bon], out=bon[:], in_=bk[:, 0:TB * H].rearrange('p (b h) -> p b h', b=TB))
                dma('sp', s_bonus.rearrange('(b p) h -> p b h', p=128)[:, t0 // 128:t0 // 128 + TB, :], bon[:], reads=[bon])
                for j in range(NJ):
                    op('dve', 'tensor_copy', [hTj[j]], [hTj[j]], out=hT[:, j, 0:1], in_=hT[:, j, T:T + 1])
            kb.mute = False
            kb.barrier()

        if LAST_PHASE >= 2:
            kb.mute = 2 in SKIP_PHASES
            arena.reset()
            wkv_pre = arena.alloc([NJ, 2 * D], BF16, 'wkv_pre')
            slot_arena = Arena(wkv_pre[:].rearrange('p a b -> p (a b)'), NJ * 2 * D * 2)
            Wout = arena.alloc([NJ, D], BF16, 'Wout')
            load_w_bf16(Wout, a_w_out.rearrange('(kc p) n -> p kc n', p=128), D)
            ptml = arena.alloc([2, D], F32, 'ptml')
            dma('sp', ptml[:], ptm_d[:, 0:2, :], writes=[ptml])

            def mk_loads(i):
                d_ = {}
                d_['AR'] = arena.alloc([NJ, 2 * C], BF16, 'AR%d' % i)
                d_['KT'] = arena.alloc([NJ, C], BF16, 'KT%d' % i)
                d_['BT'] = arena.alloc([NJ, C], BF16, 'BT%d' % i)
                for n_ in ('At', 'Bbt', 'Kbt', 'Vt', 'SGt'):
                    d_[n_] = arena.alloc([D], BF16, n_ + str(i))
                d_['bon'] = arena.alloc([H], F32, 'bonl%d' % i)
                d_['xin'] = arena.alloc([D], F32, 'xin%d' % i)
                d_['ARz'] = arena.alloc([NJ, 2, 2 * C], BF16, 'ARz%d' % i)
                d_['BTz'] = arena.alloc([NJ, 2, C], BF16, 'BTz%d' % i)
                op('pool', 'memset', [], [d_['ARz']], ap=d_['ARz'][:], constant=0.0)
                op('pool', 'memset', [], [d_['BTz']], ap=d_['BTz'][:], constant=0.0)
                return d_
            lds = Ring([mk_loads(i) for i in range(2)])
            NSL = 4
            slots = []
            for i in range(NSL):
                sl = {}
                sl['S1m'] = slot_arena.alloc([2, 2 * C], BF16, 'S1m%d' % i)
                sl['S2m'] = slot_arena.alloc([2, 2 * C], BF16, 'S2m%d' % i)
                sl['Aoff'] = slot_arena.alloc([2, C], BF16, 'Aoff%d' % i)
                sl['NA'] = [slot_arena.alloc([2, 2 * C], BF16, 'NA%d_%d' % (i, k)) for k in range(2)]
                sl['N'] = [Buf(sl['NA'][k][:, :, 0:C], 'N%d_%d' % (i, k)) for k in range(2)]
                sl['A'] = [Buf(sl['NA'][k][:, :, C:2 * C], 'A%d_%d' % (i, k)) for k in range(2)]
                sl['X'] = [slot_arena.alloc([2, C], BF16, 'X%d_%d' % (i, k)) for k in range(2)]
                sl['Zp'] = slot_arena.alloc([2, C], BF16, 'Zp%d' % i)
                sl['Pp'] = slot_arena.alloc([2, C], BF16, 'Pp%d' % i)
                sl['tmpV'] = slot_arena.alloc([2, N], BF16, 'tmpV%d' % i)
                sl['WU'] = slot_arena.alloc([2, C], BF16, 'WU%d' % i)
                sl['RpT'] = slot_arena.alloc([2, C], BF16, 'RpT%d' % i)
                op('pool', 'memset', [], [sl['RpT']], ap=sl['RpT'][:], constant=0.0)
                slots.append(sl)
            y_sb = arena.alloc([D], F32, 'y_sb')
            yn = arena.alloc([D], F32, 'yn'); bv = arena.alloc([D], F32, 'bv'); ysq = bv
            stt = arena.alloc([4, H], F32, 'stt')
            yfin = arena.alloc([D], BF16, 'yfin')
            yT = arena.alloc([NJ, C], BF16, 'yT')
            t_o = arena.alloc([D], F32, 't_o')
            xr_o = Ring([arena.alloc([D], F32, 'xr_o%d' % i) for i in range(1)])

            bctx2 = ada_setup(nbuf=1)

            def issue_loads(c):
                L = lds.next()
                dma('sp', L['AR'][:], s_ar.rearrange('j p c a t -> p j c (a t)')[:, :, c, :], writes=[L['AR']])
                dma('sp', L['KT'][:], s_kt.rearrange('j p t -> p j t')[:, :, c * C:(c + 1) * C], writes=[L['KT']])
                dma('sp', L['BT'][:], s_bt.rearrange('j p t -> p j t')[:, :, c * C:(c + 1) * C], writes=[L['BT']])
                for n_ in ('At', 'Bbt', 'Kbt', 'Vt', 'SGt'):
                    dma('sp', L[n_][:], s_tm[n_][c * C:(c + 1) * C, :], writes=[L[n_]])
                dma('sp', L['bon'][:], s_bonus[c * C:(c + 1) * C, :], writes=[L['bon']])
                dma('sp', L['xin'][:], x_d[c * C:(c + 1) * C, :], writes=[L['xin']])
                return L
            def stage(k):
                kb.mute = (k > P2STOP) or (2 in SKIP_PHASES)
            L_next = issue_loads(0)
            for c in range(NCH):
                L = L_next
                if c + 1 < NCH:
                    L_next = issue_loads(c + 1)
                AR, KT, BT, At, Bbt, Kbt, Vt, SGt = (L[k_] for k_ in ('AR', 'KT', 'BT', 'At', 'Bbt', 'Kbt', 'Vt', 'SGt'))
                if c < 6:
                    ada_load(bctx2, 'b', c)
                ARz, BTz = L['ARz'], L['BTz']

                def zpad(Lx):
                    for h2 in range(2):
                        pb = 64 * h2
                        op('act', 'activation', [Lx['AR']], [Lx['ARz']], out=Lx['ARz'][pb:pb + 64, :, h2, :], in_=Lx['AR'][pb:pb + 64, :, :], func=AF.Copy)
                        op('act', 'activation', [Lx['BT']], [Lx['BTz']], out=Lx['BTz'][pb:pb + 64, :, h2, :], in_=Lx['BT'][pb:pb + 64, :, :], func=AF.Copy)
                if c == 0:
                    zpad(L)
                ybanks = []
                for half in range(2):
                    pairs = list(range(4 * half, 4 * half + 4))
                    stage(0)
                    for j in pairs:
                        sl = slots[j % NSL]
                        b1, b2, b3 = psum.next(), psum.next(), psum.next()
                        for h2 in range(2):
                            op('pe', 'matmul', [BT, ARz], [b1], out=b1[:, h2 * 256:(h2 + 1) * 256], lhsT=BT[:, j, :],
                               rhs=ARz[:, j, h2, :], start=True, stop=True)
                            op('pe', 'matmul', [KT, ARz], [b2], out=b2[:, h2 * 256:(h2 + 1) * 256], lhsT=KT[:, j, :],
                               rhs=ARz[:, j, h2, :], start=True, stop=True)
                            op('pe', 'matmul', [AR, BTz], [b3], out=b3[:, h2 * 128:(h2 + 1) * 128], lhsT=AR[:, j, 0:C],
                               rhs=BTz[:, j, h2, :], start=True, stop=True)
                        v2 = lambda ap: ap.rearrange('p (h c) -> p h c', h=2)
                        op('dve', 'tensor_tensor', [b1, cst], [sl['S1m']], out=sl['S1m'][:], in0=v2(b1[:, :]), in1=m_sbd_ui, op=ALU.mult)
                        op('dve', 'tensor_tensor', [b3, cst], [sl['N'][0]], out=sl['N'][0][:], in0=v2(b3[:, 0:256]), in1=m_slbd2, op=ALU.mult)
                        op('dve', 'tensor_tensor', [b1, cst], [sl['Aoff']], out=sl['Aoff'][:], in0=v2(b1[:, :])[:, :, 0:C], in1=m_off2, op=ALU.mult)
                        op('dve', 'tensor_tensor', [b2, cst], [sl['S2m']], out=sl['S2m'][:], in0=v2(b2[:, :]), in1=m_su_ui, op=ALU.mult)
                        op('pool', 'tensor_tensor', [sl['S1m'], cbf], [sl['X'][1]], out=sl['X'][1][:], in0=sl['S1m'][:, :, 0:C], in1=ident2b, op=ALU.add)
                    stage(1)
                    for s in range(1, 7):
                        if s == TMP_AFTER + 1:
                            for j in pairs:
                                sl = slots[j % NSL]
                                b8 = psum.next()
                                for h2 in range(2):
                                    h = 2 * j + h2
                                    op('pe', 'matmul', [sl['S2m'], Vt], [b8], out=b8[:, h2 * N:(h2 + 1) * N], lhsT=sl['S2m'][:, h2, 0:C],
                                       rhs=Vt[:, h * N:(h + 1) * N], start=True, stop=True)
                                op('act', 'activation', [b8], [sl['tmpV']], out=sl['tmpV'][:], in_=b8[:, 0:2 * N].rearrange('p (h c) -> p h c', h=2), func=AF.Copy)
                        for j in pairs:
                            sl = slots[j % NSL]
                            Np, Nn = sl['N'][(s - 1) % 2], sl['N'][s % 2]
                            Ap_buf = sl['S1m'] if s == 1 else sl['A'][(s - 1) % 2]
                            Ap = (lambda h2, b_=Ap_buf: b_[:, h2, 0:C])
                            An = sl['A'][s % 2]
                            Xp, Xn = sl['X'][(s - 1) % 2], sl['X'][s % 2]
                            v2 = lambda ap: ap.rearrange('p (h c) -> p h c', h=2)
                            if P2_MERGE_NA:
                                bNA = psum.next() if s <= 5 else None
                                bD = psum.next() if s >= 2 else None
                                for h2 in range(2):
                                    if s <= 5:
                                        op('pe', 'matmul', [Ap_buf, Np], [bNA], out=bNA[:, h2 * 256:h2 * 256 + C], lhsT=Ap(h2), rhs=Np[:, h2, :],
                                           start=True, stop=True)
                                    if s <= 4:
                                        op('pe', 'matmul', [Ap_buf, Np], [bNA], out=bNA[:, h2 * 256 + C:(h2 + 1) * 256], lhsT=Np[:, h2, :], rhs=Ap(h2),
                                           start=True, stop=True)
                                    if s >= 2:
                                        op('pe', 'matmul', [Np, Xp], [bD], out=bD[:, h2 * C:(h2 + 1) * C], lhsT=Np[:, h2, :], rhs=Xp[:, h2, :],
                                           start=True, stop=True)
                                if s <= 4:
                                    op('act', 'activation', [bNA], [Nn, An], out=sl['NA'][s % 2][:], in_=v2(bNA[:, :]), func=AF.Copy)
                                elif s == 5:
                                    op('act', 'activation', [bNA], [Nn], out=Nn[:], in_=v2(bNA[:, :])[:, :, 0:C], func=AF.Copy)
                                if s >= 2:
                                    op('dve', 'tensor_tensor', [bD, Xp], [Xn], out=Xn[:], in0=v2(bD[:, 0:256]), in1=Xp[:], op=ALU.add)
                                continue
                            bN, bAD = psum.next(), psum.next()
                            for h2 in range(2):
                                if s <= 5:
                                    op('pe', 'matmul', [Ap_buf, Np], [bN], out=bN[:, h2 * C:(h2 + 1) * C], lhsT=Ap(h2), rhs=Np[:, h2, :],
                                       start=True, stop=True)
                                if s <= 4:
                                    op('pe', 'matmul', [Ap_buf, Np], [bAD], out=bAD[:, h2 * 256:h2 * 256 + C], lhsT=Np[:, h2, :], rhs=Ap(h2),
                                       start=True, stop=True)
                                if s >= 2:
                                    if P2_XADD_PE:
                                        op('pe', 'matmul', [cbf, Xp], [bAD], sig=False, out=bAD[:, h2 * 256 + C:(h2 + 1) * 256], lhsT=identb, rhs=Xp[:, h2, :],
                                           start=True, stop=False)
                                    op('pe', 'matmul', [Np, Xp], [bAD], out=bAD[:, h2 * 256 + C:(h2 + 1) * 256], lhsT=Np[:, h2, :], rhs=Xp[:, h2, :],
                                       start=(not P2_XADD_PE), stop=True)
                            v2 = lambda ap: ap.rearrange('p (h c) -> p h c', h=2)
                            if s <= 5:
                                op('act', 'activation', [bN], [Nn], out=Nn[:], in_=v2(bN[:, 0:256]), func=AF.Copy)
                            if s <= 4:
                                if P2_ACOPY_ACT == 0 or j % P2_ACOPY_ACT == 0:
                                    op('act', 'activation', [bAD], [An], out=An[:], in_=v2(bAD[:, :])[:, :, 0:C], func=AF.Copy)
                                else:
                                    op('dve', 'tensor_copy', [bAD], [An], out=An[:], in_=v2(bAD[:, :])[:, :, 0:C])
                            if s >= 2:
                                if not P2_XADD_PE:
                                    op('dve', 'tensor_tensor', [bAD, Xp], [Xn], out=Xn[:], in0=v2(bAD[:, :])[:, :, C:2 * C], in1=Xp[:], op=ALU.add)
                                elif P2_XADD_PE == 1 and j % 2 == 0:
                                    op('act', 'activation', [bAD], [Xn], out=Xn[:], in_=v2(bAD[:, :])[:, :, C:2 * C], func=AF.Copy)
                                else:
                                    op('dve', 'tensor_copy', [bAD], [Xn], out=Xn[:], in_=v2(bAD[:, :])[:, :, C:2 * C])
                    stage(2)
                    for j in pairs:
                        sl = slots[j % NSL]
                        Xb = sl['X'][0]
                        b9 = psum.next()
                        for h2 in range(2):
                            h = 2 * j + h2
                            op('pe', 'matmul', [Xb, At], [b9], out=b9[:, h2 * C:h2 * C + N], lhsT=Xb[:, h2, :],
                               rhs=At[:, h * N:(h + 1) * N], start=True, stop=True)
                            op('pe', 'matmul', [Xb, sl['tmpV']], [b9], out=b9[:, h2 * C + N:(h2 + 1) * C], lhsT=Xb[:, h2, :],
                               rhs=sl['tmpV'][:, h2, :], start=True, stop=True)
                        op('act', 'activation', [b9], [sl['Zp']], out=sl['Zp'][:], in_=b9[:, 0:2 * C].rearrange('p (h c) -> p h c', h=2), func=AF.Copy)
                    for j in pairs:
                        sl = slots[j % NSL]
                        b9 = psum.next()
                        for h2 in range(2):
                            op('pe', 'matmul', [sl['Aoff'], sl['Zp']], [b9], out=b9[:, h2 * C:(h2 + 1) * C], lhsT=sl['Aoff'][:, h2, :],
                               rhs=sl['Zp'][:, h2, :], start=True, stop=True)
                        op('act', 'activation', [b9], [sl['Pp']], out=sl['Pp'][:], in_=b9[:, 0:2 * C].rearrange('p (h c) -> p h c', h=2), func=AF.Copy)
                    for j in pairs:
                        sl = slots[j % NSL]
                        Xb = sl['X'][0]
                        b9 = psum.next()
                        for h2 in range(2):
                            op('pe', 'matmul', [Xb, sl['Pp']], [b9], out=b9[:, h2 * C:(h2 + 1) * C], lhsT=Xb[:, h2, :],
                               rhs=sl['Pp'][:, h2, :], start=True, stop=True)
                        op('dve', 'tensor_tensor', [b9, sl['Zp']], [sl['WU']], out=sl['WU'][:], in0=b9[:, 0:2 * C].rearrange('p (h c) -> p h c', h=2),
                           in1=sl['Zp'][:], op=ALU.add)
                    stage(3)
                    for j in pairs:
                        sl = slots[j % NSL]
                        bR, bG = psum.next(), psum.next()
                        for h2 in range(2):
                            h = 2 * j + h2
                            pb = 64 * h2
                            op('pe', 'matmul', [sl['WU'], sl['S1m']], [bR], out=bR[pb:pb + 64, 0:C], lhsT=sl['WU'][:, h2, 0:N],
                               rhs=sl['S1m'][:, h2, C:2 * C], start=True, stop=True)
                            op('pe', 'matmul', [sl['WU'], Bbt], [bG], out=bG[pb:pb + 64, 0:N], lhsT=sl['WU'][:, h2, 0:N],
                               rhs=Bbt[:, h * N:(h + 1) * N], start=True, stop=True)
                        for h2 in range(2):
                            pb = 64 * h2
                            op('dve', 'tensor_tensor', [bR, AR], [sl['RpT']], out=sl['RpT'][pb:pb + 64, h2, :], in0=bR[pb:pb + 64, 0:C],
                               in1=AR[pb:pb + 64, j, C:2 * C], op=ALU.add)
                        for h2 in range(2):
                            pb = 64 * h2
                            op('act', 'activation', [bG], [GT[j]], out=GT[j][pb:pb + 64, pb:pb + 64], in_=bG[pb:pb + 64, 0:N], func=AF.Copy)
                    stage(4)
                    bY = psum.next()
                    ybanks.append(bY)
                    for j in pairs:
                        sl = slots[j % NSL]
                        for h2 in range(2):
                            h = 2 * j + h2
                            pb = 64 * h2
                            hc = (h - 8 * half) * N
                            op('pe', 'matmul', [sl['RpT'], STb[j]], [bY], sig=False, out=bY[:, hc:hc + N], lhsT=sl['RpT'][:, h2, :],
                               rhs=STb_all[:, j, :], start=True, stop=False)
                            op('pe', 'matmul', [sl['S1m'], sl['WU']], [bY], sig=False, out=bY[:, hc:hc + N], lhsT=sl['S1m'][:, h2, C:2 * C],
                               rhs=sl['WU'][:, h2, N:2 * N], start=False, stop=False)
                            op('pe', 'matmul', [sl['S2m'], Vt], [bY], out=bY[:, hc:hc + N], lhsT=sl['S2m'][:, h2, C:2 * C],
                               rhs=Vt[:, h * N:(h + 1) * N], start=False, stop=True)
                    for j in pairs:
                        sl = slots[j % NSL]
                        bH = psum.next()
                        for h2 in range(2):
                            h = 2 * j + h2
                            pb = 64 * h2
                            op('pe', 'matmul', [Bbt, sl['WU']], [bH], sig=False, out=bH[pb:pb + 64, 0:N], lhsT=Bbt[:, h * N:(h + 1) * N],
                               rhs=sl['WU'][:, h2, N:2 * N], start=True, stop=False)
                            op('pe', 'matmul', [Kbt, Vt], [bH], sig=False, out=bH[pb:pb + 64, 0:N], lhsT=Kbt[:, h * N:(h + 1) * N],
                               rhs=Vt[:, h * N:(h + 1) * N], start=False, stop=False)
                        op('pe', 'matmul', [GT[j], ST[j]], [bH], out=bH[:, 0:N], lhsT=GT[j][:], rhs=ST_all[:, j, :], start=False, stop=True)
                        op('dve', 'scalar_tensor_tensor', [ST[j], glast, bH], [ST[j]], out=ST_all[:, j, :], in0=ST_all[:, j, :],
                           scalar=glast[:, j, c:c + 1], in1=bH[:, 0:N], op0=ALU.mult, op1=ALU.add)
                        op('act', 'activation', [ST[j]], [STb[j]], out=STb_all[:, j, :], in_=ST_all[:, j, :], func=AF.Copy)
                    op('act', 'activation', [bY], [y_sb], out=y_sb[:, half * 512:(half + 1) * 512], in_=bY[:, :], func=AF.Copy)
                    if half == 0 and c + 1 < NCH:
                        zpad(L_next)
                if c == NCH - 1:
                    slot_bufs = [wkv_pre]
                    for sl in slots:
                        for v_ in sl.values():
                            slot_bufs += (v_ if isinstance(v_, list) else [v_])
                    wv_ = w_kv.rearrange('(kc p) n -> p kc n', p=128)
                    for c0_ in range(0, 2 * D, 512):
                        dma('pool', wkv_pre[:, :, c0_:c0_ + 512], wv_[:, :, c0_:c0_ + 512], writes=slot_bufs)
                stage(5)
                y3 = lambda ap: ap.rearrange('p (h n) -> p h n', h=H)
                bc = lambda ap: ap.unsqueeze(2).broadcast_to([128, H, N])
                op('dve', 'tensor_reduce', [y_sb], [stt], out=stt[:, 0, :], in_=y3(y_sb[:]), axis=mybir.AxisListType.X, op=ALU.add)
                op('act', 'activation', [y_sb], [ysq], out=ysq[:], in_=y_sb[:], func=AF.Square)
                op('dve', 'tensor_reduce', [ysq, stt], [stt], out=stt[:, 1, :], in_=y3(ysq[:]), axis=mybir.AxisListType.X, op=ALU.add)
                op('dve', 'tensor_scalar', [stt], [stt], out=stt[:, 0, :], in0=stt[:, 0, :], scalar1=1.0 / N, scalar2=None, op0=ALU.mult)
                op('dve', 'tensor_tensor', [stt], [stt], out=stt[:, 2, :], in0=stt[:, 0, :], in1=stt[:, 0, :], op=ALU.mult)
                op('dve', 'scalar_tensor_tensor', [stt], [stt], out=stt[:, 1, :], in0=stt[:, 1, :], scalar=1.0 / N, in1=stt[:, 2, :],
                   op0=ALU.mult, op1=ALU.subtract)
                op('act', 'activation', [stt, epsb], [stt], out=stt[:, 1, :], in_=stt[:, 1, :], func=AF.Ln, bias=epsb[:, 2:3])
                op('act', 'activation', [stt], [stt], out=stt[:, 1, :], in_=stt[:, 1, :], func=AF.Exp, scale=-0.5)
                op('dve', 'tensor_tensor', [y_sb, stt], [yn], out=y3(yn[:]), in0=y3(y_sb[:]), in1=bc(stt[:, 0, :]), op=ALU.subtract)
                op('dve', 'tensor_tensor', [yn, stt], [yn], out=y3(yn[:]), in0=y3(yn[:]), in1=bc(stt[:, 1, :]), op=ALU.mult)
                op(('pool' if P2_POST_POOL else 'dve'), 'tensor_tensor', [yn, ptml], [yn], out=yn[:], in0=yn[:], in1=ptml[:, 0, :], op=ALU.mult)
                op(('pool' if P2_POST_POOL else 'dve'), 'tensor_tensor', [yn, ptml], [yn], out=yn[:], in0=yn[:], in1=ptml[:, 1, :], op=ALU.add)
                op(('pool' if P2_POST_POOL else 'dve'), 'tensor_tensor', [Vt, L['bon']], [bv], out=y3(bv[:]), in0=y3(Vt[:]), in1=bc(L['bon'][:]), op=ALU.mult)
                op(('pool' if P2_POST_POOL else 'dve'), 'tensor_tensor', [yn, bv], [yn], out=yn[:], in0=yn[:], in1=bv[:], op=ALU.add)
                op(('pool' if P2_POST_POOL else 'dve'), 'tensor_tensor', [yn, SGt], [yfin], out=yfin[:], in0=yn[:], in1=SGt[:], op=ALU.mult)
                stage(6)
                bk = psum.next()
                bkb = bk[:].bitcast(BF16)
                for j in range(NJ):
                    op('pe', 'transpose', [yfin, cbf], [bk], sig=(j == NJ - 1), out=bkb[:, j * 128:(j + 1) * 128],
                       in_=yfin[:, j * 128:(j + 1) * 128], identity=identb)
                op('act', 'activation', [bk], [yT], out=yT[:], in_=bkb.rearrange('p (j t) -> p j t', j=NJ), func=AF.Copy)
                xr = xr_o.next()
                for half in range(2):
                    bk = psum.next()
                    for kc in range(NJ):
                        op('pe', 'matmul', [yT, Wout], [bk], sig=(kc == NJ - 1), out=bk[:, :], lhsT=yT[:, kc, :],
                           rhs=Wout[:, kc, half * 512:(half + 1) * 512], start=(kc == 0), stop=(kc == NJ - 1))
                    hs = slice(half * 512, (half + 1) * 512)
                    op('dve', 'tensor_tensor', [bk, gateA], [t_o], out=t_o[:, hs], in0=bk[:, :], in1=gateA[:, hs], op=ALU.mult)
                    op('dve', 'tensor_tensor', [t_o, L['xin']], [xr], out=xr[:, hs], in0=t_o[:, hs], in1=L['xin'][:, hs], op=ALU.add)
                dma('sp', s_xr1[c * C:(c + 1) * C, :], xr[:], reads=[xr])
                if c < 6:
                    ada_ct(bctx2, 'b', c, load=False)
            kb.mute = False
            kb.barrier()

        s_KT = scr('s_KT', [NJ, 128, S], BF16)
        s_V = scr('s_V', [S, D], BF16)
        s_QT = scr('s_QT', [3 * NJ, 128, S], BF16)
        s_SG = scr('s_SG', [NJ, 128, S], BF16)
        if LAST_PHASE >= 3:
            kb.mute = 3 in SKIP_PHASES
            TB = 4
            T = TB * 128
            for sub in ('kv', 'q'):
                if sub == 'kv':
                    arena.reset()
                    arena.alloc([NJ, 2 * D], BF16, 'Wb')
                    if LAST_PHASE >= 2 and 2 not in SKIP_PHASES:
                        Wb = wkv_pre
                    else:
                        Wb = Buf(wkv_pre.ap if LAST_PHASE >= 2 else arena.base[:, 0:NJ * 2 * D].rearrange('p (a b) -> p a b', a=NJ), 'Wb')
                        load_w_bf16(Wb, w_kv.rearrange('(kc p) n -> p kc n', p=128), 2 * D)
                    Wq = arena.alloc_top([NJ, 4 * D], BF16, 'Wq')
                    load_w_bf16(Wq, b_w_in.rearrange('(kc p) n -> p kc n', p=128), 4 * D)
                else:
                    arena.reset(keep_top=True)
                    Wb = Wq
                qkg = arena.alloc([4], F32, 'qkg')
                TC = arena.alloc([S], F32, 'TC'); TS = arena.alloc([S], F32, 'TS')
                xts = [arena.alloc([TB, D], F32, 'xt3_%d' % i) for i in range(2)]
                rope = xts[1]
                rope_v = xts[1][:].rearrange('p a b -> p (a b)').rearrange('p (r s) -> p r s', r=2)
                bctx = None
                sq = arena.alloc([D], BF16, 'sq3'); ss = arena.alloc([4], F32, 'ss3')
                hT = arena.alloc([NJ, T], BF16, 'hT3'); hTj = split(hT, NJ)
                raw = Ring([arena.alloc([T], BF16, 'raw%d' % i) for i in range(4)])
                sqr = Ring([arena.alloc([T], BF16, 'sqr%d' % i) for i in range(4)])
                rs = Ring([arena.alloc([T], F32, 'rs%d' % i) for i in range(2)])
                t1 = Ring([arena.alloc([T], F32, 't1_%d' % i) for i in range(2)])
                t2 = Ring([arena.alloc([T], F32, 't2_%d' % i) for i in range(2)])
                ofm = Ring([arena.alloc([T], BF16, 'ofm%d' % i) for i in range(2)])
                otm = Ring([arena.alloc([D], BF16, 'otm%d' % i) for i in range(1)])
                load_rows(s_xr1, 0, TB, xts[0])
                dma('sp', rope_v, rope_d, writes=[rope])
                dma('sp', qkg[:], qkg_d, writes=[qkg])
                gi = 2 if sub == 'kv' else 0
                op('dve', 'tensor_scalar', [rope, qkg], [TC], out=TC[:], in0=rope_v[:, 0, :], scalar1=qkg[:, gi:gi + 1],
                   scalar2=(8.0 if sub == 'kv' else 1.0), op0=ALU.mult, op1=ALU.mult)
                op('dve', 'tensor_scalar', [rope, qkg], [TS], out=TS[:], in0=rope_v[:, 1, :], scalar1=qkg[:, gi + 1:gi + 2],
                   scalar2=(8.0 if sub == 'kv' else 1.0), op0=ALU.mult, op1=ALU.mult)
                if sub == 'kv':
                    gs_ap = lambda j: P('kv_norm_g', j)
                    sh_ap = lambda j: 0.0
                    fm_chunks = [(j, j * 128, s_KT, j) for j in range(NJ)]
                else:
                    gs_ap = lambda j: gsh[:, 2, j:j + 1]
                    sh_ap = lambda j: gsh[:, 3, j:j + 1]
                    fm_chunks = [(jq, jq * 128, s_QT, jq) for jq in range(3 * NJ)]
                for tt in range(S // T):
                    t0 = tt * T
                    xt = xts[tt % 2]
                    norm_transpose(TB, xt, sq, ss, xt, hT, hTj, gs_ap, sh_ap, 0)
                    if tt + 1 < S // T:
                        load_rows(s_xr1, t0 + T, TB, xts[(tt + 1) % 2])
                    if bctx is not None and tt < 3:
                        ada_load(bctx, 'b', 2 * tt)
                    cx = {}

                    def st0(i):
                        (ci, c0_, dst, di) = fm_chunks[i]
                        bk = psum.next()
                        for kc in range(NJ):
                            op('pe', 'matmul', [Wb, hTj[kc]], [bk], sig=(kc == NJ - 1), out=bk[:, :], lhsT=Wb[:, kc, c0_:c0_ + 128],
                               rhs=hT[:, kc, :], start=(kc == 0), stop=(kc == NJ - 1))
                        cx[i] = dict(bk=bk)

                    def st1(i):
                        c_ = cx[i]
                        raw_, sq_ = raw.next(), sqr.next()
                        op('act', 'activation', [c_['bk']], [raw_], out=raw_[:], in_=c_['bk'][:, :], func=AF.Copy)
                        op('act', 'activation', [c_['bk']], [sq_], out=sq_[:], in_=c_['bk'][:, :], func=AF.Square)
                        c_.update(raw=raw_, sq=sq_)

                    def st2(i):
                        c_ = cx[i]
                        b_ss, b_rot = psum.next(), psum.next()
                        t1_ = t1.next()
                        op('pe', 'matmul', [cbf, c_['sq']], [b_ss], out=b_ss[:, :], lhsT=blockones, rhs=c_['sq'][:], start=True, stop=True)
                        op('pe', 'matmul', [cbf, c_['raw']], [b_rot], out=b_rot[:, :], lhsT=rotp, rhs=c_['raw'][:], start=True, stop=True)
                        op('pool', 'tensor_tensor', [c_['raw'], TC], [t1_], out=t1_[:], in0=c_['raw'][:], in1=TC[:, t0:t0 + T], op=ALU.mult)
                        c_.update(t1=t1_, b_ss=b_ss, b_rot=b_rot)

                    def st3(i):
                        c_ = cx[i]
                        rs_ = rs.next()
                        op('act', 'activation', [c_['b_ss'], epsb], [rs_], out=rs_[:], in_=c_['b_ss'][:, :], func=AF.Ln, bias=epsb[:, 3:4])
                        op('act', 'activation', [rs_], [rs_], out=rs_[:], in_=rs_[:], func=AF.Exp, scale=-0.5)
                        t2_ = t2.next()
                        op('dve', 'tensor_tensor', [c_['b_rot'], TS], [t2_], out=t2_[:], in0=c_['b_rot'][:, :], in1=TS[:, t0:t0 + T], op=ALU.mult)
                        op('dve', 'tensor_tensor', [c_['t1'], t2_], [t2_], out=t2_[:], in0=c_['t1'][:], in1=t2_[:], op=ALU.add)
                        c_.update(rs=rs_, t2=t2_)

                    def st4(i):
                        (ci, c0_, dst, di) = fm_chunks[i]
                        c_ = cx.pop(i)
                        o_ = ofm.next()
                        op('dve', 'tensor_tensor', [c_['t2'], c_['rs']], [o_], out=o_[:], in0=c_['t2'][:], in1=c_['rs'][:], op=ALU.mult)
                        dma('sp', dst[di, :, t0:t0 + T], o_[:], reads=[o_])
                    wavefront(len(fm_chunks), [st0, st1, st2, st3, st4], order=(list(P3_ORDER) if P3_ORDER else None))
                    if bctx is not None and tt < 3:
                        ada_ct(bctx, 'b', 2 * tt, load=False)
                        ada_load(bctx, 'b', 2 * tt + 1)
                    if sub == 'kv':
                        for b in range(TB):
                            o_ = otm.next()
                            for half in range(2):
                                bk = psum.next()
                                for kc in range(NJ):
                                    op('pe', 'matmul', [hTj[kc], Wb], [bk], sig=(kc == NJ - 1), out=bk[:, :], lhsT=hT[:, kc, b * 128:(b + 1) * 128],
                                       rhs=Wb[:, kc, D + half * 512:D + (half + 1) * 512], start=(kc == 0), stop=(kc == NJ - 1))
                                op('act', 'activation', [bk], [o_], out=o_[:, half * 512:(half + 1) * 512], in_=bk[:, :], func=AF.Copy)
                            dma('sp', s_V[t0 + b * 128:t0 + (b + 1) * 128, :], o_[:], reads=[o_])
                        if bctx is not None and tt < 3:
                            ada_ct(bctx, 'b', 2 * tt + 1, load=False)
                    else:
                        for j in range(NJ):
                            bk = psum.next()
                            for kc in range(NJ):
                                op('pe', 'matmul', [Wb, hTj[kc]], [bk], sig=(kc == NJ - 1), out=bk[:, :],
                                   lhsT=Wb[:, kc, 3 * D + j * 128:3 * D + (j + 1) * 128], rhs=hT[:, kc, :], start=(kc == 0), stop=(kc == NJ - 1))
                            o_ = ofm.next()
                            op('act', 'activation', [bk], [o_], out=o_[:], in_=bk[:, :], func=AF.Silu)
                            dma('sp', s_SG[j, :, t0:t0 + T], o_[:], reads=[o_])
                kb.mute2 = kb.mute
                kb.mute = False
                kb.barrier()
                kb.mute = kb.mute2

        if LAST_PHASE >= 4:
            kb.mute = False
            arena.reset()
            yT = arena.alloc([NJ, S], BF16, 'yTall'); yTj = split(yT, NJ)

            def mk_pl(i):
                d_ = dict(KT=arena.alloc([S], BF16, 'KTp%d' % i), SG=arena.alloc([S], BF16, 'SGp%d' % i),
                          QTz=arena.alloc([2, 3, S], BF16, 'QTz%d' % i))
                op('pool', 'memset', [], [d_['QTz']], ap=d_['QTz'][:], constant=0.0)
                d_['VL'] = [[arena.alloc([16, 128], BF16, 'VL%d_%d_%d' % (i, g, h2)) for h2 in range(2)] for g in range(3)]
                for g in range(3):
                    for h2 in range(2):
                        op('pool', 'memset', [], [d_['VL'][g][h2]], ap=d_['VL'][g][h2][:], constant=1.0)
                return d_
            pls = [mk_pl(i) for i in range(2)]
            GR = ((1, 16), (4, 4), (16, 1))

            def pair_loads(j):
                L = pls[j % 2]
                dma('sp', L['KT'][:], s_KT[j], writes=[L['KT']])
                qv = s_QT.rearrange('(g j) p t -> j p g t', g=3)[j]
                for h2 in range(2):
                    pb = 64 * h2
                    dma('sp', L['QTz'][pb:pb + 64, h2, :, :], qv[pb:pb + 64, :, :], writes=[L['QTz']])
                dma('sp', L['SG'][:], s_SG[j], writes=[L['SG']])
                for g, (dil, nblk) in enumerate(GR):
                    for h2 in range(2):
                        dma('sp', L['VL'][g][h2][:, :, 64 * h2:64 * h2 + 64].rearrange('p (r nb) d -> p r nb d', r=dil),
                            s_V.rearrange('(nb p r) d -> p r nb d', p=128, r=dil)[:, :, :, j * 128 + 64 * h2:j * 128 + 64 * h2 + 64],
                            writes=[L['VL'][g][h2]])
                return L
            accO = arena.alloc([S], F32, 'accA'); accL = arena.alloc([S], F32, 'accB')
            rec = arena.alloc([S], F32, 'rec')
            Pm = Ring([arena.alloc([2, 256], BF16, 'Pm%d' % i) for i in range(P4_NPM)])
            mbias = cbf[:, 2944:3200]
            sw1 = cst[:, 128:256]; sw2 = cst[:, 256:384]
            bOr = [banks[0], banks[1]]
            bLr = [banks[2], banks[3]]
            sring = Ring(banks[4:8])
            L_next = pair_loads(0)
            for j in range(NJ):
                L_ = L_next
                if j + 1 < NJ:
                    L_next = pair_loads(j + 1)
                KT, QTz, SG, VL = L_['KT'], L_['QTz'], L_['SG'], L_['VL']
                items = []
                for g, (dil, nblk) in enumerate(GR):
                    for r in range(dil):
                        for kbi in range(nblk):
                            nq = 2 if kbi + 1 < nblk else 1
                            ncol = 128 * nq
                            st_ = dil * 128 * kbi + r
                            items.append(dict(g=g, dil=dil, nblk=nblk, r=r, kbi=kbi, nq=nq, ncol=ncol,
                                              kcols=slice(st_, st_ + dil * 127 + 1, dil),
                                              qcols=slice(st_, st_ + dil * (ncol - 1) + 1, dil)))
                v2 = lambda ap: ap.rearrange('p (h c) -> p h c', h=2)

                def emit_scores(it):
                    bS = sring.next()
                    ncol = it['ncol']
                    for h2 in range(2):
                        op('pe', 'matmul', [KT, QTz], [bS], sig=False, out=bS[:, h2 * 256:h2 * 256 + ncol], lhsT=KT[:, it['kcols']],
                           rhs=QTz[:, h2, it['g'], it['qcols']], start=True, stop=False)
                        op('pe', 'matmul', [cbf], [bS], out=bS[:, h2 * 256:h2 * 256 + ncol], lhsT=identb,
                           rhs=mbias[:, 0:ncol], start=False, stop=True)
                    pm = Pm.next()
                    op('act', 'activation', [bS], [pm], out=pm[:, :, 0:ncol], in_=v2(bS[:, :])[:, :, 0:ncol], func=AF.Exp)
                    it['pm'] = pm

                def emit_pv(it):
                    g, dil, r, kbi, pm = it['g'], it['dil'], it['r'], it['kbi'], it['pm']
                    vblk = r * it['nblk'] + kbi
                    for qt in range(it['nq']):
                        nb = kbi + qt
                        reg = nb % 2
                        first = (qt == 1) or (nb == 0)
                        last = (qt == 0)
                        for h2, bq in ((0, bOr[reg]), (1, bLr[reg])):
                            op('pe', 'matmul', [VL[g][h2], pm], [bq], out=bq[:, 0:128],
                               lhsT=VL[g][h2][:, vblk, :], rhs=pm[:, h2, qt * 128:(qt + 1) * 128], start=first, stop=last)
                        if last:
                            q0 = dil * 128 * nb + r
                            tcols = slice(q0, q0 + dil * 127 + 1, dil)
                            if g == 0:
                                op('dve', 'tensor_copy', [bOr[reg]], [accO], out=accO[:, tcols], in_=bOr[reg][:, 0:128])
                                op('act', 'activation', [bLr[reg]], [accL], out=accL[:, tcols], in_=bLr[reg][:, 0:128], func=AF.Copy)
                            else:
                                op('dve', 'tensor_tensor', [bOr[reg], accO], [accO], out=accO[:, tcols], in0=bOr[reg][:, 0:128],
                                   in1=accO[:, tcols], op=ALU.add)
                                op('dve', 'tensor_tensor', [bLr[reg], accL], [accL], out=accL[:, tcols], in0=bLr[reg][:, 0:128],
                                   in1=accL[:, tcols], op=ALU.add)
                LOOK = P4_LOOK
                for i in range(len(items) + LOOK):
                    if i < len(items):
                        emit_scores(items[i])
                    if i - LOOK >= 0:
                        emit_pv(items[i - LOOK])
                for q4 in range(S // 512):
                    cs_ = slice(q4 * 512, (q4 + 1) * 512)
                    bk = sring.next()
                    op('pe', 'matmul', [cst, accO], [bk], sig=False, out=bk[:, :], lhsT=sw1, rhs=accO[:, cs_], start=True, stop=False)
                    op('pe', 'matmul', [cst, accL], [bk], out=bk[:, :], lhsT=sw2, rhs=accL[:, cs_], start=False, stop=True)
                    op('act', 'activation', [bk], [rec], out=rec[:, cs_], in_=bk[:, :], func=AF.Ln)
                op('act', 'activation', [rec], [rec], out=rec[:], in_=rec[:], func=AF.Exp, scale=-1.0)
                for (pb, src) in ((0, accO), (64, accL)):
                    op('dve', 'tensor_tensor', [src, rec], [src], out=src[pb:pb + 64, :], in0=src[pb:pb + 64, :], in1=rec[pb:pb + 64, :], op=ALU.mult)
                    op('dve', 'tensor_tensor', [src, SG], [yTj[j]], out=yT[pb:pb + 64, j, :], in0=src[pb:pb + 64, :], in1=SG[pb:pb + 64, :], op=ALU.mult)
                if j == NJ - 2:
                    Wout = Buf(pls[0]['QTz'][:].rearrange('p a b c -> p (a b c)')[:, 0:NJ * D].rearrange('p (k n) -> p k n', k=NJ), 'WoutB')
                    wv_ = b_w_out.rearrange('(kc p) n -> p kc n', p=128)
                    for c0_ in range(0, D, 512):
                        dma('pool', Wout[:, :, c0_:c0_ + 512], wv_[:, :, c0_:c0_ + 512], writes=[Wout, pls[0]['QTz']])
            kb.barrier()
            xr_in = Ring([Buf(accO[:, 0:D], 'xr_in0'), Buf(accO[:, D:2 * D], 'xr_in1')])
            t_o = Buf(accL[:, 0:D], 't_o5')
            o5 = Ring([Buf(rec[:, 0:D], 'o5_0'), Buf(rec[:, D:2 * D], 'o5_1')])
            allbanks = Ring(banks)
            xi_next = xr_in.next()
            dma('sp', xi_next[:], s_xr1[0:128, :], writes=[xi_next])
            for b in range(S // 128):
                xi = xi_next
                if b + 1 < S // 128:
                    xi_next = xr_in.next()
                    dma('sp', xi_next[:], s_xr1[(b + 1) * 128:(b + 2) * 128, :], writes=[xi_next])
                oo = o5.next()
                for half in range(2):
                    bk = allbanks.next()
                    for kc in range(NJ):
                        op('pe', 'matmul', [yTj[kc], Wout], [bk], sig=(kc == NJ - 1), out=bk[:, :], lhsT=yT[:, kc, b * 128:(b + 1) * 128],
                           rhs=Wout[:, kc, half * 512:(half + 1) * 512], start=(kc == 0), stop=(kc == NJ - 1))
                    hs = slice(half * 512, (half + 1) * 512)
                    op('dve', 'tensor_tensor', [bk, gateB], [t_o], out=t_o[:, hs], in0=bk[:, :], in1=gateB[:, hs], op=ALU.mult)
                    op('dve', 'tensor_tensor', [t_o, xi], [oo], out=oo[:, hs], in0=t_o[:, hs], in1=xi[:, hs], op=ALU.add)
                dma('sp', out_d[b * 128:(b + 1) * 128, :], oo[:], reads=[oo])

        kb.barrier()
        kb.emit()
    return nc


def _host_layout(inputs, b):
    f32 = np.float32
    fm = lambda v: np.ascontiguousarray(np.asarray(v, f32).reshape(NJ, 128).T)
    d = {}
    pf = {}
    mu = inputs['a_mix_mu'][0]
    for p in range(6):
        pf['mu%d' % p] = fm(mu[p])
    pf['a_norm_g'] = fm(inputs['a_norm_g'][0]); pf['w0'] = fm(inputs['a_w0'][0]); pf['a0'] = fm(inputs['a_a0'][0])
    pf['k_k'] = fm(inputs['a_k_k'][0]); pf['k_a'] = fm(inputs['a_k_a'][0]); pf['r_k'] = fm(inputs['a_r_k'][0].reshape(-1))
    pf['kv_norm_g'] = fm(inputs['kv_norm_g']); pf['b_norm_g'] = fm(inputs['b_norm_g'][0])
    ab, bb = inputs['a_ada_b'][0], inputs['b_ada_b'][0]
    pf['a_ada_b_shift'] = fm(ab[:D]); pf['a_ada_b_scale'] = fm(ab[D:2 * D])
    pf['b_ada_b_shift'] = fm(bb[:D]); pf['b_ada_b_scale'] = fm(bb[D:2 * D])
    pf['c'] = fm(inputs['c'][b])
    d['pfm'] = np.ascontiguousarray(np.stack([pf[n] for n in PFM], axis=1))
    rep = lambda v: np.broadcast_to(np.asarray(v, f32)[None, :], (128, D))
    d['ptm'] = np.ascontiguousarray(np.stack([rep(inputs['a_ln_g'][0]), rep(inputs['a_ln_b'][0]),
                                              rep(ab[2 * D:]), rep(bb[2 * D:])], axis=1))
    return d


def _consts():
    f32 = np.float32
    ti = np.arange(128)
    ident = np.eye(128, dtype=f32)
    m_su = (ti[:, None] < ti[None, :]).astype(f32)
    m_ui = (ti[:, None] <= ti[None, :]).astype(f32)
    m_sl = (ti[:, None] > ti[None, :]).astype(f32)
    m2 = np.concatenate([m_su, m_ui], 1)
    m_su_ui = np.concatenate([m2, m2], 1)
    blockones = np.kron(np.eye(2, dtype=f32), np.ones((64, 64), f32))
    rot = np.zeros((128, 128), f32)
    for po in range(128):
        d_ = po % 64
        pi = po + 32 if d_ < 32 else po - 32
        rot[pi, po] = 1.0
    m_li = (ti[:, None] >= ti[None, :]).astype(f32)
    bd = ((ti[:, None] // 64) == (ti[None, :] // 64)).astype(f32)
    sw1 = (ti[:, None] == ti[None, :] + 64).astype(f32)
    sw2 = (ti[:, None] + 64 == ti[None, :]).astype(f32)
    m_off = ((ti[:, None] < 64) & (ti[None, :] >= 64)).astype(f32)
    cst = np.concatenate([ident, m_su_ui, m_sl, m_sl, ident, ident, blockones, rot, m_ui, m_li, m_ui, m_li,
                          m_su * bd, m_ui, m_su * bd, m_ui, m_sl * bd, m_sl * bd, m_off, m_off,
                          (np.concatenate([m_ui, m_li], 1) - 1.0) * 30000.0, sw1, sw2], 1).astype(f32)
    assert cst.shape[1] == CW
    sel = np.zeros((128, NJ, H), f32)
    for p in range(128):
        for j in range(NJ):
            sel[p, j, 2 * j + p // 64] = 1.0
    pos = np.arange(S, dtype=f32)
    inv = (np.float32(10000.0) ** (-np.arange(0, 64, 2, dtype=f32) / np.float32(64))).astype(f32)
    ang = pos[:, None] * inv[None, :]
    cos, sin = np.cos(ang).astype(f32), np.sin(ang).astype(f32)
    rope = np.zeros((128, 2, S), f32)
    for p in range(128):
        d_ = p % 64
        rope[p, 0] = cos[:, d_ % 32]
        rope[p, 1] = sin[:, d_ % 32] * (-1.0 if d_ < 32 else 1.0)
    return {'cst': cst, 'sel': sel, 'rope': rope}


def _qkg(qg, kg):
    idx = np.arange(128) % 64
    par = (idx + 32) % 64
    qg = np.asarray(qg, np.float32).reshape(64); kg = np.asarray(kg, np.float32).reshape(64)
    return np.ascontiguousarray(np.stack([qg[idx], qg[par], kg[idx], kg[par]], 1).astype(np.float32))


def kernel(**inputs):
    inputs = {k: np.asarray(v) for k, v in inputs.items()}
    n = 8
    nc = build_nc()
    consts = _consts()
    shared = {k: np.ascontiguousarray(inputs[k][0], dtype=np.float32) for k in
              ('a_ada_w', 'b_ada_w', 'a_w_in', 'b_w_in', 'a_w1', 'a_w2', 'a_a1', 'a_a2', 'a_w_out', 'b_w_out')}
    shared['w_kv'] = np.ascontiguousarray(inputs['w_kv'], dtype=np.float32)
    shared['qkg'] = _qkg(inputs['b_q_norm_g'][0], inputs['k_norm_g'])
    shared.update(consts)
    in_maps = []
    for b in range(n):
        m = dict(shared)
        m['x'] = np.ascontiguousarray(inputs['x'][b], dtype=np.float32)
        m.update(_host_layout(inputs, b))
        in_maps.append(m)
    res = run_bass_kernel_spmd(nc, in_maps, core_ids=list(range(n)))
    return np.stack([r['out'] for r in res.results], axis=0).astype(np.float32)
```
